# Optimizing a Trainium2 kernel written in Bass

```python
import math
import jax, jax.numpy as jnp
from jax import lax
import numpy as np

D_MODEL = 1024
BATCH = 2
SEQ = 8192
DEPTH = 2
DEC_BATCH = 16
DEC_SEQ = 64
PAST_LEN = 2048

CHUNK = 64
N_A = DEPTH // 2
N_B = DEPTH - N_A
D_FF = 2816
RWKV_HEAD = 64
RWKV_HEADS = D_MODEL // RWKV_HEAD
DECAY_LORA = 64
A_LORA = 64
GATE_LORA = 128
GN_EPS = 64e-5
N_HEADS_B = 8
HEAD_QK = D_MODEL // N_HEADS_B // 2
HEAD_V = 2 * HEAD_QK
QK_W = N_HEADS_B * 2 * HEAD_QK
V_W = N_HEADS_B * HEAD_V
ROT_DIM = HEAD_QK // 4
ROPE_THETA = 500000.0
ATTN_SCALE = HEAD_QK ** -0.5
Q_BLOCK = 128
NORM_EPS = 1e-6
NEG_INF = -1e30

kernel_name = "streaming_rwkv7_diffattn_yoco"

F32 = jnp.float32


def _rms(x, g):
    x32 = x.astype(F32)
    y = x32 * lax.rsqrt(jnp.mean(x32 * x32, axis=-1, keepdims=True) + NORM_EPS) * g.astype(F32)
    return y.astype(x.dtype)


def _modulate(h, shift, scale):
    return h * (1.0 + scale) + shift


def _swiglu(h, w_in, w_out):
    gu = h @ w_in
    gate, up = jnp.split(gu, 2, axis=-1)
    return (jax.nn.silu(gate) * up) @ w_out


def _rope(x, pos):
    half = ROT_DIM // 2
    inv = ROPE_THETA ** (-jnp.arange(half, dtype=F32) * 2.0 / ROT_DIM)
    ang = pos.astype(F32)[:, None] * inv[None, :]
    cos = jnp.cos(ang)[None, :, None, None, :]
    sin = jnp.sin(ang)[None, :, None, None, :]
    xf = x.astype(F32)
    x1 = xf[..., :half]
    x2 = xf[..., half:ROT_DIM]
    out = jnp.concatenate([x1 * cos - x2 * sin, x2 * cos + x1 * sin, xf[..., ROT_DIM:]], axis=-1)
    return out.astype(x.dtype)


def _rwkv_step(S, inp):
    r_t, w_t, k_t, v_t, kk_t, b_t = inp
    sa = jnp.einsum('bhvk,bhk->bhv', S, -kk_t)
    S = S * w_t[:, :, None, :] + sa[..., None] * b_t[:, :, None, :] + v_t[..., None] * k_t[:, :, None, :]
    y = jnp.einsum('bhvk,bhk->bhv', S, r_t)
    return S, y


def _rwkv7_time_mix(h, shift_row, S0, i, P):
    B, T, D = h.shape
    dt = h.dtype
    mu = P['rwkv_mu'][i]
    xprev = jnp.concatenate([shift_row.astype(dt), h[:, :-1]], axis=1)
    dx = xprev - h
    xr, xw, xk, xv, xa, xg = [h + dx * mu[j] for j in range(6)]
    W = P['rwkv_w_rkv'][i]
    r = xr @ W[0]
    k = xk @ W[1]
    v = xv @ W[2]
    z = (P['rwkv_w0'][i] + jnp.tanh(xw @ P['rwkv_w1'][i]) @ P['rwkv_w2'][i]).astype(F32)
    w_log = -jax.nn.softplus(-z) - 0.5
    decay = jnp.exp(-jnp.exp(w_log))
    a = jax.nn.sigmoid((P['rwkv_a0'][i] + (xa @ P['rwkv_a1'][i]) @ P['rwkv_a2'][i]).astype(F32))
    g = jax.nn.sigmoid(xg @ P['rwkv_g1'][i]) @ P['rwkv_g2'][i]

    def heads(t):
        return t.astype(F32).reshape(B, T, RWKV_HEADS, RWKV_HEAD)

    kk = heads(k * P['rwkv_k_k'][i])
    kk = kk / jnp.maximum(jnp.linalg.norm(kk, axis=-1, keepdims=True), 1e-12)
    k = k.astype(F32) * (1.0 + (a - 1.0) * P['rwkv_k_a'][i].astype(F32))
    r_h, k_h, v_h, w_h, a_h = heads(r), heads(k), heads(v), heads(decay), heads(a)

    def tm(t):
        return jnp.moveaxis(t, 1, 0)

    S_T, y = lax.scan(_rwkv_step, S0.astype(F32),
                      (tm(r_h), tm(w_h), tm(k_h), tm(v_h), tm(kk), tm(kk * a_h)))
    y = jnp.moveaxis(y, 0, 1)
    mean = jnp.mean(y, axis=-1, keepdims=True)
    var = jnp.mean(jnp.square(y - mean), axis=-1, keepdims=True)
    y = ((y - mean) * lax.rsqrt(var + GN_EPS)).reshape(B, T, D)
    y = y * P['rwkv_ln_w'][i].astype(F32) + P['rwkv_ln_b'][i].astype(F32)
    bonus = (jnp.sum(r_h * k_h * P['rwkv_r_k'][i].astype(F32), axis=-1, keepdims=True) * v_h).reshape(B, T, D)
    out = ((y + bonus) * g.astype(F32)).astype(dt) @ P['rwkv_w_o'][i]
    return out, S_T, h[:, -1:]


def _diff_attend(q, qpos, k, v, kpos, lam):
    s = jnp.einsum('bqhcd,bkhcd->bhcqk', q.astype(F32), k) * ATTN_SCALE
    visible = (kpos // CHUNK)[None, :] <= (qpos // CHUNK)[:, None]
    s = jnp.where(visible, s, NEG_INF)
    p = jax.nn.softmax(s, axis=-1)
    a = p[:, :, 0] - lam * p[:, :, 1]
    return jnp.einsum('bhqk,bkhd->bqhd', a, v)


def _attend(q, qpos, k, v, kpos, lam):
    B, T = q.shape[0], q.shape[1]
    if T <= Q_BLOCK:
        return _diff_attend(q, qpos, k, v, kpos, lam)
    nb = T // Q_BLOCK
    qb = jnp.moveaxis(q.reshape(B, nb, Q_BLOCK, N_HEADS_B, 2, HEAD_QK), 1, 0)
    pb = qpos.reshape(nb, Q_BLOCK)
    ob = lax.map(lambda args: _diff_attend(args[0], args[1], k, v, kpos, lam), (qb, pb))
    return jnp.moveaxis(ob, 0, 1).reshape(B, T, N_HEADS_B, HEAD_V)


def _diff_attn_layer(h, qpos, k_all, v_all, kpos, j, layer_idx, P):
    B, T, _ = h.shape
    lam_init = 0.8 - 0.6 * math.exp(-0.3 * layer_idx)
    lp = P['diff_lambda'][j].astype(F32)
    lam = jnp.exp(jnp.sum(lp[0] * lp[1])) - jnp.exp(jnp.sum(lp[2] * lp[3])) + lam_init
    q = _rope((h @ P['diff_w_q'][j]).reshape(B, T, N_HEADS_B, 2, HEAD_QK), qpos)
    o = _attend(q, qpos, k_all, v_all, kpos, lam)
    o = o * lax.rsqrt(jnp.mean(o * o, axis=-1, keepdims=True) + NORM_EPS)
    o = o * P['diff_subln_g'][j].astype(F32) * (1.0 - lam_init)
    return o.reshape(B, T, V_W).astype(h.dtype) @ P['diff_w_o'][j]


def _trunk(x, c, past_len, wkv_init, shift_init, k_past, v_past, P):
    B, T, _ = x.shape
    dt = x.dtype
    pos = past_len + jnp.arange(T)
    sc = jax.nn.silu(c)
    new_wkv, new_shift = [], []
    k_new = v_new = k_all = v_all = kpos = None
    for l in range(DEPTH):
        if l == N_A:
            kv_mod = (sc @ P['kv_ada_w'] + P['kv_ada_b'])[:, None, :]
            kv_shift, kv_scale = jnp.split(kv_mod, 2, axis=-1)
            hk = _modulate(_rms(x, P['kv_norm_g']), kv_shift, kv_scale)
            kv = hk @ P['kv_w']
            k_new = _rope(kv[..., :QK_W].reshape(B, T, N_HEADS_B, 2, HEAD_QK), pos)
            v_new = kv[..., QK_W:].reshape(B, T, N_HEADS_B, HEAD_V)
            if k_past is None:
                k_all, v_all = k_new.astype(F32), v_new.astype(F32)
            else:
                k_all = jnp.concatenate([k_past.astype(F32), k_new.astype(F32)], axis=1)
                v_all = jnp.concatenate([v_past.astype(F32), v_new.astype(F32)], axis=1)
            kpos = jnp.arange(past_len + T)
        mod = (sc @ P['ada_w'][l] + P['ada_b'][l])[:, None, :]
        s1, c1, g1, s2, c2, g2, s3, c3, g3 = jnp.split(mod, 9, axis=-1)
        ng = P['norm_g'][l]
        hf = _modulate(_rms(x, ng[0]), s1, c1)
        x = x + 0.5 * g1 * _rms(_swiglu(hf, P['ffn_w_in'][l, 0], P['ffn_w_out'][l, 0]), ng[1])
        hm = _modulate(_rms(x, ng[2]), s2, c2)
        if l < N_A:
            out, S_T, last = _rwkv7_time_mix(hm, shift_init[l], wkv_init[l], l, P)
            new_wkv.append(S_T)
            new_shift.append(last)
        else:
            out = _diff_attn_layer(hm, pos, k_all, v_all, kpos, l - N_A, l, P)
        x = x + g2 * _rms(out, ng[3])
        hf = _modulate(_rms(x, ng[4]), s3, c3)
        x = x + 0.5 * g3 * _rms(_swiglu(hf, P['ffn_w_in'][l, 1], P['ffn_w_out'][l, 1]), ng[5])
    return x, jnp.stack(new_wkv), jnp.stack(new_shift), k_new, v_new


def setup_inputs(seed: int = 0) -> dict:
    key = jax.random.key(seed)
    ks = iter(jax.random.split(key, 64))
    D = D_MODEL

    def nrm(shape, scale):
        return jax.random.normal(next(ks), shape, F32) * scale

    def unif(shape, lo, hi):
        return jax.random.uniform(next(ks), shape, F32, lo, hi)

    return {
        "x_prompt": nrm((BATCH, SEQ, D), 1.0),
        "x_sample": nrm((DEC_BATCH, DEC_SEQ, D), 1.0),
        "c_prompt": nrm((BATCH, D), 1.0),
        "c_sample": nrm((DEC_BATCH, D), 1.0),
        "state_wkv": nrm((N_A, DEC_BATCH, RWKV_HEADS, RWKV_HEAD, RWKV_HEAD), 0.5),
        "state_shift": nrm((N_A, DEC_BATCH, 1, D), 1.0),
        "cache_k": nrm((DEC_BATCH, PAST_LEN, N_HEADS_B, 2, HEAD_QK), 1.0),
        "cache_v": nrm((DEC_BATCH, PAST_LEN, N_HEADS_B, HEAD_V), 1.0),
        "ada_w": nrm((DEPTH, D, 9 * D), 0.5 * D ** -0.5),
        "ada_b": nrm((DEPTH, 9 * D), 0.02),
        "norm_g": 1.0 + nrm((DEPTH, 6, D), 0.05),
        "ffn_w_in": nrm((DEPTH, 2, D, 2 * D_FF), D ** -0.5),
        "ffn_w_out": nrm((DEPTH, 2, D_FF, D), D_FF ** -0.5),
        "rwkv_mu": unif((N_A, 6, D), 0.0, 1.0),
        "rwkv_w_rkv": nrm((N_A, 3, D, D), D ** -0.5),
        "rwkv_w0": unif((N_A, D), -6.0, -1.0),
        "rwkv_w1": nrm((N_A, D, DECAY_LORA), D ** -0.5),
        "rwkv_w2": nrm((N_A, DECAY_LORA, D), 0.5 * DECAY_LORA ** -0.5),
        "rwkv_a0": nrm((N_A, D), 0.1),
        "rwkv_a1": nrm((N_A, D, A_LORA), D ** -0.5),
        "rwkv_a2": nrm((N_A, A_LORA, D), 0.5 * A_LORA ** -0.5),
        "rwkv_g1": nrm((N_A, D, GATE_LORA), D ** -0.5),
        "rwkv_g2": nrm((N_A, GATE_LORA, D), GATE_LORA ** -0.5),
        "rwkv_k_k": 0.85 + nrm((N_A, D), 0.05),
        "rwkv_k_a": 1.0 + nrm((N_A, D), 0.05),
        "rwkv_r_k": nrm((N_A, RWKV_HEADS, RWKV_HEAD), 0.1),
        "rwkv_ln_w": 1.0 + nrm((N_A, D), 0.05),
        "rwkv_ln_b": nrm((N_A, D), 0.02),
        "rwkv_w_o": nrm((N_A, D, D), D ** -0.5),
        "kv_ada_w": nrm((D, 2 * D), 0.5 * D ** -0.5),
        "kv_ada_b": nrm((2 * D,), 0.02),
        "kv_norm_g": 1.0 + nrm((D,), 0.05),
        "kv_w": nrm((D, QK_W + V_W), D ** -0.5),
        "diff_w_q": nrm((N_B, D, QK_W), D ** -0.5),
        "diff_lambda": nrm((N_B, 4, HEAD_QK), 0.1),
        "diff_subln_g": 1.0 + nrm((N_B, HEAD_V), 0.05),
        "diff_w_o": nrm((N_B, V_W, D), V_W ** -0.5),
    }


def reference(x_prompt, x_sample, c_prompt, c_sample, state_wkv, state_shift, cache_k, cache_v,
              ada_w, ada_b, norm_g, ffn_w_in, ffn_w_out,
              rwkv_mu, rwkv_w_rkv, rwkv_w0, rwkv_w1, rwkv_w2, rwkv_a0, rwkv_a1, rwkv_a2,
              rwkv_g1, rwkv_g2, rwkv_k_k, rwkv_k_a, rwkv_r_k, rwkv_ln_w, rwkv_ln_b, rwkv_w_o,
              kv_ada_w, kv_ada_b, kv_norm_g, kv_w,
              diff_w_q, diff_lambda, diff_subln_g, diff_w_o):
    P = dict(ada_w=ada_w, ada_b=ada_b, norm_g=norm_g, ffn_w_in=ffn_w_in, ffn_w_out=ffn_w_out,
             rwkv_mu=rwkv_mu, rwkv_w_rkv=rwkv_w_rkv, rwkv_w0=rwkv_w0, rwkv_w1=rwkv_w1,
             rwkv_w2=rwkv_w2, rwkv_a0=rwkv_a0, rwkv_a1=rwkv_a1, rwkv_a2=rwkv_a2,
             rwkv_g1=rwkv_g1, rwkv_g2=rwkv_g2, rwkv_k_k=rwkv_k_k, rwkv_k_a=rwkv_k_a,
             rwkv_r_k=rwkv_r_k, rwkv_ln_w=rwkv_ln_w, rwkv_ln_b=rwkv_ln_b, rwkv_w_o=rwkv_w_o,
             kv_ada_w=kv_ada_w, kv_ada_b=kv_ada_b, kv_norm_g=kv_norm_g, kv_w=kv_w,
             diff_w_q=diff_w_q, diff_lambda=diff_lambda, diff_subln_g=diff_subln_g,
             diff_w_o=diff_w_o)
    Bp = x_prompt.shape[0]
    wkv0 = jnp.zeros((N_A, Bp, RWKV_HEADS, RWKV_HEAD, RWKV_HEAD), F32)
    shift0 = jnp.zeros((N_A, Bp, 1, D_MODEL), x_prompt.dtype)
    y_prompt, wkv_prompt, shift_prompt, k_prompt, v_prompt = _trunk(
        x_prompt, c_prompt, 0, wkv0, shift0, None, None, P)
    y_sample, wkv_sample, shift_sample, k_sample, v_sample = _trunk(
        x_sample, c_sample, PAST_LEN, state_wkv, state_shift, cache_k, cache_v, P)
    return (y_prompt, y_sample, wkv_prompt, shift_prompt, k_prompt, v_prompt,
            wkv_sample, shift_sample, k_sample, v_sample)
```

```python
import numpy as np
import concourse.bass as bass
import concourse.mybir as mybir

F32 = mybir.dt.float32
BF16 = mybir.dt.bfloat16
ALU = mybir.AluOpType
AF = mybir.ActivationFunctionType
AX = mybir.AxisListType

SEM_EPOCH = 30000


class T:
    __slots__ = ("name", "lw", "rd", "excl")

    def __init__(self, name="", excl=False):
        self.name = name
        self.excl = excl
        self.lw = None
        self.rd = []


class Sched:
    CE = ("pe", "act", "dve", "pool", "sp")

    def __init__(self, nc, nsp=8, npool=4):
        self.nc = nc
        self.ops = {e: [] for e in self.CE}
        self.known = {e: {f: -1 for f in self.CE} for e in self.CE}
        self.clock = {e: [] for e in self.CE}
        self.dwaited = {e: set() for e in self.CE}
        self.nq = {"sp": nsp, "pool": npool}
        self.dq = {"sp": [], "pool": []}
        self.ndma = 0
        self.dma_info = {}
        self.ncc = 0
        self.pending_barrier = {e: None for e in self.CE}

    def _deps(self, reads, writes, eng=None):
        deps = []
        for t in reads:
            if t.lw is not None:
                deps.append((t.lw, "raw"))
            if t.excl:
                for r in t.rd:
                    if r[0] == "c" and r[1] != eng:
                        deps.append((r, "raw"))
        for t in writes:
            if t.lw is not None:
                deps.append((t.lw, "waw"))
            for r in t.rd:
                deps.append((r, "war"))
        return deps

    def _add_wait(self, eng, waits, ev, kind):
        if ev[0] == "d":
            if ev[1] in self.dwaited[eng]:
                return
            self.dwaited[eng].add(ev[1])
            waits.append(ev)
            return
        if ev[0] == "cc":
            if ev in self.dwaited[eng]:
                return
            self.dwaited[eng].add(ev)
            waits.append(ev)
            return
        _, f, j = ev
        if f == eng:
            if eng in ("pe", "sp"):
                return
            if self.known[eng][f] >= j:
                return
        elif self.known[eng][f] >= j:
            return
        waits.append(ev)
        snap = self.clock[f][j]
        k = self.known[eng]
        for g, v in snap.items():
            if v > k[g]:
                k[g] = v
        if j > k[f]:
            k[f] = j

    def _commit(self, eng, ev, reads, writes):
        for t in reads:
            t.rd.append(ev)
        for t in writes:
            t.lw = ev
            t.rd = []

    def _barrier_waits(self, eng, waits):
        b = self.pending_barrier[eng]
        if b is None:
            return
        self.pending_barrier[eng] = None
        for ev in b:
            self._add_wait(eng, waits, ev, "raw")

    def barrier(self):
        evs = []
        for e in self.CE:
            for j in range(len(self.ops[e]) - 1, -1, -1):
                if self.ops[e][j]["kind"] == "c":
                    evs.append(("c", e, j))
                    break
        for c in range(self.ncc):
            evs.append(("cc", c))
        for q in self.dq:
            for d in self.dq[q][-self.nq[q]:]:
                evs.append(("d", d))
        for e in self.CE:
            self.pending_barrier[e] = list(evs)

    def op(self, eng, fn, reads=(), writes=()):
        waits = []
        self._barrier_waits(eng, waits)
        for ev, kind in self._deps(reads, writes, eng):
            self._add_wait(eng, waits, ev, kind)
        j = len(self.ops[eng])
        self.ops[eng].append(dict(fn=fn, waits=waits, kind="c"))
        self.clock[eng].append(dict(self.known[eng]))
        ev = ("c", eng, j)
        self._commit(eng, ev, reads, writes)
        return ev

    def dma(self, q, out_ap, in_ap, reads=(), writes=(), **kw):
        waits = []
        self._barrier_waits(q, waits)
        for ev, kind in self._deps(reads, writes):
            self._add_wait(q, waits, ev, "raw")
        k = len(self.dq[q])
        n = self.nq[q]
        if k >= n:
            self._add_wait(q, waits, ("d", self.dq[q][k - n]), "raw")
        did = self.ndma
        self.ndma += 1
        self.dq[q].append(did)
        self.dma_info[did] = (q, k)
        j = len(self.ops[q])
        self.ops[q].append(dict(fn=None, waits=waits, kind="d", out=out_ap, in_=in_ap, did=did, kw=kw))
        self.clock[q].append(dict(self.known[q]))
        ev = ("d", did)
        self._commit(q, ev, reads, writes)
        return ev

    def collective(self, kind, groups, in_ap, out_ap, reads=(), writes=()):
        q = "pool"
        waits = []
        self._barrier_waits(q, waits)
        for ev, kd in self._deps(reads, writes):
            self._add_wait(q, waits, ev, "raw")
        cid = self.ncc
        self.ncc += 1
        self.ops[q].append(dict(fn=None, waits=waits, kind="cc", cckind=kind, groups=groups,
                                in_=in_ap, out=out_ap, cid=cid))
        self.clock[q].append(dict(self.known[q]))
        ev = ("cc", cid)
        self._commit(q, ev, reads, writes)
        return ev

    def emit(self, stack):
        nc = self.nc
        marked = {e: set() for e in self.CE}
        for e in self.CE:
            for o in self.ops[e]:
                for w in o["waits"]:
                    if w[0] == "c":
                        marked[w[1]].add(w[2])
        rank = {}
        csem = {}
        for e in self.CE:
            ms = sorted(marked[e])
            rank[e] = {j: i for i, j in enumerate(ms)}
            nep = (len(ms) + SEM_EPOCH - 1) // SEM_EPOCH
            csem[e] = [stack.enter_context(nc.semaphore(f"c_{e}_{i}")) for i in range(max(nep, 1))]
        dsem = {q: [stack.enter_context(nc.semaphore(f"d_{q}_{i}")) for i in range(self.nq[q])]
                for q in self.dq}
        ccsem = [stack.enter_context(nc.semaphore(f"cc_{i}")) for i in range(self.ncc)]
        self.stats = {e: (len(self.ops[e]), len(marked[e])) for e in self.CE}

        def waitspec(w):
            if w[0] == "c":
                r = rank[w[1]][w[2]]
                return csem[w[1]][r // SEM_EPOCH], (r % SEM_EPOCH) + 1
            if w[0] == "d":
                q, k = self.dma_info[w[1]]
                n = self.nq[q]
                return dsem[q][k % n], 16 * (k // n + 1)
            if w[0] == "cc":
                return ccsem[w[1]], 1
            raise ValueError(w)

        def run(engname, e):
            for j, o in enumerate(self.ops[engname]):
                for w in o["waits"]:
                    s, v = waitspec(w)
                    e.wait_ge(s, v)
                if o["kind"] == "c":
                    ins = o["fn"](e)
                    if j in rank[engname]:
                        r = rank[engname][j]
                        ins.then_inc(csem[engname][r // SEM_EPOCH], 1)
                elif o["kind"] == "d":
                    q, k = self.dma_info[o["did"]]
                    n = self.nq[q]
                    e.dma_start(out=o["out"], in_=o["in_"], **o["kw"]).then_inc(dsem[q][k % n], 16)
                elif o["kind"] == "cc":
                    e.collective_compute(o["cckind"], ALU.bypass, replica_groups=o["groups"],
                                         ins=[o["in_"]], outs=[o["out"]]).then_inc(ccsem[o["cid"]], 1)
            if engname in self.dq:
                q = engname
                for d in self.dq[q][-self.nq[q]:]:
                    s, v = waitspec(("d", d))
                    e.wait_ge(s, v)
            if engname == "pool":
                for c in range(self.ncc):
                    e.wait_ge(ccsem[c], 1)

        with nc.Block() as block:
            @block.tensor
            def _(e):
                run("pe", e)

            @block.scalar
            def _(e):
                run("act", e)

            @block.vector
            def _(e):
                run("dve", e)

            @block.gpsimd
            def _(e):
                run("pool", e)

            @block.sync
            def _(e):
                run("sp", e)


class Arena:
    def __init__(self, ap_f32, n):
        self.ap = ap_f32
        self.n = n
        self.off = 0
        self.marks = []

    def push(self):
        self.marks.append(self.off)

    def pop(self):
        self.off = self.marks.pop()

    def f32(self, n):
        a = self.ap[:, self.off:self.off + n]
        self.off += n
        self.peak = max(getattr(self, "peak", 0), self.off)
        assert self.off <= self.n, f"arena overflow {self.off} > {self.n}"
        return a

    def bf16(self, n):
        m = (n + 1) // 2
        a = self.ap[:, self.off:self.off + m].bitcast(BF16)
        self.off += m
        self.peak = max(getattr(self, "peak", 0), self.off)
        assert self.off <= self.n, f"arena overflow {self.off} > {self.n}"
        return a[:, 0:n]

import math
from contextlib import ExitStack
import ml_dtypes
from concourse.bass_utils import run_bass_kernel_spmd

D = 1024
KC = 8
L = 2176
G4 = 8704
DFF = 2816
NHC = 22
NGRP = 17
EPS = 1e-6
GN_EPS = 64e-5
GROUPS = [[0, 1, 2, 3], [4, 5, 6, 7]]
LAM_INIT = 0.8 - 0.6 * math.exp(-0.3 * 1)
ATT_SCALE = 0.125
ARENA_WORDS = 53184
STOP_AFTER = 99
PHASES = None
RWKV_STOP = 0
ATT_G = 0


class _Stop(Exception):
    pass


def _ckpt(level):
    if RWKV_STOP == level:
        raise _Stop()
SKIP_CC = None
DEBUG = False


def _mm(S, out, lhsT, rhs, start, stop, reads, writes):
    S.op("pe", lambda e: e.matmul(out, lhsT=lhsT, rhs=rhs, start=start, stop=stop), reads, writes)


def _tr(S, out, in_, ident, reads, writes):
    S.op("pe", lambda e: e.transpose(out, in_, ident), reads, writes)


def _act(S, out, in_, func, reads, writes, bias=None, scale=None):
    kw = {}
    if bias is not None:
        kw["bias"] = bias
    if scale is not None:
        kw["scale"] = scale
    S.op("act", lambda e: e.activation(out=out, in_=in_, func=func, **kw), reads, writes)


def _tt(S, eng, out, in0, in1, op, reads, writes):
    S.op(eng, lambda e: e.tensor_tensor(out=out, in0=in0, in1=in1, op=op), reads, writes)


def _ts(S, eng, out, in0, s1, s2, op0, op1, reads, writes):
    if s2 is None:
        S.op(eng, lambda e: e.tensor_scalar(out=out, in0=in0, scalar1=s1, scalar2=None, op0=op0), reads, writes)
    else:
        S.op(eng, lambda e: e.tensor_scalar(out=out, in0=in0, scalar1=s1, scalar2=s2, op0=op0, op1=op1), reads, writes)


def _stt(S, out, in0, scalar, in1, op0, op1, reads, writes):
    S.op("dve", lambda e: e.scalar_tensor_tensor(out=out, in0=in0, scalar=scalar, in1=in1, op0=op0, op1=op1),
         reads, writes)


def _cp(S, eng, out, in_, reads, writes):
    if eng == "act":
        S.op("act", lambda e: e.activation(out=out, in_=in_, func=AF.Copy), reads, writes)
    else:
        S.op(eng, lambda e: e.tensor_copy(out=out, in_=in_), reads, writes)


def _rsqrt(S, out, in_, eps, reads, t_out, scale=1.0):
    S.op("act", lambda e: e.activation(out=out, in_=in_, func=AF.Sqrt, bias=eps, scale=scale), reads, [t_out])
    S.op("dve", lambda e: e.reciprocal(out=out, in_=out), [t_out], [t_out])


def _memset(S, eng, ap, val, writes):
    S.op(eng, lambda e: e.memset(ap, val), (), writes)


def grp_pieces(gi):
    if gi < 16:
        return [(gi // 4, (gi % 4) * 512, 512, 0)]
    return [(s // 2, 2048 + 64 * (s % 2), 64, 64 * s) for s in range(8)]


def build_program():
    nc = bass.Bass("TRN2", target_bir_lowering=False)

    def din(name, shape, dt=F32):
        return nc.dram_tensor(name, list(shape), dt, kind="ExternalInput").ap()

    def dout(name, shape, dt=F32):
        return nc.dram_tensor(name, list(shape), dt, kind="ExternalOutput").ap()

    def dscr(name, shape, dt=BF16):
        return nc.dram_tensor(name, list(shape), dt).ap()

    xT = din("xT", [D, L])
    cT = din("cT", [128, KC, 3])
    ada_w = din("ada_w", [2, D, 9 * D])
    ada_bT = din("ada_bT", [128, 2, 72])
    normgT = din("normgT", [128, 2, 6, KC])
    ffn_w_in = din("ffn_w_in", [2, 2, D, 2 * DFF])
    ffn_w_out = din("ffn_w_out", [2, 2, DFF, D])
    kv_ada_w = din("kv_ada_w", [D, 2 * D])
    kv_ada_bT = din("kv_ada_bT", [128, 16])
    kv_normgT = din("kv_normgT", [128, KC])
    muT = din("muT", [128, 6, KC])
    w_rkv = din("w_rkv", [3, D, 256])
    w_l1 = din("w_l1", [D, 256])
    w_w2 = din("w_w2", [64, 256])
    w_a2 = din("w_a2", [64, 256])
    w_g2 = din("w_g2", [128, 256])
    rvecT = din("rvecT", [128, 7, 2])
    shiftT = din("shiftT", [128, KC, 8])
    wkv0 = din("wkv0", [128, 2, 8, 64])
    w_o1 = din("w_o1", [D, D])
    kvwk = din("kvwk", [D, 256])
    kvwv = din("kvwv", [D, 256])
    wq = din("wq", [D, 256])
    w_o2 = din("w_o2", [D, D])
    cache_k = din("cache_k", [8, 2048, 256])
    cache_v = din("cache_v", [8, 2048, 256])
    lamb = din("lamb", [1, 256])
    sublnT = din("sublnT", [128, 1])
    ropeC = din("ropeC", [NGRP, 128, 512])
    ropeS = din("ropeS", [NGRP, 128, 512])
    permT = din("permT", [128, 128])
    amask = din("amask", [4, 128, 512], BF16)
    selT = din("selT", [128, 4])
    cmask = din("cmask", [128, 640])
    yT = dout("yT", [D, L])
    o_wkv = dout("o_wkv", [128, 9, 2, 64])
    o_shift = dout("o_shift", [128, KC, 3])
    o_k = dout("o_k", [256, G4])
    o_v = dout("o_v", [G4, 256])
    A_in = dscr("A_in", [D, L]); A_out = dscr("A_out", [4 * D, L])
    B_in = dscr("B_in", [8 * 256, 1088]); B_out = dscr("B_out", [8 * D, 1088])
    C_in = dscr("C_in", [D, L]); C_out = dscr("C_out", [4 * D, L])
    D_in = dscr("D_in", [D, L]); D_out = dscr("D_out", [4 * D, L])
    E_in = dscr("E_in", [8 * 256, 1088]); E_out = dscr("E_out", [8 * D, 1088])
    tA_in, tA_out, tB_in, tB_out, tC_in, tC_out, tD_in, tD_out, tE_in, tE_out = [T(f"scr{i}") for i in range(10)]
    t_out = T("outs")

    with ExitStack() as st:
        arena_t = st.enter_context(nc.sbuf_tensor("arena", [128, ARENA_WORDS], F32))
        ps = [st.enter_context(nc.psum_tensor(f"ps{i}", [128, 512], F32))[:] for i in range(8)]
        tps = [T(f"ps{i}", excl=True) for i in range(8)]
        ar = Arena(arena_t[:], ARENA_WORDS)
        S = Sched(nc, nsp=8, npool=6)

        def gath_tok(buf_in, t_in, buf_out, t_out_):
            for kc in range(KC):
                S.collective("AllGather", GROUPS, buf_in[kc * 128:(kc + 1) * 128, :],
                             buf_out[kc * 512:(kc + 1) * 512, :], reads=[t_in], writes=[t_out_])

        def tok_view(buf_out, r, c0, n):
            return buf_out.rearrange("(k r p) n -> r p k n", k=KC, r=4)[r][:, :, c0:c0 + n]

        def gath_z(buf_in, t_in, buf_out, t_out_):
            for ch in range(8):
                S.collective("AllGather", GROUPS, buf_in[ch * 256:(ch + 1) * 256, :],
                             buf_out[ch * D:(ch + 1) * D, :], reads=[t_in], writes=[t_out_])

        def zsplit(g0, n):
            out = []
            rel = 0
            while n > 0:
                ch, off = g0 // 1088, g0 % 1088
                m = min(n, 1088 - off)
                out.append((ch, off, m, rel))
                g0 += m; n -= m; rel += m
            return out

        X = ar.f32(KC * L).rearrange("p (k n) -> p k n", k=KC)
        tX = [T(f"X{i}") for i in range(5)]

        def xg(col0):
            return tX[min(col0 // 512, 4)]
        ones_bf = ar.bf16(128); ones_f = ar.f32(128); ident = ar.f32(128); blk_f = ar.f32(128)
        tconst = T("const")
        modv = ar.f32(2 * 72 * 3).rearrange("p (l f s) -> p l f s", l=2, f=72)
        kvmod = ar.f32(16 * 3).rearrange("p (f s) -> p f s", f=16)
        normg = ar.f32(2 * 6 * KC).rearrange("p (l i k) -> p l i k", l=2, i=6)
        kvng = ar.f32(KC)
        gsv = ar.f32(2 * 3 * 3 * KC).rearrange("p (l w s k) -> p l w s k", l=2, w=3, s=3)
        cov = ar.f32(2 * 3 * 3 * KC).rearrange("p (l w s k) -> p l w s k", l=2, w=3, s=3)
        kgs = ar.f32(3 * KC).rearrange("p (s k) -> p s k", s=3)
        tmod = T("mod")
        _memset(S, "dve", ones_bf, 1.0, [tconst])
        _memset(S, "dve", ones_f, 1.0, [tconst])
        _memset(S, "pool", ident, 0.0, [tconst])
        S.op("pool", lambda e: e.affine_select(out=ident, in_=ident, pattern=[[-1, 128]], compare_op=ALU.not_equal,
                                               fill=1.0, base=0, channel_multiplier=1), [tconst], [tconst])
        _memset(S, "dve", blk_f, 0.0, [tconst])
        _memset(S, "dve", blk_f[0:64, 0:64], 1.0, [tconst])
        _memset(S, "dve", blk_f[64:128, 64:128], 1.0, [tconst])

        shcap = ar.f32(KC * 3).rearrange("p (k s) -> p k s", k=KC)
        t_shcap = T("shcap")
        sel = ar.f32(4)
        t_sel = T("sel")
        S.dma("sp", sel, selT, writes=[t_sel])
        cs = ar.f32(KC * 3).rearrange("p (k s) -> p k s", k=KC)
        common_start = ar.off
        NSLOT = 2
        WS = 2816
        wst_f = [ar.f32(WS) for _ in range(NSLOT)]
        wst_b = [ar.bf16(WS) for _ in range(NSLOT)]
        t_wf = [T(f"wf{i}") for i in range(NSLOT)]
        t_wb = [T(f"wb{i}") for i in range(NSLOT)]
        wctr = [0]

        def load_w(dram_view, kdim, n, cast_eng=None):
            i = wctr[0] % NSLOT
            wctr[0] += 1
            f = wst_f[i][:, 0:kdim * n].rearrange("p (k n) -> p k n", k=kdim)
            b = wst_b[i][:, 0:kdim * n].rearrange("p (k n) -> p k n", k=kdim)
            S.dma("sp", f, dram_view, writes=[t_wf[i]])
            eng = cast_eng or ("pool" if (wctr[0] % 2 == 0) else "act")
            _cp(S, eng, b, f, [t_wf[i]], [t_wb[i]])
            return b, t_wb[i]

        rstd = [ar.f32(512) for _ in range(2)]
        t_rstd = [T("rstd0"), T("rstd1")]
        sqb = [ar.bf16(512) for _ in range(3)]
        t_sq = [T(f"sq{i}") for i in range(3)]
        tmpf = [ar.f32(512) for _ in range(3)]
        t_tmp = [T(f"tmp{i}") for i in range(3)]
        ctr = {"sq": 0, "tmp": 0, "rstd": 0}

        def nxt(kind, n):
            i = ctr[kind] % n
            ctr[kind] += 1
            return i

        for gi, (c0, n) in enumerate([(0, 512), (512, 512), (1024, 512), (1536, 512), (2048, 128)]):
            S.dma("sp", X[:, :, c0:c0 + n], xT[:, c0:c0 + n].rearrange("(k p) n -> p k n", p=128), writes=[tX[gi]])
        t_cs = T("cs")
        S.dma("sp", cs, cT, writes=[t_cs])
        S.dma("sp", normg, normgT, writes=[tmod])
        S.dma("sp", kvng, kv_normgT, writes=[tmod])
        _act(S, cs, cs, AF.Silu, [t_cs], [t_cs])
        ar.push()
        ast = [ar.f32(KC * 256).rearrange("p (k n) -> p k n", k=KC) for _ in range(2)]
        t_ast = [T("ast0"), T("ast1")]
        bias_tmp = ar.f32(2 * 72 + 16)
        S.dma("sp", bias_tmp[:, 0:144].rearrange("p (l f) -> p l f", l=2), ada_bT, writes=[tmod])
        S.dma("sp", bias_tmp[:, 144:160], kv_ada_bT, writes=[tmod])
        ai = 0
        jobs = [(ada_w[0], 36, lambda f: (modv[:, 0, f, :], bias_tmp[:, f:f + 1])),
                (ada_w[1], 36, lambda f: (modv[:, 1, f, :], bias_tmp[:, 72 + f:72 + f + 1])),
                (kv_ada_w, 8, lambda f: (kvmod[:, f, :], bias_tmp[:, 144 + f:144 + f + 1]))]
        for wsrc, ntile, dst in jobs:
            for ti in range(ntile):
                sl = ai % 2
                ai += 1
                S.dma("sp", ast[sl], wsrc[:, ti * 256:(ti + 1) * 256].rearrange("(k p) n -> p k n", p=128),
                      writes=[t_ast[sl]])
                pb = ai % 2
                for fi in range(2):
                    for kc in range(KC):
                        _mm(S, ps[pb][:, fi * 4:fi * 4 + 3], ast[sl][:, kc, fi * 128:(fi + 1) * 128], cs[:, kc, :],
                            kc == 0, kc == KC - 1, [t_ast[sl], t_cs], [tps[pb]])
                for fi in range(2):
                    o, b = dst(ti * 2 + fi)
                    _ts(S, "dve", o, ps[pb][:, fi * 4:fi * 4 + 3], b, None, ALU.add, None, [tps[pb], tmod], [tmod])
        ar.pop()
        SQD = 32.0
        for l in range(2):
            for w in range(3):
                for s in range(3):
                    sc_ap = modv[:, l, (3 * w + 1) * 8:(3 * w + 2) * 8, s]
                    g_ap = modv[:, l, (3 * w + 2) * 8:(3 * w + 3) * 8, s]
                    _stt(S, gsv[:, l, w, s, :], sc_ap, 1.0, normg[:, l, 2 * w, :], ALU.add, ALU.mult, [tmod], [tmod])
                    _ts(S, "dve", gsv[:, l, w, s, :], gsv[:, l, w, s, :], SQD, None, ALU.mult, None, [tmod], [tmod])
                    fct = (1.0 if w == 1 else 0.5) * SQD
                    _stt(S, cov[:, l, w, s, :], g_ap, fct, normg[:, l, 2 * w + 1, :], ALU.mult, ALU.mult, [tmod], [tmod])
        for s in range(3):
            _stt(S, kgs[:, s, :], kvmod[:, 8:16, s], 1.0, kvng, ALU.add, ALU.mult, [tmod], [tmod])
            _ts(S, "dve", kgs[:, s, :], kgs[:, s, :], SQD, None, ALU.mult, None, [tmod], [tmod])

        def seq_of(c0):
            return 0 if c0 < 2048 else 1 + (c0 - 2048) // 64

        def rms_rstd(src3, c0, n, src_reads, psb):
            ri = nxt("rstd", 2)
            for kc in range(KC):
                qi = nxt("sq", 3)
                _act(S, sqb[qi][:, 0:n], src3[:, kc, c0:c0 + n], AF.Square, src_reads, [t_sq[qi]])
                _mm(S, ps[psb][:, 0:n], ones_bf, sqb[qi][:, 0:n], kc == 0, kc == KC - 1, [t_sq[qi], tconst], [tps[psb]])
            _rsqrt(S, rstd[ri][:, 0:n], ps[psb][:, 0:n], EPS * D, [tps[psb]], t_rstd[ri])
            return ri

        def modulate(c0, n, gs_ap, sh_ap, dst3, dcol0, dst_t, psb, cap=None):
            ri = rms_rstd(X, c0, n, [xg(c0)], psb)
            for kc in range(KC):
                ti = nxt("tmp", 3)
                _stt(S, tmpf[ti][:, 0:n], X[:, kc, c0:c0 + n], gs_ap[:, kc:kc + 1], rstd[ri][:, 0:n], ALU.mult, ALU.mult,
                     [xg(c0), t_rstd[ri], tmod], [t_tmp[ti]])
                _act(S, dst3[:, kc, dcol0:dcol0 + n], tmpf[ti][:, 0:n], AF.Identity, [t_tmp[ti], tmod], [dst_t],
                     bias=sh_ap[:, kc:kc + 1])
                if cap is not None:
                    for (lc, oc) in cap:
                        _act(S, shcap[:, kc, oc:oc + 1], tmpf[ti][:, lc:lc + 1], AF.Identity, [t_tmp[ti], tmod], [t_shcap],
                             bias=sh_ap[:, kc:kc + 1])

        def postnorm_add(src3, src_t, scol0, c0, n, co_ap, psb):
            ri = rms_rstd(src3, scol0, n, [src_t], psb)
            for kc in range(KC):
                ti = nxt("tmp", 3)
                _stt(S, tmpf[ti][:, 0:n], src3[:, kc, scol0:scol0 + n], co_ap[:, kc:kc + 1], rstd[ri][:, 0:n],
                     ALU.mult, ALU.mult, [src_t, t_rstd[ri], tmod], [t_tmp[ti]])
                _tt(S, "pool", X[:, kc, c0:c0 + n], X[:, kc, c0:c0 + n], tmpf[ti][:, 0:n], ALU.add,
                    [t_tmp[ti], xg(c0)], [xg(c0)])


        def ffn(l, i, w):
            ar.push()
            NP = 1088
            hF = ar.f32(KC * NP)
            hb = hF.bitcast(BF16)[:, 0:KC * NP].rearrange("p (k n) -> p k n", k=KC)
            Fo = hF.rearrange("p (k n) -> p k n", k=KC)
            t_hF = T("hF")
            hid = ar.bf16(NHC * NP).rearrange("p (k n) -> p k n", k=NHC)
            t_hid = [T("hid0"), T("hid1"), T("hid2")]
            sg = [ar.bf16(512) for _ in range(2)]
            t_sg = [T("sg0"), T("sg1")]
            for p in range(2):
                cgs = [(p * 1024, 512, 0), (p * 1024 + 512, 512, 512), (2048 + 64 * p, 64, 1024)]
                for (c0, n, hc0) in cgs:
                    s = seq_of(c0)
                    modulate(c0, n, gsv[:, l, w, s, :], modv[:, l, 3 * w * 8:(3 * w + 1) * 8, s], hb, hc0, t_hF, 6)
                for hc in range(NHC):
                    wv = ffn_w_in[l, i].rearrange("(k p) (u n) -> p k u n", p=128, u=2)[:, :, :, hc * 128:(hc + 1) * 128]
                    si = wctr[0] % NSLOT
                    wctr[0] += 1
                    f = wst_f[si][:, 0:KC * 256].rearrange("p (k u n) -> p k u n", k=KC, u=2)
                    b = wst_b[si][:, 0:KC * 256].rearrange("p (k u n) -> p k u n", k=KC, u=2)
                    for u in range(2):
                        S.dma("sp", f[:, :, u, :], wv[:, :, u, :], writes=[t_wf[si]])
                    _cp(S, "pool" if hc % 2 == 0 else "act", b, f, [t_wf[si]], [t_wb[si]])
                    for ci, (c0, n, hc0) in enumerate(cgs):
                        pg, pu = 2 * ci, 2 * ci + 1
                        for kc in range(KC):
                            _mm(S, ps[pg][:, 0:n], b[:, kc, 0, :], hb[:, kc, hc0:hc0 + n], kc == 0, kc == KC - 1,
                                [t_wb[si], t_hF], [tps[pg]])
                        for kc in range(KC):
                            _mm(S, ps[pu][:, 0:n], b[:, kc, 1, :], hb[:, kc, hc0:hc0 + n], kc == 0, kc == KC - 1,
                                [t_wb[si], t_hF], [tps[pu]])
                        gi2 = (hc * 3 + ci) % 2
                        _act(S, sg[gi2][:, 0:n], ps[pg][:, 0:n], AF.Silu, [tps[pg]], [t_sg[gi2]])
                        _tt(S, "dve", hid[:, hc, hc0:hc0 + n], sg[gi2][:, 0:n], ps[pu][:, 0:n], ALU.mult,
                            [t_sg[gi2], tps[pu]], [t_hid[ci]])
                for dc in range(KC):
                    wv = ffn_w_out[l, i][:, dc * 128:(dc + 1) * 128].rearrange("(k p) n -> p k n", p=128)
                    b, tb = load_w(wv, NHC, 128, cast_eng="pool" if dc % 2 == 0 else "act")
                    for ci, (c0, n, hc0) in enumerate(cgs):
                        pb = ci
                        for hc in range(NHC):
                            _mm(S, ps[pb][:, 0:n], b[:, hc, :], hid[:, hc, hc0:hc0 + n], hc == 0, hc == NHC - 1,
                                [tb, t_hid[ci]], [tps[pb]])
                        _cp(S, "act" if ci % 2 == 0 else "dve", Fo[:, dc, hc0:hc0 + n], ps[pb][:, 0:n], [tps[pb]], [t_hF])
                for (c0, n, hc0) in cgs:
                    s = seq_of(c0)
                    postnorm_add(Fo, t_hF, hc0, c0, n, cov[:, l, w, s, :], 7)
            ar.pop()

        def emit_hm(gs_sel, sh_sel, dst_in, t_dst, capture):
            ar.push()
            hb = [ar.bf16(KC * 512).rearrange("p (k n) -> p k n", k=KC) for _ in range(2)]
            t_hb = [T("hmb0"), T("hmb1")]
            for gi, (c0, n) in enumerate([(0, 512), (512, 512), (1024, 512), (1536, 512), (2048, 64), (2112, 64)]):
                s = seq_of(c0)
                bi = gi % 2
                cap = None
                if capture:
                    if c0 == 1536:
                        cap = [(511, 0)]
                    elif c0 >= 2048:
                        cap = [(63, 1 + (c0 - 2048) // 64)]
                modulate(c0, n, gs_sel(s), sh_sel(s), hb[bi], 0, t_hb[bi], 6, cap=cap)
                S.dma("pool", dst_in[:, c0:c0 + n].rearrange("(k p) n -> p k n", p=128), hb[bi][:, :, 0:n],
                      reads=[t_hb[bi]], writes=[t_dst])
            ar.pop()

        def gcol_of(gi, tt):
            if gi < 16:
                return (gi // 4) * L + (gi % 4) * 512 + tt * 128
            return tt * L + 2048

        def attention():
            ar.push()
            ar.off = common_start
            KT = ar.bf16(2 * G4).rearrange("p (h n) -> p h n", h=2)
            Vt = ar.bf16(68 * 256).rearrange("p (t n) -> p t n", t=68)
            t_KT = T("KT"); t_Vt = T("Vt")
            Wk = ar.bf16(KC * 256).rearrange("p (k n) -> p k n", k=KC)
            Wv = ar.bf16(KC * 256).rearrange("p (k n) -> p k n", k=KC)
            Wq = ar.bf16(KC * 256).rearrange("p (k n) -> p k n", k=KC)
            t_W = T("attW")
            _mark = ar.off
            stg = ar.f32(2048)
            t_stg = T("stg")
            for (wd, wb) in ((kvwk, Wk), (kvwv, Wv), (wq, Wq)):
                S.dma("sp", stg.rearrange("p (k n) -> p k n", k=KC), wd.rearrange("(k p) n -> p k n", p=128), writes=[t_stg])
                _cp(S, "act", wb, stg.rearrange("p (k n) -> p k n", k=KC), [t_stg], [t_W])
            S.barrier()
            ar.off = _mark
            PT = ar.f32(128)
            S.dma("sp", PT, permT, writes=[t_W])
            am = ar.bf16(4 * 512).rearrange("p (j n) -> p j n", j=4)
            S.dma("sp", am, amask.rearrange("j p n -> p j n"), writes=[t_W])
            sub = ar.f32(1)
            S.dma("sp", sub, sublnT, writes=[t_W])
            lam_t = ar.f32(256)
            S.dma("sp", lam_t, lamb.partition_broadcast(128), writes=[t_W])
            lsc = ar.f32(8)
            _tt(S, "dve", lam_t[:, 0:64], lam_t[:, 0:64], lam_t[:, 64:128], ALU.mult, [t_W], [t_W])
            _tt(S, "dve", lam_t[:, 128:192], lam_t[:, 128:192], lam_t[:, 192:256], ALU.mult, [t_W], [t_W])
            S.op("dve", lambda e: e.reduce_sum(out=lsc[:, 0:1], in_=lam_t[:, 0:64], axis=AX.X), [t_W], [t_W])
            S.op("dve", lambda e: e.reduce_sum(out=lsc[:, 1:2], in_=lam_t[:, 128:192], axis=AX.X), [t_W], [t_W])
            _act(S, lsc[:, 2:4], lsc[:, 0:2], AF.Exp, [t_W], [t_W])
            _stt(S, lsc[:, 4:5], lsc[:, 3:4], -LAM_INIT, lsc[:, 2:3], ALU.add, ALU.subtract, [t_W], [t_W])
            _ts(S, "dve", sub, sub, 1.0 - LAM_INIT, None, ALU.mult, None, [t_W], [t_W])
            _ckpt(11)
            hkb = ar.bf16(KC * 512).rearrange("p (k n) -> p k n", k=KC)
            t_hk = T("hkb")
            rc = ar.f32(512); rs = ar.f32(512)
            t_rope = T("rope")
            kA = ar.f32(512); kr = ar.f32(512); kt2 = ar.f32(512)
            t_kA = T("kA"); t_kr = T("kr"); t_kt2 = T("kt2")
            vst = [ar.f32(256) for _ in range(2)]
            t_vst = [T("vst0"), T("vst1")]

            def load_grp(src, t_src, gi):
                for (r, c0, n, dst) in grp_pieces(gi):
                    S.dma("sp", hkb[:, :, dst:dst + n],
                          tok_view(src, r, c0, n),
                          reads=[t_src], writes=[t_hk])
                S.dma("sp", rc, ropeC[gi], writes=[t_rope])
                S.dma("sp", rs, ropeS[gi], writes=[t_rope])

            def proj_rope(W, hc, scale_out=None):
                for kc in range(KC):
                    _mm(S, ps[0], W[:, kc, hc * 128:(hc + 1) * 128], hkb[:, kc, :], kc == 0, kc == KC - 1, [t_W, t_hk], [tps[0]])
                _cp(S, "act", kA, ps[0], [tps[0]], [t_kA])
                _mm(S, ps[1], PT, kA, True, True, [t_W, t_kA], [tps[1]])
                _tt(S, "dve", kr, kA, rc, ALU.mult, [t_kA, t_rope], [t_kr])
                _tt(S, "dve", kt2, ps[1], rs, ALU.mult, [tps[1], t_rope], [t_kt2])
                _tt(S, "dve", kr, kr, kt2, ALU.add, [t_kr, t_kt2], [t_kr])

            for gi in range(NGRP):
                load_grp(C_out, tC_out, gi)
                for hc in range(2):
                    proj_rope(Wk, hc)
                    for (r, c0, n, dst) in grp_pieces(gi):
                        g0 = r * L + c0
                        _cp(S, "act", KT[:, hc, g0:g0 + n], kr[:, dst:dst + n], [t_kr], [t_KT])
                        S.dma("pool", o_k[hc * 128:(hc + 1) * 128, g0:g0 + n], kr[:, dst:dst + n], reads=[t_kr], writes=[t_out])
                for tt in range(4):
                    vi = tt % 2
                    for kc in range(KC):
                        _mm(S, ps[2 + vi][:, 0:256], hkb[:, kc, tt * 128:(tt + 1) * 128], Wv[:, kc, :], kc == 0, kc == KC - 1,
                            [t_W, t_hk], [tps[2 + vi]])
                    g0 = gcol_of(gi, tt)
                    _cp(S, "act", vst[vi], ps[2 + vi][:, 0:256], [tps[2 + vi]], [t_vst[vi]])
                    _cp(S, "dve", Vt[:, g0 // 128, :], vst[vi], [t_vst[vi]], [t_Vt])
                    S.dma("pool", o_v[g0:g0 + 128, :], vst[vi], reads=[t_vst[vi]], writes=[t_out])
                if gi == ATT_G:
                    _ckpt(12)
            S.barrier()
            _ckpt(13)
            Qp = [[ar.bf16(512) for _ in range(2)] for _ in range(2)]
            t_Qp = [[T(f"Qp{a}{b}") for b in range(2)] for a in range(2)]
            for a_ in range(2):
                for b_ in range(2):
                    _memset(S, "pool", Qp[a_][b_], 0.0, [t_Qp[a_][b_]])

            def qproj_gen(gq_):
                load_grp(D_out, tD_out, gq_)
                yield
                for hc_ in range(2):
                    for kc in range(KC):
                        _mm(S, ps[0], Wq[:, kc, hc_ * 128:(hc_ + 1) * 128], hkb[:, kc, :], kc == 0, kc == KC - 1, [t_W, t_hk], [tps[0]])
                    yield
                    _cp(S, "act", kA, ps[0], [tps[0]], [t_kA])
                    yield
                    _mm(S, ps[0], PT, kA, True, True, [t_W, t_kA], [tps[0]])
                    _tt(S, "dve", kr, kA, rc, ALU.mult, [t_kA, t_rope], [t_kr])
                    yield
                    _tt(S, "dve", kt2, ps[0], rs, ALU.mult, [tps[0], t_rope], [t_kt2])
                    yield
                    _tt(S, "dve", kr, kr, kt2, ALU.add, [t_kr, t_kt2], [t_kr])
                    yield
                    _cp(S, "act", Qp[hc_][0][0:64, :], kr[0:64, :], [t_kr], [t_Qp[hc_][0]])
                    _cp(S, "act", Qp[hc_][1][64:128, :], kr[64:128, :], [t_kr], [t_Qp[hc_][1]])
                    yield
            Eb = [[ar.bf16(512) for _ in range(2)] for _ in range(2)]
            t_Eb = [[T(f"E{a}{b}") for b in range(2)] for a in range(2)]
            o0 = ar.f32(512); o1 = ar.f32(512); rr = ar.f32(512)
            t_o0 = T("o0"); t_o1 = T("o1"); t_rr = T("rr")
            zb = [ar.bf16(512) for _ in range(2)]
            t_zb = [T("zo0"), T("zo1")]
            Eacc = [ar.f32(512) for _ in range(2)]
            t_Eacc = [T("Eacc0"), T("Eacc1")]

            def finish(hc, n, pO0, pO1, pS0, pS1, dsts):
                S.op("dve", lambda e: e.reciprocal(out=rr[:, 0:n], in_=pS0), [tps[6]], [t_rr])
                _tt(S, "dve", o0[:, 0:n], pO0, rr[:, 0:n], ALU.mult, [tps[4], t_rr], [t_o0])
                S.op("dve", lambda e: e.reciprocal(out=rr[:, 0:n], in_=pS1), [tps[7], tps[6]], [t_rr])
                _tt(S, "dve", o1[:, 0:n], pO1, rr[:, 0:n], ALU.mult, [tps[5], tps[4], t_rr], [t_o1])
                _stt(S, o0[:, 0:n], o1[:, 0:n], lsc[:, 4:5], o0[:, 0:n], ALU.mult, ALU.add, [t_o1, t_o0, t_W], [t_o0])
                _tt(S, "pool", o1[:, 0:n], o0[:, 0:n], o0[:, 0:n], ALU.mult, [t_o0], [t_o1])
                _mm(S, ps[0][:, 0:n], ones_f, o1[:, 0:n], True, True, [tconst, t_o1], [tps[0]])
                _rsqrt(S, rr[:, 0:n], ps[0][:, 0:n], EPS, [tps[0]], t_rr, scale=1.0 / 128)
                _tt(S, "dve", o0[:, 0:n], o0[:, 0:n], rr[:, 0:n], ALU.mult, [t_o0, t_rr], [t_o0])
                zi = hc
                _ts(S, "dve", zb[zi][:, 0:n], o0[:, 0:n], sub[:, 0:1], None, ALU.mult, None, [t_o0, t_W], [t_zb[zi]])
                for (g0, d0, nn) in dsts:
                    for (ch, off, m, rel) in zsplit(g0, nn):
                        S.dma("pool", E_in[ch * 256 + hc * 128:ch * 256 + (hc + 1) * 128, off:off + m],
                              zb[zi][:, d0 + rel:d0 + rel + m], reads=[t_zb[zi]], writes=[tE_in])

            for gq in range(16):
                for _ in qproj_gen(gq):
                    pass
                nkt = 4 * (gq + 1)
                EbL = [Eb[0][0], Eb[0][1], Eb[1][0], Eb[1][1]]
                t_EbL = [t_Eb[0][0], t_Eb[0][1], t_Eb[1][0], t_Eb[1][1]]
                LOOK = 2
                for hc in range(2):
                    units = [(kt_, c) for kt_ in range(nkt) for c in range(2)]

                    def emit_s(i, hc=hc, units=units):
                        kt_, c = units[i]
                        g0 = (kt_ // 16) * L + (kt_ % 16) * 128
                        sb = 1 + i % 3
                        _mm(S, ps[sb], KT[:, hc, g0:g0 + 128], Qp[hc][c], True, True, [t_KT, t_Qp[hc][c]], [tps[sb]])

                    def emit_rest(i, hc=hc, units=units, nkt=nkt, gq=gq):
                        kt_, c = units[i]
                        g0 = (kt_ // 16) * L + (kt_ % 16) * 128
                        sb = 1 + i % 3
                        E_, tE_ = EbL[i % 4], t_EbL[i % 4]
                        _act(S, E_, ps[sb], AF.Exp, [tps[sb]], [tE_], scale=ATT_SCALE)
                        if kt_ >= 4 * gq:
                            _tt(S, "pool" if c == 0 else "dve", E_, E_, am[:, kt_ - 4 * gq, :], ALU.mult, [tE_, t_W], [tE_])
                        _mm(S, ps[4 + c], Vt[:, g0 // 128, hc * 128:(hc + 1) * 128], E_, kt_ == 0, kt_ == nkt - 1,
                            [t_Vt, tE_], [tps[4 + c]])
                        aeng = "pool" if c == 0 else "dve"
                        if kt_ == 0:
                            _cp(S, aeng, Eacc[c], E_, [tE_], [t_Eacc[c]])
                        else:
                            _tt(S, aeng, Eacc[c], Eacc[c], E_, ALU.add, [tE_, t_Eacc[c]], [t_Eacc[c]])
                        if kt_ == nkt - 1:
                            _mm(S, ps[6 + c], ones_f, Eacc[c], True, True, [tconst, t_Eacc[c]], [tps[6 + c]])
                    for i in range(min(LOOK, len(units))):
                        emit_s(i)
                    for i in range(len(units)):
                        if i + LOOK < len(units):
                            emit_s(i + LOOK)
                        emit_rest(i)
                    gg0 = (gq // 4) * L + (gq % 4) * 512
                    finish(hc, 512, ps[4], ps[5], ps[6], ps[7], [(gg0, 0, 512)])
                _ckpt(14)
            _ckpt(15)
            for _ in qproj_gen(16):
                pass
            ck = [ar.f32(256) for _ in range(2)]; cv = [ar.f32(256) for _ in range(2)]
            t_ck = [T("ck0"), T("ck1")]; t_cv = [T("cv0"), T("cv1")]
            kTt = [ar.bf16(256) for _ in range(2)]; vbt = [ar.bf16(256) for _ in range(2)]
            t_kTt = [T("kTt0"), T("kTt1")]; t_vbt = [T("vbt0"), T("vbt1")]
            EbS = [Eb[0][0], Eb[0][1], Eb[1][0], Eb[1][1]]
            t_EbS = [t_Eb[0][0], t_Eb[0][1], t_Eb[1][0], t_Eb[1][1]]
            for s in range(8):
                q0 = 64 * s
                gs0 = (s // 2) * L + 2048 + 64 * (s % 2)
                R_ = slice(64 * (s % 2), 64 * (s % 2) + 64)
                for hc in range(2):
                    hcs = slice(hc * 128, (hc + 1) * 128)
                    units = [(kt_, c) for kt_ in range(17) for c in range(2)]

                    def s_emit_s(i, s=s, hc=hc, hcs=hcs, units=units, q0=q0, gs0=gs0, R_=R_):
                        kt_, c = units[i]
                        bi = kt_ % 2
                        last = kt_ == 16
                        P = slice(64 * c, 64 * c + 64)
                        sb = 1 + i % 3
                        if c == 0 and not last:
                            S.dma("sp", ck[bi][:, 0:128], cache_k[s, kt_ * 128:(kt_ + 1) * 128, hcs], writes=[t_ck[bi]])
                            S.dma("sp", cv[bi][:, 0:128], cache_v[s, kt_ * 128:(kt_ + 1) * 128, hcs], writes=[t_cv[bi]])
                            _cp(S, "pool", vbt[bi][:, 0:128], cv[bi][:, 0:128], [t_cv[bi]], [t_vbt[bi]])
                            _tr(S, ps[0][:, 0:128], ck[bi][:, 0:128], ident, [t_ck[bi], tconst], [tps[0]])
                            _cp(S, "act", kTt[bi][:, 0:128], ps[0][:, 0:128], [tps[0]], [t_kTt[bi]])
                        if not last:
                            _mm(S, ps[sb][:, 0:64], kTt[bi][:, 0:128], Qp[hc][c][:, q0:q0 + 64], True, True,
                                [t_kTt[bi], t_Qp[hc][c]], [tps[sb]])
                        else:
                            _mm(S, ps[sb][R_, 0:64], KT[:, hc, gs0:gs0 + 64], Qp[hc][c][:, q0:q0 + 64], True, True,
                                [t_KT, t_Qp[hc][c]], [tps[sb]])

                    def s_emit_rest(i, hc=hc, hcs=hcs, units=units, gs0=gs0, R_=R_):
                        kt_, c = units[i]
                        bi = kt_ % 2
                        last = kt_ == 16
                        sb = 1 + i % 3
                        E_, tE_ = EbS[i % 4], t_EbS[i % 4]
                        if not last:
                            so = ps[sb][:, 0:64]
                            eo = E_[:, 0:64]
                            vl = vbt[bi][:, 0:128]
                            vr = [t_vbt[bi]]
                        else:
                            _memset(S, "pool", E_[:, 0:64], 0.0, [tE_])
                            so = ps[sb][R_, 0:64]
                            eo = E_[R_, 0:64]
                            vl = Vt[:, gs0 // 128, hcs]
                            vr = [t_Vt]
                        _act(S, eo, so, AF.Exp, [tps[sb]], [tE_], scale=ATT_SCALE)
                        ef = E_[:, 0:64]
                        _mm(S, ps[4 + c][:, 0:64], vl, ef, kt_ == 0, last, vr + [tE_], [tps[4 + c]])
                        _mm(S, ps[6 + c][:, 0:64], ones_bf, ef, kt_ == 0, last, [tconst, tE_], [tps[6 + c]])
                    for i in range(2):
                        s_emit_s(i)
                    for i in range(len(units)):
                        if i + 2 < len(units):
                            s_emit_s(i + 2)
                        s_emit_rest(i)
                    finish(hc, 64, ps[4][:, 0:64], ps[5][:, 0:64], ps[6][:, 0:64], ps[7][:, 0:64], [(gs0, 0, 64)])
            ar.pop()


        def out_proj(Zout, t_Z, w_dram, co_sel):
            ar.push()
            zb = ar.bf16(KC * 1088).rearrange("p (k n) -> p k n", k=KC)
            t_zb = T("zb")
            zt = [ar.bf16(KC * 1088).rearrange("p (k n) -> p k n", k=KC) for _ in range(2)]
            t_zt = [T("zt0"), T("zt1")]
            Fo = ar.f32(KC * 1088).rearrange("p (k n) -> p k n", k=KC)
            t_Fo = T("Fo2")
            for p in range(2):
                cgs = [(p * 1024, 512, 0), (p * 1024 + 512, 512, 512), (2048 + 64 * p, 64, 1024)]
                for r in range(4):
                    zi = r % 2
                    for (c0, n, hc0) in cgs:
                        for (ch, off, m, rel) in zsplit(r * L + c0, n):
                            S.dma("sp", zt[zi][:, :, hc0 + rel:hc0 + rel + m],
                                  Zout[ch * D:(ch + 1) * D, off:off + m].rearrange("(k p) n -> p k n", p=128),
                                  reads=[t_Z], writes=[t_zt[zi]])
                    if r == 0:
                        _ts(S, "dve", zb, zt[zi], sel[:, 0:1], None, ALU.mult, None, [t_zt[zi], t_sel], [t_zb])
                    else:
                        for kc in range(KC):
                            _stt(S, zb[:, kc, :], zt[zi][:, kc, :], sel[:, r:r + 1], zb[:, kc, :], ALU.mult, ALU.add,
                                 [t_zt[zi], t_sel, t_zb], [t_zb])
                for dc in range(KC):
                    wv = w_dram[:, dc * 128:(dc + 1) * 128].rearrange("(k p) n -> p k n", p=128)
                    b, tb = load_w(wv, KC, 128)
                    for ci, (c0, n, hc0) in enumerate(cgs):
                        pb = ci
                        for kc in range(KC):
                            _mm(S, ps[pb][:, 0:n], b[:, kc, :], zb[:, kc, hc0:hc0 + n], kc == 0, kc == KC - 1,
                                [tb, t_zb], [tps[pb]])
                        _cp(S, "act" if ci % 2 == 0 else "dve", Fo[:, dc, hc0:hc0 + n], ps[pb][:, 0:n], [tps[pb]], [t_Fo])
                for (c0, n, hc0) in cgs:
                    postnorm_add(Fo, t_Fo, hc0, c0, n, co_sel(seq_of(c0)), 7)
            ar.pop()

        def rwkv():
            ar.push()
            ar.off = common_start
            cm = ar.f32(640)
            t_cm = T("cm")
            S.dma("sp", cm, cmask, writes=[t_cm])
            m4 = cm[:, 0:512]
            m_ij = cm[:, 512:640]
            scanm = ar.f32(512)
            _memset(S, "pool", scanm, 1.0, [t_cm])
            _memset(S, "pool", scanm.rearrange("p (c t) -> p c t", t=64)[:, :, 0:1], 0.0, [t_cm])
            rv = ar.f32(14).rearrange("p (w h) -> p w h", w=7)
            mu = ar.f32(48).rearrange("p (w k) -> p w k", w=6)
            shf = ar.f32(64).rearrange("p (k s) -> p k s", k=KC)
            S.dma("sp", rv, rvecT, writes=[t_cm])
            S.dma("sp", mu, muT, writes=[t_cm])
            S.dma("sp", shf, shiftT, writes=[t_cm])
            Wst = ar.bf16(16 * 768).rearrange("p (k w n) -> p k w n", k=16, w=3)
            Wl = ar.bf16(16 * 256).rearrange("p (k n) -> p k n", k=16)
            W2 = ar.bf16(768)
            t_W = T("rwkvW")
            NT = 9
            tfall = ar.f32(NT * 512)
            tf = [tfall[:, i * 512:(i + 1) * 512] for i in range(NT)]
            t_wf = [T("stgA"), T("stgB")]
            wst_l = [tfall[:, 0:2048], tfall[:, 2048:4096]]
            lctr = [0]
            for pi, mi in ((0, 0), (1, 2), (2, 3)):
                si = lctr[0] % 2
                lctr[0] += 1
                f = wst_l[si][:, 0:KC * 256].rearrange("p (k n) -> p k n", k=KC)
                S.dma("sp", f, w_rkv[pi].rearrange("(k p) n -> p k n", p=128), writes=[t_wf[si]])
                _cp(S, "act", Wst[:, 0:8, pi, :], f, [t_wf[si]], [t_W])
                for kc in range(KC):
                    _ts(S, "dve", Wst[:, 8 + kc, pi, :], f[:, kc, :], mu[:, mi, kc:kc + 1], None, ALU.mult, None,
                        [t_wf[si], t_cm], [t_W])
            si = lctr[0] % 2
            lctr[0] += 1
            f = wst_l[si][:, 0:KC * 256].rearrange("p (k n) -> p k n", k=KC)
            S.dma("sp", f, w_l1.rearrange("(k p) n -> p k n", p=128), writes=[t_wf[si]])
            _cp(S, "act", Wl[:, 0:8, :], f, [t_wf[si]], [t_W])
            for kc in range(KC):
                for (a0_, a1_, mi) in ((0, 64, 1), (64, 128, 4), (128, 256, 5)):
                    _ts(S, "dve", Wl[:, 8 + kc, a0_:a1_], f[:, kc, a0_:a1_], mu[:, mi, kc:kc + 1], None, ALU.mult, None,
                        [t_wf[si], t_cm], [t_W])
            si = lctr[0] % 2
            lctr[0] += 1
            f = wst_l[si][:, 0:768]
            S.dma("sp", f[0:64, 0:256], w_w2, writes=[t_wf[si]])
            S.dma("sp", f[0:64, 256:512], w_a2, writes=[t_wf[si]])
            S.dma("sp", f[:, 512:768], w_g2, writes=[t_wf[si]])
            _cp(S, "act", W2[0:64, 0:512], f[0:64, 0:512], [t_wf[si]], [t_W])
            _cp(S, "act", W2[:, 512:768], f[:, 512:768], [t_wf[si]], [t_W])

            Hb1 = ar.bf16(KC * 514).rearrange("p (k n) -> p k n", k=KC)
            Hb = [Hb1, Hb1]
            t_H1 = T("H0")
            t_H = [t_H1, t_H1]
            lastc = ar.bf16(KC * 2).rearrange("p (k n) -> p k n", k=KC)
            t_lastc = T("lastc")
            dxb = ar.bf16(KC * 512).rearrange("p (k n) -> p k n", k=KC)
            t_dx = T("dx")
            twb = ar.bf16(512); tab = ar.bf16(512); tgb = ar.bf16(512)
            t_lm = T("loramid")
            t_tf = [T(f"tf{i}") for i in range(NT)]
            S.barrier()
            _ckpt(1)
            rt = [ar.bf16(512) for _ in range(2)]; kt = [ar.bf16(512) for _ in range(2)]
            at = [ar.bf16(512) for _ in range(2)]; bt = [ar.bf16(512) for _ in range(2)]
            t_fm = [T("fm0"), T("fm1")]
            bon = [ar.f32(512) for _ in range(2)]; gg = [ar.bf16(512) for _ in range(2)]
            t_bg = [T("bg0"), T("bg1")]
            gCs = ar.f32(16).rearrange("p (h c) -> p h c", h=2)
            t_gC = T("gC")
            Vtm = ar.bf16(4 * 256).rearrange("p (t n) -> p t n", t=4)
            Khtm = ar.bf16(4 * 256).rearrange("p (t n) -> p t n", t=4)
            Bhtm = ar.bf16(4 * 256).rearrange("p (t n) -> p t n", t=4)
            t_tm = T("tokmaj")
            A_k = [[ar.bf16(128) for _ in range(2)] for _ in range(4)]
            BS = [[ar.bf16(256) for _ in range(2)] for _ in range(4)]
            Sf = [ar.f32(128) for _ in range(4)]
            t_ut = [T(f"ut{i}") for i in range(4)]
            Tt = [[ar.bf16(128) for _ in range(4)] for _ in range(2)]
            A3 = [[ar.bf16(384) for _ in range(4)] for _ in range(2)]
            t_res = [[T(f"res{a}{b}") for b in range(4)] for a in range(2)]
            Mst = ar.f32(128).rearrange("p (h v) -> p h v", h=2)
            Mbf = ar.bf16(256).rearrange("p (k v) -> p k v", k=4)
            Mbf4 = Mbf.rearrange("p (c h) v -> p c h v", h=2)
            t_M = T("M"); t_Mbf = T("Mbf")
            Xs = ar.bf16(256); Us = ar.bf16(256)
            t_Xs = T("Xs"); t_Us = T("Us")
            Yg = [ar.f32(512) for _ in range(2)]; Zb = [ar.bf16(512) for _ in range(2)]
            t_Yg = [T("Yg0"), T("Yg1")]
            t_Yf = T("Yf"); t_Zb = [T("Zb0"), T("Zb1")]
            _memset(S, "dve", Mst, 0.0, [t_M])
            _memset(S, "dve", Mbf, 0.0, [t_Mbf])
            _memset(S, "dve", Xs, 0.0, [t_Xs])
            _memset(S, "dve", Us, 0.0, [t_Us])

            def upd_mbf(eng):
                for hh_ in range(2):
                    P_ = slice(64 * hh_, 64 * hh_ + 64)
                    _cp(S, eng, Mbf4[P_, :, hh_, :], Mst[P_, :, :], [t_M], [t_Mbf])
            PYM = 0; PXU = 1; PEX = 2

            for gi in range(NGRP):
                hi = gi % 2
                H = Hb[hi]
                if gi > 0:
                    _cp(S, "dve", lastc[:, :, 0:1], H[:, :, 512:513], [t_H[hi]], [t_lastc])
                for (r, c0, n, dst) in grp_pieces(gi):
                    S.dma("sp", H[:, :, 1 + dst:1 + dst + n],
                          tok_view(A_out, r, c0, n),
                          reads=[tA_out], writes=[t_H[hi]])
                if gi == 0:
                    _memset(S, "dve", H[:, :, 0:1], 0.0, [t_H[hi]])
                else:
                    _cp(S, "dve", H[:, :, 0:1], lastc[:, :, 0:1], [t_lastc], [t_H[hi]])
                _tt(S, "dve", dxb, H[:, :, 0:512], H[:, :, 1:513], ALU.subtract, [t_H[hi]], [t_dx])
                if gi == 16:
                    _tt(S, "dve", dxb.rearrange("p k (s t) -> p k s t", t=64)[:, :, :, 0], shf,
                        H[:, :, 1:513].rearrange("p k (s t) -> p k s t", t=64)[:, :, :, 0], ALU.subtract,
                        [t_H[hi], t_cm, t_dx], [t_dx])

                def rhs(kk):
                    return H[:, kk, 1:513] if kk < 8 else dxb[:, kk - 8, :]
                for (b, c0_, c1_, dstb, fn) in ((4, 0, 64, twb, AF.Tanh), (5, 64, 128, tab, AF.Copy), (6, 128, 256, tgb, AF.Sigmoid)):
                    m = c1_ - c0_
                    for kk in range(16):
                        _mm(S, ps[b][0:m, :], Wl[:, kk, c0_:c1_], rhs(kk), kk == 0, kk == 15, [t_W, t_H[hi], t_dx], [tps[b]])
                    _act(S, dstb[0:m, :], ps[b][0:m, :], fn, [tps[b]], [t_lm])
                _ckpt(2)
                for hc in range(2):
                    R, K_, V_, LW, CUM, A_, KKN, T1, T2 = tf
                    tR, tK, tV, tLW, tCUM, tA, tKKN, tT1, tT2 = t_tf
                    for pi, (dstt, tdst) in enumerate(((R, tR), (K_, tK), (V_, tV))):
                        b = 4 + pi
                        for kk in range(16):
                            _mm(S, ps[b], Wst[:, kk, pi, hc * 128:(hc + 1) * 128], rhs(kk), kk == 0, kk == 15,
                                [t_W, t_H[hi], t_dx], [tps[b]])
                        _cp(S, "act" if pi != 1 else "dve", dstt, ps[b], [tps[b]], [tdst])
                    _mm(S, ps[7], W2[0:64, hc * 128:(hc + 1) * 128], twb[0:64, :], True, True, [t_W, t_lm], [tps[7]])
                    _act(S, LW, ps[7], AF.Sigmoid, [tps[7], t_cm], [tLW], bias=rv[:, 0, hc:hc + 1])
                    _ts(S, "dve", LW, LW, -math.exp(-0.5), None, ALU.mult, None, [tLW], [tLW])
                    _mm(S, ps[7], W2[0:64, 256 + hc * 128:256 + (hc + 1) * 128], tab[0:64, :], True, True, [t_W, t_lm], [tps[7]])
                    _act(S, A_, ps[7], AF.Sigmoid, [tps[7], t_cm], [tA], bias=rv[:, 1, hc:hc + 1])
                    _mm(S, ps[7], W2[:, 512 + hc * 128:512 + (hc + 1) * 128], tgb, True, True, [t_W, t_lm], [tps[7]])
                    _cp(S, "act", gg[hc], ps[7], [tps[7]], [t_bg[hc]])
                    _ts(S, "dve", T1, K_, rv[:, 2, hc:hc + 1], None, ALU.mult, None, [tK, t_cm], [tT1])
                    _tt(S, "dve", T2, T1, T1, ALU.mult, [tT1], [tT2])
                    _mm(S, ps[7], blk_f, T2, True, True, [tconst, tT2], [tps[7]])
                    _rsqrt(S, T2, ps[7], 1e-24, [tps[7]], tT2)
                    _tt(S, "dve", KKN, T1, T2, ALU.mult, [tT1, tT2], [tKKN])
                    _ts(S, "dve", T1, A_, -1.0, rv[:, 3, hc:hc + 1], ALU.add, ALU.mult, [tA, t_cm], [tT1])
                    _ts(S, "dve", T1, T1, 1.0, None, ALU.add, None, [tT1], [tT1])
                    _tt(S, "dve", K_, K_, T1, ALU.mult, [tK, tT1], [tK])
                    _stt(S, T1, R, rv[:, 6, hc:hc + 1], K_, ALU.mult, ALU.mult, [tR, tK, t_cm], [tT1])
                    _mm(S, ps[7], blk_f, T1, True, True, [tconst, tT1], [tps[7]])
                    _tt(S, "dve", bon[hc], ps[7], V_, ALU.mult, [tps[7], tV], [t_bg[hc]])
                    _tt(S, "dve", A_, KKN, A_, ALU.mult, [tKKN, tA], [tA])
                    S.op("dve", lambda e, o=CUM, d0=scanm, d1=LW: e.tensor_tensor_scan(out=o, data0=d0, data1=d1, initial=0.0,
                                                                                        op0=ALU.mult, op1=ALU.add),
                         [tLW, t_cm], [tCUM])
                    cum3 = CUM.rearrange("p (c t) -> p c t", t=64)
                    _ckpt(3)
                    _act(S, gCs[:, hc, :], cum3[:, :, 63], AF.Exp, [tCUM], [t_gC])
                    _act(S, T1, CUM, AF.Exp, [tCUM], [tT1])
                    _tt(S, "dve", rt[hc], R, T1, ALU.mult, [tR, tT1], [t_fm[hc]])
                    _tt(S, "dve", T2, CUM, LW, ALU.subtract, [tCUM, tLW], [tT2])
                    _act(S, T2, T2, AF.Exp, [tT2], [tT2])
                    _stt(S, at[hc], KKN, -1.0, T2, ALU.mult, ALU.mult, [tKKN, tT2], [t_fm[hc]])
                    _act(S, T1, CUM, AF.Exp, [tCUM], [tT1], scale=-1.0)
                    _tt(S, "dve", kt[hc], K_, T1, ALU.mult, [tK, tT1], [t_fm[hc]])
                    _tt(S, "dve", bt[hc], A_, T1, ALU.mult, [tA, tT1], [t_fm[hc]])
                    _tt(S, "dve", T2.rearrange("p (c t) -> p c t", t=64), cum3[:, :, 63:64].broadcast_to([128, 8, 64]), cum3,
                        ALU.subtract, [tCUM], [tT2])
                    _act(S, T2, T2, AF.Exp, [tT2], [tT2])
                    _tt(S, "dve", K_, K_, T2, ALU.mult, [tK, tT2], [tK])
                    _tt(S, "dve", A_, A_, T2, ALU.mult, [tA, tT2], [tA])
                    for (src, tsrc, dstm, b) in ((V_, tV, Vtm, 4), (K_, tK, Khtm, 5), (A_, tA, Bhtm, 6)):
                        for tp in range(4):
                            _tr(S, ps[b][:, tp * 128:(tp + 1) * 128], src[:, tp * 128:(tp + 1) * 128], ident,
                                [tsrc, tconst], [tps[b]])
                        _cp(S, "act" if b != 5 else "dve", dstm[:, :, hc * 128:(hc + 1) * 128],
                            ps[b].rearrange("p (t n) -> p t n", t=4), [tps[b]], [t_tm])

                _ckpt(4)
                def ut_init(tp, k4):
                    hc, hh = k4 // 2, k4 % 2
                    P = slice(64 * hh, 64 * hh + 64)
                    cs_ = slice(tp * 128, (tp + 1) * 128)
                    rb = tp % 2
                    pa = 4 + k4
                    fmr = [t_fm[hc]]
                    ex = ps[PEX][:, k4 * 128:(k4 + 1) * 128]
                    _mm(S, ps[pa][:, 0:128], bt[hc][P, cs_], at[hc][P, cs_], True, True, fmr, [tps[pa]])
                    _mm(S, ps[pa][:, 128:256], bt[hc][P, cs_], rt[hc][P, cs_], True, True, fmr, [tps[pa]])
                    _mm(S, ps[pa][:, 256:384], kt[hc][P, cs_], at[hc][P, cs_], True, True, fmr, [tps[pa]])
                    _mm(S, ps[pa][:, 384:512], kt[hc][P, cs_], rt[hc][P, cs_], True, True, fmr, [tps[pa]])
                    _mm(S, ex, at[hc][P, cs_], bt[hc][P, cs_], True, True, fmr, [tps[PEX]])
                    _tt(S, "dve", Sf[k4], ps[pa][:, 0:128], m4[:, 0:128], ALU.mult, [tps[pa], t_cm], [t_ut[k4]])
                    _tt(S, "dve", A3[rb][k4], ps[pa][:, 128:512], m4[:, 128:512], ALU.mult, [tps[pa], t_cm], [t_res[rb][k4]])
                    _cp(S, "act", BS[k4][0][:, 0:128], Sf[k4], [t_ut[k4]], [t_ut[k4]])
                    _tt(S, "dve", Sf[k4], Sf[k4], ident, ALU.add, [t_ut[k4], tconst], [t_ut[k4]])
                    _cp(S, "act", BS[k4][0][:, 128:256], Sf[k4], [t_ut[k4]], [t_ut[k4]])
                    _tt(S, "dve", A_k[k4][0], ex, m_ij, ALU.mult, [tps[PEX], t_cm], [t_ut[k4]])

                def ut_level(tp, k4, lvl):
                    rb = tp % 2
                    pa = 4 + k4
                    tu = t_ut[k4]
                    if lvl == 0:
                        _mm(S, ps[pa][:, 0:128], A_k[k4][0], BS[k4][0][:, 0:128], True, True, [tu], [tps[pa]])
                        _mm(S, ps[pa][:, 256:384], BS[k4][0][:, 0:128], A_k[k4][0], True, True, [tu], [tps[pa]])
                        _cp(S, "act", BS[k4][1][:, 0:128], ps[pa][:, 0:128], [tps[pa]], [tu])
                        _cp(S, "act", BS[k4][1][:, 128:256], BS[k4][0][:, 128:256], [tu], [tu])
                        _cp(S, "dve", A_k[k4][1], ps[pa][:, 256:384], [tps[pa]], [tu])
                    elif lvl < 5:
                        c_, n_ = (lvl % 2), 1 - (lvl % 2)
                        _mm(S, ps[pa][:, 0:256], A_k[k4][c_], BS[k4][c_], True, True, [tu], [tps[pa]])
                        _mm(S, ps[pa][:, 256:384], BS[k4][c_][:, 0:128], A_k[k4][c_], True, True, [tu], [tps[pa]])
                        _cp(S, "act", BS[k4][n_][:, 0:128], ps[pa][:, 0:128], [tps[pa]], [tu])
                        _tt(S, "dve", Sf[k4], Sf[k4], ps[pa][:, 128:256], ALU.add, [tps[pa], tu], [tu])
                        _cp(S, "act", BS[k4][n_][:, 128:256], Sf[k4], [tu], [tu])
                        _cp(S, "dve", A_k[k4][n_], ps[pa][:, 256:384], [tps[pa]], [tu])
                    else:
                        c_ = lvl % 2
                        _mm(S, ps[pa][:, 0:128], A_k[k4][c_], BS[k4][c_][:, 128:256], True, True, [tu], [tps[pa]])
                        _tt(S, "dve", Tt[rb][k4], Sf[k4], ps[pa][:, 0:128], ALU.add, [tps[pa], tu], [t_res[rb][k4]])

                def chain_stage(tp, cc, st):
                    rb = tp % 2
                    c = 2 * tp + cc
                    Q = slice(64 * cc, 64 * cc + 64)
                    ccol = slice(c * 64, c * 64 + 64)
                    if st == 0:
                        if gi == 16:
                            S.dma("sp", Mst, wkv0[:, :, c, :], writes=[t_M])
                            upd_mbf("dve")
                        for k4 in range(4):
                            hc, hh = k4 // 2, k4 % 2
                            vcol = slice(hc * 128 + hh * 64, hc * 128 + hh * 64 + 64)
                            _mm(S, ps[PXU][Q, k4 * 64:(k4 + 1) * 64], at[hc][:, ccol], Mbf[:, k4, :], True, False,
                                [t_fm[hc], t_Mbf], [tps[PXU]])
                            _mm(S, ps[PXU][Q, k4 * 64:(k4 + 1) * 64], A3[rb][k4][:, 128 + 64 * cc:128 + 64 * cc + 64],
                                Vtm[:, tp, vcol], False, True, [t_res[rb][k4], t_tm], [tps[PXU]])
                        _cp(S, "act", Xs[Q, :], ps[PXU][Q, 0:256], [tps[PXU]], [t_Xs])
                    elif st == 1:
                        for k4 in range(4):
                            _mm(S, ps[PXU][Q, 256 + k4 * 64:256 + (k4 + 1) * 64], Tt[rb][k4][:, 64 * cc:64 * cc + 64],
                                Xs[:, k4 * 64:(k4 + 1) * 64], True, True, [t_res[rb][k4], t_Xs], [tps[PXU]])
                        _cp(S, "dve", Us[Q, :], ps[PXU][Q, 256:512], [tps[PXU]], [t_Us])
                    else:
                        for k4 in range(4):
                            hc, hh = k4 // 2, k4 % 2
                            P = slice(64 * hh, 64 * hh + 64)
                            vcol = slice(hc * 128 + hh * 64, hc * 128 + hh * 64 + 64)
                            yo = ps[PYM][P, hc * 128 + cc * 64:hc * 128 + cc * 64 + 64]
                            _mm(S, yo, Mbf[:, k4, :], rt[hc][:, ccol], True, False, [t_Mbf, t_fm[hc]], [tps[PYM]])
                            _mm(S, yo, Us[:, k4 * 64:(k4 + 1) * 64], A3[rb][k4][:, 64 * cc:64 * cc + 64], False, False,
                                [t_Us, t_res[rb][k4]], [tps[PYM]])
                            _mm(S, yo, Vtm[:, tp, vcol], A3[rb][k4][:, 256 + 64 * cc:256 + 64 * cc + 64], False, True,
                                [t_tm, t_res[rb][k4]], [tps[PYM]])
                            mo = ps[PYM][P, 256 + hc * 64:256 + (hc + 1) * 64]
                            _mm(S, mo, Bhtm[Q, tp, vcol], Us[Q, k4 * 64:(k4 + 1) * 64], True, False, [t_tm, t_Us], [tps[PYM]])
                            _mm(S, mo, Khtm[Q, tp, vcol], Vtm[Q, tp, vcol], False, True, [t_tm], [tps[PYM]])
                        for hc in range(2):
                            _stt(S, Mst[:, hc, :], Mst[:, hc, :], gCs[:, hc, c:c + 1], ps[PYM][:, 256 + hc * 64:256 + (hc + 1) * 64],
                                 ALU.mult, ALU.add, [t_M, t_gC, tps[PYM]], [t_M])
                        upd_mbf("act")
                        if gi == 16:
                            S.dma("pool", o_wkv[:, 1 + c, :, :], Mst, reads=[t_M], writes=[t_out])
                        elif gi == 15 and c == 7:
                            S.dma("pool", o_wkv[:, 0, :, :], Mst, reads=[t_M], writes=[t_out])
                        if cc == 1:
                            for hc in range(2):
                                _cp(S, "dve", Yg[hc][:, tp * 128:(tp + 1) * 128], ps[PYM][:, hc * 128:(hc + 1) * 128],
                                    [tps[PYM]], [t_Yg[hc]])

                for tp in range(5):
                    for step in range(7):
                        if tp < 4:
                            for k4 in range(4):
                                if step == 0:
                                    ut_init(tp, k4)
                                else:
                                    ut_level(tp, k4, step - 1)
                        if tp >= 1 and step >= 1:
                            cc_, st_ = divmod(step - 1, 3)
                            chain_stage(tp - 1, cc_, st_)
                _ckpt(6)
                for hc in range(2):
                    T1, T2, T3 = tf[0], tf[1], tf[2]
                    tT1, tT2, tT3 = t_tf[0], t_tf[1], t_tf[2]
                    Yf = Yg[hc]
                    t_Yf = t_Yg[hc]
                    _mm(S, ps[7], blk_f, Yf, True, True, [tconst, t_Yf], [tps[7]])
                    _ts(S, "dve", T1, ps[7], 1.0 / 64, None, ALU.mult, None, [tps[7]], [tT1])
                    _tt(S, "dve", T2, Yf, Yf, ALU.mult, [t_Yf], [tT2])
                    _mm(S, ps[7], blk_f, T2, True, True, [tconst, tT2], [tps[7]])
                    _tt(S, "dve", T3, T1, T1, ALU.mult, [tT1], [tT3])
                    _stt(S, T2, ps[7], 1.0 / 64, T3, ALU.mult, ALU.subtract, [tps[7], tT3], [tT2])
                    _rsqrt(S, T2, T2, GN_EPS, [tT2], tT2)
                    _tt(S, "dve", T1, Yf, T1, ALU.subtract, [t_Yf, tT1], [tT1])
                    _tt(S, "dve", T1, T1, T2, ALU.mult, [tT1, tT2], [tT1])
                    _ts(S, "dve", T1, T1, rv[:, 4, hc:hc + 1], rv[:, 5, hc:hc + 1], ALU.mult, ALU.add, [tT1, t_cm], [tT1])
                    _tt(S, "dve", T1, T1, bon[hc], ALU.add, [tT1, t_bg[hc]], [tT1])
                    zi = hc
                    _tt(S, "dve", Zb[zi], T1, gg[hc], ALU.mult, [tT1, t_bg[hc]], [t_Zb[zi]])
                    for (r, c0, n, dst) in grp_pieces(gi):
                        for (ch, off, m, rel) in zsplit(r * L + c0, n):
                            S.dma("pool", B_in[ch * 256 + hc * 128:ch * 256 + (hc + 1) * 128, off:off + m],
                                  Zb[zi][:, dst + rel:dst + rel + m], reads=[t_Zb[zi]], writes=[tB_in])
            ar.pop()

        S.barrier()

        def PH(name):
            return PHASES is None or name in PHASES

        def NOCC(name):
            return SKIP_CC is not None and name in SKIP_CC
        if PH("ffn0a"):
            ffn(0, 0, 0)
            S.barrier()
        if PH("hm1"):
            emit_hm(lambda s: gsv[:, 0, 1, s, :], lambda s: modv[:, 0, 24:32, s], A_in, tA_in, True)
            S.dma("pool", o_shift, shcap, reads=[t_shcap], writes=[t_out])
            if not NOCC("A"):
                gath_tok(A_in, tA_in, A_out, tA_out)
            S.barrier()
        if PH("rwkv"):
            try:
                rwkv()
            except _Stop:
                ar.pop()
            S.barrier()
            if not NOCC("B"):
                gath_z(B_in, tB_in, B_out, tB_out)
            S.barrier()
        if PH("oproj1"):
            out_proj(B_out, tB_out, w_o1, lambda s: cov[:, 0, 1, s, :])
            S.barrier()
        if PH("ffn0b"):
            ffn(0, 1, 2)
            S.barrier()
        if PH("hk"):
            emit_hm(lambda s: kgs[:, s, :], lambda s: kvmod[:, 0:8, s], C_in, tC_in, False)
            if not NOCC("C"):
                gath_tok(C_in, tC_in, C_out, tC_out)
            S.barrier()
        if PH("ffn1a"):
            ffn(1, 0, 0)
            S.barrier()
        if PH("hm2"):
            emit_hm(lambda s: gsv[:, 1, 1, s, :], lambda s: modv[:, 1, 24:32, s], D_in, tD_in, False)
            if not NOCC("D"):
                gath_tok(D_in, tD_in, D_out, tD_out)
            S.barrier()
        if PH("attn"):
            try:
                attention()
            except _Stop:
                ar.pop()
            S.barrier()
            if not NOCC("E"):
                gath_z(E_in, tE_in, E_out, tE_out)
            S.barrier()
        if PH("oproj2"):
            out_proj(E_out, tE_out, w_o2, lambda s: cov[:, 1, 1, s, :])
            S.barrier()
        if PH("ffn1b"):
            ffn(1, 1, 2)
            S.barrier()
        for gi, (c0, n) in enumerate([(0, 512), (512, 512), (1024, 512), (1536, 512), (2048, 128)]):
            S.dma("pool", yT[:, c0:c0 + n].rearrange("(k p) n -> p k n", p=128), X[:, :, c0:c0 + n],
                  reads=[tX[gi]], writes=[t_out])
        S.emit(st)
        print("op stats", S.stats, "arena peak", ar.peak, "of", ARENA_WORDS, flush=True)
    return nc


_NC_CACHE = {}


def _consts():
    half = 8
    inv = (500000.0 ** (-np.arange(half, dtype=np.float32) * 2.0 / 16)).astype(np.float32)
    ropeC = np.ones((NGRP, 128, 512), np.float32)
    ropeS = np.zeros((NGRP, 128, 512), np.float32)
    for gi in range(NGRP):
        pos = np.zeros(512, np.float32)
        for (r, c0, n, dst) in grp_pieces(gi):
            if gi < 16:
                pos[dst:dst + n] = 2048 * r + c0 + np.arange(n)
            else:
                pos[dst:dst + n] = 2048 + np.arange(n)
        ang = pos[None, :].astype(np.float32) * inv[:, None]
        cs, sn = np.cos(ang).astype(np.float32), np.sin(ang).astype(np.float32)
        for blk in range(2):
            b0 = blk * 64
            ropeC[gi, b0:b0 + 8] = cs
            ropeC[gi, b0 + 8:b0 + 16] = cs
            ropeS[gi, b0:b0 + 8] = -sn
            ropeS[gi, b0 + 8:b0 + 16] = sn
    permT = np.zeros((128, 128), np.float32)
    for p in range(128):
        d = p % 64
        if d < 8:
            permT[p + 8, p] = 1.0
        elif d < 16:
            permT[p - 8, p] = 1.0
    k = np.arange(128)[:, None]
    q = np.arange(512)[None, :]
    amask = np.stack([((128 * j + k) // 64 <= q // 64) for j in range(4)]).astype(np.float32).astype(ml_dtypes.bfloat16)
    jj = np.arange(128)[:, None]
    ii = np.arange(128)[None, :]
    same = (jj // 64) == (ii // 64)
    strict = ((ii > jj) & same).astype(np.float32)
    incl = ((ii >= jj) & same).astype(np.float32)
    strict_ij = ((jj > ii) & same).astype(np.float32)
    cmask = np.concatenate([strict, incl, strict, incl, strict_ij], axis=1).astype(np.float32)
    return ropeC, ropeS, permT, amask, cmask


def _fm(v):
    v = np.asarray(v, np.float32)
    lead = v.shape[:-1]
    x = v.reshape(lead + (8, 128))
    x = np.moveaxis(x, -1, 0)
    return np.ascontiguousarray(x)


def kernel(**inp):
    f32 = np.float32
    I = {k: np.asarray(v) for k, v in inp.items()}
    if "nc" not in _NC_CACHE:
        _NC_CACHE["nc"] = build_program()
    nc = _NC_CACHE["nc"]
    ropeC, ropeS, permT, amask, cmask = _consts()
    in_maps = []
    for c in range(8):
        g, j = c // 4, c % 4
        s0 = 8 * g + 2 * j
        m = {}
        m["xT"] = np.ascontiguousarray(np.concatenate(
            [I["x_prompt"][g, 2048 * j:2048 * (j + 1)].T, I["x_sample"][s0].T, I["x_sample"][s0 + 1].T], axis=1))
        cv = np.stack([I["c_prompt"][g], I["c_sample"][s0], I["c_sample"][s0 + 1]], 0)
        m["cT"] = np.ascontiguousarray(cv.reshape(3, 8, 128).transpose(2, 1, 0))
        m["ada_w"] = I["ada_w"]
        m["ada_bT"] = np.ascontiguousarray(I["ada_b"].reshape(2, 72, 128).transpose(2, 0, 1))
        m["normgT"] = np.ascontiguousarray(I["norm_g"].reshape(2, 6, 8, 128).transpose(3, 0, 1, 2))
        m["ffn_w_in"] = I["ffn_w_in"]
        m["ffn_w_out"] = I["ffn_w_out"]
        m["kv_ada_w"] = I["kv_ada_w"]
        m["kv_ada_bT"] = np.ascontiguousarray(I["kv_ada_b"].reshape(16, 128).T)
        m["kv_normgT"] = np.ascontiguousarray(I["kv_norm_g"].reshape(8, 128).T)
        m["muT"] = np.ascontiguousarray(I["rwkv_mu"][0].reshape(6, 8, 128).transpose(2, 0, 1))
        cols = slice(256 * j, 256 * j + 256)
        m["w_rkv"] = np.ascontiguousarray(I["rwkv_w_rkv"][0][:, :, cols])
        m["w_l1"] = np.ascontiguousarray(np.concatenate([I["rwkv_w1"][0], I["rwkv_a1"][0], I["rwkv_g1"][0]], axis=1))
        m["w_w2"] = np.ascontiguousarray(I["rwkv_w2"][0][:, cols])
        m["w_a2"] = np.ascontiguousarray(I["rwkv_a2"][0][:, cols])
        m["w_g2"] = np.ascontiguousarray(I["rwkv_g2"][0][:, cols])
        vecs = [I["rwkv_w0"][0][cols], I["rwkv_a0"][0][cols], I["rwkv_k_k"][0][cols], I["rwkv_k_a"][0][cols],
                I["rwkv_ln_w"][0][cols], I["rwkv_ln_b"][0][cols], I["rwkv_r_k"][0].reshape(-1)[cols]]
        m["rvecT"] = np.ascontiguousarray(np.stack(vecs, 0).reshape(7, 2, 128).transpose(2, 0, 1))
        m["shiftT"] = np.ascontiguousarray(I["state_shift"][0, 8 * g:8 * g + 8, 0, :].reshape(8, 8, 128).transpose(2, 1, 0))
        sw = I["state_wkv"][0, 8 * g:8 * g + 8, 4 * j:4 * j + 4]
        sw = sw.reshape(8, 2, 2, 64, 64)
        m["wkv0"] = np.ascontiguousarray(sw.transpose(2, 4, 1, 0, 3).reshape(128, 2, 8, 64))
        m["w_o1"] = I["rwkv_w_o"][0]
        m["kvwk"] = np.ascontiguousarray(I["kv_w"][:, cols])
        m["kvwv"] = np.ascontiguousarray(I["kv_w"][:, 1024 + 256 * j:1024 + 256 * j + 256])
        m["wq"] = np.ascontiguousarray(I["diff_w_q"][0][:, cols])
        m["w_o2"] = I["diff_w_o"][0]
        m["cache_k"] = np.ascontiguousarray(I["cache_k"][8 * g:8 * g + 8, :, 2 * j:2 * j + 2].reshape(8, 2048, 256))
        m["cache_v"] = np.ascontiguousarray(I["cache_v"][8 * g:8 * g + 8, :, 2 * j:2 * j + 2].reshape(8, 2048, 256))
        m["lamb"] = np.ascontiguousarray(I["diff_lambda"][0].reshape(1, 256))
        m["sublnT"] = np.ascontiguousarray(I["diff_subln_g"][0].reshape(128, 1))
        m["ropeC"] = ropeC; m["ropeS"] = ropeS; m["permT"] = permT; m["amask"] = amask; m["cmask"] = cmask
        sel = np.zeros((128, 4), f32); sel[:, j] = 1.0
        m["selT"] = sel
        in_maps.append({k: (v if v.dtype != np.float64 else v.astype(f32)) for k, v in m.items()})
    res = run_bass_kernel_spmd(nc, in_maps, core_ids=list(range(8)))
    R = res.results
    y_prompt = np.zeros((2, 8192, D), f32); y_sample = np.zeros((16, 64, D), f32)
    wkv_prompt = np.zeros((1, 2, 16, 64, 64), f32); wkv_sample = np.zeros((1, 16, 16, 64, 64), f32)
    shift_prompt = np.zeros((1, 2, 1, D), f32); shift_sample = np.zeros((1, 16, 1, D), f32)
    k_prompt = np.zeros((2, 8192, 8, 2, 64), f32); v_prompt = np.zeros((2, 8192, 8, 128), f32)
    k_sample = np.zeros((16, 64, 8, 2, 64), f32); v_sample = np.zeros((16, 64, 8, 128), f32)
    for c in range(8):
        g, j = c // 4, c % 4
        s0 = 8 * g + 2 * j
        r = R[c]
        yT = np.asarray(r["yT"])
        y_prompt[g, 2048 * j:2048 * (j + 1)] = yT[:, :2048].T
        y_sample[s0] = yT[:, 2048:2112].T
        y_sample[s0 + 1] = yT[:, 2112:2176].T
        ow = np.asarray(r["o_wkv"]).reshape(2, 64, 9, 2, 64)
        st = ow.transpose(2, 3, 0, 4, 1)
        wkv_prompt[0, g, 4 * j:4 * j + 4] = st[0].reshape(4, 64, 64)
        for s in range(8):
            wkv_sample[0, 8 * g + s, 4 * j:4 * j + 4] = st[1 + s].reshape(4, 64, 64)
        osf = np.asarray(r["o_shift"])
        sh = osf.transpose(2, 1, 0).reshape(3, D)
        if j == 3:
            shift_prompt[0, g, 0] = sh[0]
        shift_sample[0, s0, 0] = sh[1]
        shift_sample[0, s0 + 1, 0] = sh[2]
        ok = np.asarray(r["o_k"]).reshape(2, 2, 64, 4, L)
        ov = np.asarray(r["o_v"]).reshape(4, L, 2, 128)
        for rk in range(4):
            k_prompt[g, 2048 * rk:2048 * (rk + 1), 2 * j:2 * j + 2] = ok[:, :, :, rk, :2048].transpose(3, 0, 1, 2)
            v_prompt[g, 2048 * rk:2048 * (rk + 1), 2 * j:2 * j + 2] = ov[rk, :2048]
            for p in range(2):
                sidx = 8 * g + 2 * rk + p
                k_sample[sidx, :, 2 * j:2 * j + 2] = ok[:, :, :, rk, 2048 + 64 * p:2048 + 64 * p + 64].transpose(3, 0, 1, 2)
                v_sample[sidx, :, 2 * j:2 * j + 2] = ov[rk, 2048 + 64 * p:2048 + 64 * p + 64]
    return (y_prompt, y_sample, wkv_prompt, shift_prompt, k_prompt, v_prompt,
            wkv_sample, shift_sample, k_sample, v_sample)
```

```python
import numpy as np
import concourse.bass as bass
import concourse.mybir as mybir

F32 = mybir.dt.float32
BF16 = mybir.dt.bfloat16
ALU = mybir.AluOpType
AF = mybir.ActivationFunctionType
AX = mybir.AxisListType

SEM_EPOCH = 30000


class T:
    __slots__ = ("name", "lw", "rd", "excl")

    def __init__(self, name="", excl=False):
        self.name = name
        self.excl = excl
        self.lw = None
        self.rd = []


class Sched:
    CE = ("pe", "act", "dve", "pool", "sp")

    def __init__(self, nc, nsp=8, npool=4):
        self.nc = nc
        self.ops = {e: [] for e in self.CE}
        self.known = {e: {f: -1 for f in self.CE} for e in self.CE}
        self.clock = {e: [] for e in self.CE}
        self.dwaited = {e: set() for e in self.CE}
        self.nq = {"sp": nsp, "pool": npool}
        self.dq = {"sp": [], "pool": []}
        self.ndma = 0
        self.dma_info = {}
        self.ncc = 0
        self.pending_barrier = {e: None for e in self.CE}

    def _deps(self, reads, writes, eng=None):
        deps = []
        for t in reads:
            if t.lw is not None:
                deps.append((t.lw, "raw"))
            if t.excl:
                for r in t.rd:
                    if r[0] == "c" and r[1] != eng:
                        deps.append((r, "raw"))
        for t in writes:
            if t.lw is not None:
                deps.append((t.lw, "waw"))
            for r in t.rd:
                deps.append((r, "war"))
        return deps

    def _add_wait(self, eng, waits, ev, kind):
        if ev[0] == "d":
            if ev[1] in self.dwaited[eng]:
                return
            self.dwaited[eng].add(ev[1])
            waits.append(ev)
            return
        if ev[0] == "cc":
            if ev in self.dwaited[eng]:
                return
            self.dwaited[eng].add(ev)
            waits.append(ev)
            return
        _, f, j = ev
        if f == eng:
            if eng in ("pe", "sp"):
                return
            if self.known[eng][f] >= j:
                return
        elif self.known[eng][f] >= j:
            return
        waits.append(ev)
        snap = self.clock[f][j]
        k = self.known[eng]
        for g, v in snap.items():
            if v > k[g]:
                k[g] = v
        if j > k[f]:
            k[f] = j

    def _commit(self, eng, ev, reads, writes):
        for t in reads:
            t.rd.append(ev)
        for t in writes:
            t.lw = ev
            t.rd = []

    def _barrier_waits(self, eng, waits):
        b = self.pending_barrier[eng]
        if b is None:
            return
        self.pending_barrier[eng] = None
        for ev in b:
            self._add_wait(eng, waits, ev, "raw")

    def barrier(self):
        evs = []
        for e in self.CE:
            for j in range(len(self.ops[e]) - 1, -1, -1):
                if self.ops[e][j]["kind"] == "c":
                    evs.append(("c", e, j))
                    break
        for c in range(self.ncc):
            evs.append(("cc", c))
        for q in self.dq:
            for d in self.dq[q][-self.nq[q]:]:
                evs.append(("d", d))
        for e in self.CE:
            self.pending_barrier[e] = list(evs)

    def op(self, eng, fn, reads=(), writes=()):
        waits = []
        self._barrier_waits(eng, waits)
        for ev, kind in self._deps(reads, writes, eng):
            self._add_wait(eng, waits, ev, kind)
        j = len(self.ops[eng])
        self.ops[eng].append(dict(fn=fn, waits=waits, kind="c"))
        self.clock[eng].append(dict(self.known[eng]))
        ev = ("c", eng, j)
        self._commit(eng, ev, reads, writes)
        return ev

    def dma(self, q, out_ap, in_ap, reads=(), writes=(), **kw):
        waits = []
        self._barrier_waits(q, waits)
        for ev, kind in self._deps(reads, writes):
            self._add_wait(q, waits, ev, "raw")
        k = len(self.dq[q])
        n = self.nq[q]
        if k >= n:
            self._add_wait(q, waits, ("d", self.dq[q][k - n]), "raw")
        did = self.ndma
        self.ndma += 1
        self.dq[q].append(did)
        self.dma_info[did] = (q, k)
        j = len(self.ops[q])
        self.ops[q].append(dict(fn=None, waits=waits, kind="d", out=out_ap, in_=in_ap, did=did, kw=kw))
        self.clock[q].append(dict(self.known[q]))
        ev = ("d", did)
        self._commit(q, ev, reads, writes)
        return ev

    def collective(self, kind, groups, in_ap, out_ap, reads=(), writes=()):
        q = "pool"
        waits = []
        self._barrier_waits(q, waits)
        for ev, kd in self._deps(reads, writes):
            self._add_wait(q, waits, ev, "raw")
        cid = self.ncc
        self.ncc += 1
        self.ops[q].append(dict(fn=None, waits=waits, kind="cc", cckind=kind, groups=groups,
                                in_=in_ap, out=out_ap, cid=cid))
        self.clock[q].append(dict(self.known[q]))
        ev = ("cc", cid)
        self._commit(q, ev, reads, writes)
        return ev

    def emit(self, stack):
        nc = self.nc
        marked = {e: set() for e in self.CE}
        for e in self.CE:
            for o in self.ops[e]:
                for w in o["waits"]:
                    if w[0] == "c":
                        marked[w[1]].add(w[2])
        rank = {}
        csem = {}
        for e in self.CE:
            ms = sorted(marked[e])
            rank[e] = {j: i for i, j in enumerate(ms)}
            nep = (len(ms) + SEM_EPOCH - 1) // SEM_EPOCH
            csem[e] = [stack.enter_context(nc.semaphore(f"c_{e}_{i}")) for i in range(max(nep, 1))]
        dsem = {q: [stack.enter_context(nc.semaphore(f"d_{q}_{i}")) for i in range(self.nq[q])]
                for q in self.dq}
        ccsem = [stack.enter_context(nc.semaphore(f"cc_{i}")) for i in range(self.ncc)]
        self.stats = {e: (len(self.ops[e]), len(marked[e])) for e in self.CE}

        def waitspec(w):
            if w[0] == "c":
                r = rank[w[1]][w[2]]
                return csem[w[1]][r // SEM_EPOCH], (r % SEM_EPOCH) + 1
            if w[0] == "d":
                q, k = self.dma_info[w[1]]
                n = self.nq[q]
                return dsem[q][k % n], 16 * (k // n + 1)
            if w[0] == "cc":
                return ccsem[w[1]], 1
            raise ValueError(w)

        def run(engname, e):
            for j, o in enumerate(self.ops[engname]):
                for w in o["waits"]:
                    s, v = waitspec(w)
                    e.wait_ge(s, v)
                if o["kind"] == "c":
                    ins = o["fn"](e)
                    if j in rank[engname]:
                        r = rank[engname][j]
                        ins.then_inc(csem[engname][r // SEM_EPOCH], 1)
                elif o["kind"] == "d":
                    q, k = self.dma_info[o["did"]]
                    n = self.nq[q]
                    e.dma_start(out=o["out"], in_=o["in_"], **o["kw"]).then_inc(dsem[q][k % n], 16)
                elif o["kind"] == "cc":
                    e.collective_compute(o["cckind"], ALU.bypass, replica_groups=o["groups"],
                                         ins=[o["in_"]], outs=[o["out"]]).then_inc(ccsem[o["cid"]], 1)
            if engname in self.dq:
                q = engname
                for d in self.dq[q][-self.nq[q]:]:
                    s, v = waitspec(("d", d))
                    e.wait_ge(s, v)
            if engname == "pool":
                for c in range(self.ncc):
                    e.wait_ge(ccsem[c], 1)

        with nc.Block() as block:
            @block.tensor
            def _(e):
                run("pe", e)

            @block.scalar
            def _(e):
                run("act", e)

            @block.vector
            def _(e):
                run("dve", e)

            @block.gpsimd
            def _(e):
                run("pool", e)

            @block.sync
            def _(e):
                run("sp", e)


class Arena:
    def __init__(self, ap_f32, n):
        self.ap = ap_f32
        self.n = n
        self.off = 0
        self.marks = []

    def push(self):
        self.marks.append(self.off)

    def pop(self):
        self.off = self.marks.pop()

    def f32(self, n):
        a = self.ap[:, self.off:self.off + n]
        self.off += n
        self.peak = max(getattr(self, "peak", 0), self.off)
        assert self.off <= self.n, f"arena overflow {self.off} > {self.n}"
        return a

    def bf16(self, n):
        m = (n + 1) // 2
        a = self.ap[:, self.off:self.off + m].bitcast(BF16)
        self.off += m
        self.peak = max(getattr(self, "peak", 0), self.off)
        assert self.off <= self.n, f"arena overflow {self.off} > {self.n}"
        return a[:, 0:n]

import math
from contextlib import ExitStack
import ml_dtypes
from concourse.bass_utils import run_bass_kernel_spmd

D = 1024
KC = 8
L = 2176
G4 = 8704
DFF = 2816
NHC = 22
NGRP = 17
EPS = 1e-6
GN_EPS = 64e-5
GROUPS = [[0, 1, 2, 3], [4, 5, 6, 7]]
LAM_INIT = 0.8 - 0.6 * math.exp(-0.3 * 1)
ATT_SCALE = 0.125
ARENA_WORDS = 53184
STOP_AFTER = 99
PHASES = None
RWKV_STOP = 0
ATT_G = 0


class _Stop(Exception):
    pass


def _ckpt(level):
    if RWKV_STOP == level:
        raise _Stop()
SKIP_CC = None
DEBUG = False


def _mm(S, out, lhsT, rhs, start, stop, reads, writes):
    S.op("pe", lambda e: e.matmul(out, lhsT=lhsT, rhs=rhs, start=start, stop=stop), reads, writes)


def _tr(S, out, in_, ident, reads, writes):
    S.op("pe", lambda e: e.transpose(out, in_, ident), reads, writes)


def _act(S, out, in_, func, reads, writes, bias=None, scale=None):
    kw = {}
    if bias is not None:
        kw["bias"] = bias
    if scale is not None:
        kw["scale"] = scale
    S.op("act", lambda e: e.activation(out=out, in_=in_, func=func, **kw), reads, writes)


def _tt(S, eng, out, in0, in1, op, reads, writes):
    S.op(eng, lambda e: e.tensor_tensor(out=out, in0=in0, in1=in1, op=op), reads, writes)


def _ts(S, eng, out, in0, s1, s2, op0, op1, reads, writes):
    if s2 is None:
        S.op(eng, lambda e: e.tensor_scalar(out=out, in0=in0, scalar1=s1, scalar2=None, op0=op0), reads, writes)
    else:
        S.op(eng, lambda e: e.tensor_scalar(out=out, in0=in0, scalar1=s1, scalar2=s2, op0=op0, op1=op1), reads, writes)


def _stt(S, out, in0, scalar, in1, op0, op1, reads, writes):
    S.op("dve", lambda e: e.scalar_tensor_tensor(out=out, in0=in0, scalar=scalar, in1=in1, op0=op0, op1=op1),
         reads, writes)


def _cp(S, eng, out, in_, reads, writes):
    if eng == "act":
        S.op("act", lambda e: e.activation(out=out, in_=in_, func=AF.Copy), reads, writes)
    else:
        S.op(eng, lambda e: e.tensor_copy(out=out, in_=in_), reads, writes)


def _rsqrt(S, out, in_, eps, reads, t_out, scale=1.0):
    S.op("act", lambda e: e.activation(out=out, in_=in_, func=AF.Sqrt, bias=eps, scale=scale), reads, [t_out])
    S.op("dve", lambda e: e.reciprocal(out=out, in_=out), [t_out], [t_out])


def _memset(S, eng, ap, val, writes):
    S.op(eng, lambda e: e.memset(ap, val), (), writes)


def grp_pieces(gi):
    if gi < 16:
        return [(gi // 4, (gi % 4) * 512, 512, 0)]
    return [(s // 2, 2048 + 64 * (s % 2), 64, 64 * s) for s in range(8)]


def build_program():
    nc = bass.Bass("TRN2", target_bir_lowering=False)

    def din(name, shape, dt=F32):
        return nc.dram_tensor(name, list(shape), dt, kind="ExternalInput").ap()

    def dout(name, shape, dt=F32):
        return nc.dram_tensor(name, list(shape), dt, kind="ExternalOutput").ap()

    def dscr(name, shape, dt=BF16):
        return nc.dram_tensor(name, list(shape), dt).ap()

    xT = din("xT", [D, L])
    cT = din("cT", [128, KC, 3])
    ada_w = din("ada_w", [2, D, 9 * D])
    ada_bT = din("ada_bT", [128, 2, 72])
    normgT = din("normgT", [128, 2, 6, KC])
    ffn_w_in = din("ffn_w_in", [2, 2, D, 2 * DFF])
    ffn_w_out = din("ffn_w_out", [2, 2, DFF, D])
    kv_ada_w = din("kv_ada_w", [D, 2 * D])
    kv_ada_bT = din("kv_ada_bT", [128, 16])
    kv_normgT = din("kv_normgT", [128, KC])
    muT = din("muT", [128, 6, KC])
    w_rkv = din("w_rkv", [3, D, 256])
    w_l1 = din("w_l1", [D, 256])
    w_w2 = din("w_w2", [64, 256])
    w_a2 = din("w_a2", [64, 256])
    w_g2 = din("w_g2", [128, 256])
    rvecT = din("rvecT", [128, 7, 2])
    shiftT = din("shiftT", [128, KC, 8])
    wkv0 = din("wkv0", [128, 2, 8, 64])
    w_o1 = din("w_o1", [D, D])
    kvwk = din("kvwk", [D, 256])
    kvwv = din("kvwv", [D, 256])
    wq = din("wq", [D, 256])
    w_o2 = din("w_o2", [D, D])
    cache_k = din("cache_k", [8, 2048, 256])
    cache_v = din("cache_v", [8, 2048, 256])
    lamb = din("lamb", [1, 256])
    sublnT = din("sublnT", [128, 1])
    ropeC = din("ropeC", [NGRP, 128, 512])
    ropeS = din("ropeS", [NGRP, 128, 512])
    permT = din("permT", [128, 128])
    amask = din("amask", [4, 128, 512], BF16)
    selT = din("selT", [128, 4])
    cmask = din("cmask", [128, 640])
    yT = dout("yT", [D, L])
    o_wkv = dout("o_wkv", [128, 9, 2, 64])
    o_shift = dout("o_shift", [128, KC, 3])
    o_k = dout("o_k", [256, G4])
    o_v = dout("o_v", [G4, 256])
    A_in = dscr("A_in", [D, L]); A_out = dscr("A_out", [4 * D, L])
    B_in = dscr("B_in", [8 * 256, 1088]); B_out = dscr("B_out", [8 * D, 1088])
    C_in = dscr("C_in", [D, L]); C_out = dscr("C_out", [4 * D, L])
    D_in = dscr("D_in", [D, L]); D_out = dscr("D_out", [4 * D, L])
    E_in = dscr("E_in", [8 * 256, 1088]); E_out = dscr("E_out", [8 * D, 1088])
    tA_in, tA_out, tB_in, tB_out, tC_in, tC_out, tD_in, tD_out, tE_in, tE_out = [T(f"scr{i}") for i in range(10)]
    t_out = T("outs")

    with ExitStack() as st:
        arena_t = st.enter_context(nc.sbuf_tensor("arena", [128, ARENA_WORDS], F32))
        ps = [st.enter_context(nc.psum_tensor(f"ps{i}", [128, 512], F32))[:] for i in range(8)]
        tps = [T(f"ps{i}", excl=True) for i in range(8)]
        ar = Arena(arena_t[:], ARENA_WORDS)
        S = Sched(nc, nsp=8, npool=6)

        def gath_tok(buf_in, t_in, buf_out, t_out_):
            for kc in range(KC):
                S.collective("AllGather", GROUPS, buf_in[kc * 128:(kc + 1) * 128, :],
                             buf_out[kc * 512:(kc + 1) * 512, :], reads=[t_in], writes=[t_out_])

        def tok_view(buf_out, r, c0, n):
            return buf_out.rearrange("(k r p) n -> r p k n", k=KC, r=4)[r][:, :, c0:c0 + n]

        def gath_z(buf_in, t_in, buf_out, t_out_):
            for ch in range(8):
                S.collective("AllGather", GROUPS, buf_in[ch * 256:(ch + 1) * 256, :],
                             buf_out[ch * D:(ch + 1) * D, :], reads=[t_in], writes=[t_out_])

        def zsplit(g0, n):
            out = []
            rel = 0
            while n > 0:
                ch, off = g0 // 1088, g0 % 1088
                m = min(n, 1088 - off)
                out.append((ch, off, m, rel))
                g0 += m; n -= m; rel += m
            return out

        X = ar.f32(KC * L).rearrange("p (k n) -> p k n", k=KC)
        tX = [T(f"X{i}") for i in range(5)]

        def xg(col0):
            return tX[min(col0 // 512, 4)]
        ones_bf = ar.bf16(128); ones_f = ar.f32(128); ident = ar.f32(128); blk_f = ar.f32(128)
        tconst = T("const")
        modv = ar.f32(2 * 72 * 3).rearrange("p (l f s) -> p l f s", l=2, f=72)
        kvmod = ar.f32(16 * 3).rearrange("p (f s) -> p f s", f=16)
        normg = ar.f32(2 * 6 * KC).rearrange("p (l i k) -> p l i k", l=2, i=6)
        kvng = ar.f32(KC)
        gsv = ar.f32(2 * 3 * 3 * KC).rearrange("p (l w s k) -> p l w s k", l=2, w=3, s=3)
        cov = ar.f32(2 * 3 * 3 * KC).rearrange("p (l w s k) -> p l w s k", l=2, w=3, s=3)
        kgs = ar.f32(3 * KC).rearrange("p (s k) -> p s k", s=3)
        tmod = T("mod")
        _memset(S, "dve", ones_bf, 1.0, [tconst])
        _memset(S, "dve", ones_f, 1.0, [tconst])
        _memset(S, "pool", ident, 0.0, [tconst])
        S.op("pool", lambda e: e.affine_select(out=ident, in_=ident, pattern=[[-1, 128]], compare_op=ALU.not_equal,
                                               fill=1.0, base=0, channel_multiplier=1), [tconst], [tconst])
        _memset(S, "dve", blk_f, 0.0, [tconst])
        _memset(S, "dve", blk_f[0:64, 0:64], 1.0, [tconst])
        _memset(S, "dve", blk_f[64:128, 64:128], 1.0, [tconst])

        shcap = ar.f32(KC * 3).rearrange("p (k s) -> p k s", k=KC)
        t_shcap = T("shcap")
        sel = ar.f32(4)
        t_sel = T("sel")
        S.dma("sp", sel, selT, writes=[t_sel])
        cs = ar.f32(KC * 3).rearrange("p (k s) -> p k s", k=KC)
        common_start = ar.off
        NSLOT = 2
        WS = 2816
        wst_f = [ar.f32(WS) for _ in range(NSLOT)]
        wst_b = [ar.bf16(WS) for _ in range(NSLOT)]
        t_wf = [T(f"wf{i}") for i in range(NSLOT)]
        t_wb = [T(f"wb{i}") for i in range(NSLOT)]
        wctr = [0]

        def load_w(dram_view, kdim, n, cast_eng=None):
            i = wctr[0] % NSLOT
            wctr[0] += 1
            f = wst_f[i][:, 0:kdim * n].rearrange("p (k n) -> p k n", k=kdim)
            b = wst_b[i][:, 0:kdim * n].rearrange("p (k n) -> p k n", k=kdim)
            S.dma("sp", f, dram_view, writes=[t_wf[i]])
            eng = cast_eng or ("pool" if (wctr[0] % 2 == 0) else "act")
            _cp(S, eng, b, f, [t_wf[i]], [t_wb[i]])
            return b, t_wb[i]

        rstd = [ar.f32(512) for _ in range(2)]
        t_rstd = [T("rstd0"), T("rstd1")]
        sqb = [ar.bf16(512) for _ in range(3)]
        t_sq = [T(f"sq{i}") for i in range(3)]
        tmpf = [ar.f32(512) for _ in range(3)]
        t_tmp = [T(f"tmp{i}") for i in range(3)]
        ctr = {"sq": 0, "tmp": 0, "rstd": 0}

        def nxt(kind, n):
            i = ctr[kind] % n
            ctr[kind] += 1
            return i

        for gi, (c0, n) in enumerate([(0, 512), (512, 512), (1024, 512), (1536, 512), (2048, 128)]):
            S.dma("sp", X[:, :, c0:c0 + n], xT[:, c0:c0 + n].rearrange("(k p) n -> p k n", p=128), writes=[tX[gi]])
        t_cs = T("cs")
        S.dma("sp", cs, cT, writes=[t_cs])
        S.dma("sp", normg, normgT, writes=[tmod])
        S.dma("sp", kvng, kv_normgT, writes=[tmod])
        _act(S, cs, cs, AF.Silu, [t_cs], [t_cs])
        ar.push()
        ast = [ar.f32(KC * 256).rearrange("p (k n) -> p k n", k=KC) for _ in range(2)]
        t_ast = [T("ast0"), T("ast1")]
        bias_tmp = ar.f32(2 * 72 + 16)
        S.dma("sp", bias_tmp[:, 0:144].rearrange("p (l f) -> p l f", l=2), ada_bT, writes=[tmod])
        S.dma("sp", bias_tmp[:, 144:160], kv_ada_bT, writes=[tmod])
        ai = 0
        jobs = [(ada_w[0], 36, lambda f: (modv[:, 0, f, :], bias_tmp[:, f:f + 1])),
                (ada_w[1], 36, lambda f: (modv[:, 1, f, :], bias_tmp[:, 72 + f:72 + f + 1])),
                (kv_ada_w, 8, lambda f: (kvmod[:, f, :], bias_tmp[:, 144 + f:144 + f + 1]))]
        for wsrc, ntile, dst in jobs:
            for ti in range(ntile):
                sl = ai % 2
                ai += 1
                S.dma("sp", ast[sl], wsrc[:, ti * 256:(ti + 1) * 256].rearrange("(k p) n -> p k n", p=128),
                      writes=[t_ast[sl]])
                pb = ai % 2
                for fi in range(2):
                    for kc in range(KC):
                        _mm(S, ps[pb][:, fi * 4:fi * 4 + 3], ast[sl][:, kc, fi * 128:(fi + 1) * 128], cs[:, kc, :],
                            kc == 0, kc == KC - 1, [t_ast[sl], t_cs], [tps[pb]])
                for fi in range(2):
                    o, b = dst(ti * 2 + fi)
                    _ts(S, "dve", o, ps[pb][:, fi * 4:fi * 4 + 3], b, None, ALU.add, None, [tps[pb], tmod], [tmod])
        ar.pop()
        SQD = 32.0
        for l in range(2):
            for w in range(3):
                for s in range(3):
                    sc_ap = modv[:, l, (3 * w + 1) * 8:(3 * w + 2) * 8, s]
                    g_ap = modv[:, l, (3 * w + 2) * 8:(3 * w + 3) * 8, s]
                    _stt(S, gsv[:, l, w, s, :], sc_ap, 1.0, normg[:, l, 2 * w, :], ALU.add, ALU.mult, [tmod], [tmod])
                    _ts(S, "dve", gsv[:, l, w, s, :], gsv[:, l, w, s, :], SQD, None, ALU.mult, None, [tmod], [tmod])
                    fct = (1.0 if w == 1 else 0.5) * SQD
                    _stt(S, cov[:, l, w, s, :], g_ap, fct, normg[:, l, 2 * w + 1, :], ALU.mult, ALU.mult, [tmod], [tmod])
        for s in range(3):
            _stt(S, kgs[:, s, :], kvmod[:, 8:16, s], 1.0, kvng, ALU.add, ALU.mult, [tmod], [tmod])
            _ts(S, "dve", kgs[:, s, :], kgs[:, s, :], SQD, None, ALU.mult, None, [tmod], [tmod])

        def seq_of(c0):
            return 0 if c0 < 2048 else 1 + (c0 - 2048) // 64

        def rms_rstd(src3, c0, n, src_reads, psb):
            ri = nxt("rstd", 2)
            for kc in range(KC):
                qi = nxt("sq", 3)
                _act(S, sqb[qi][:, 0:n], src3[:, kc, c0:c0 + n], AF.Square, src_reads, [t_sq[qi]])
                _mm(S, ps[psb][:, 0:n], ones_bf, sqb[qi][:, 0:n], kc == 0, kc == KC - 1, [t_sq[qi], tconst], [tps[psb]])
            _rsqrt(S, rstd[ri][:, 0:n], ps[psb][:, 0:n], EPS * D, [tps[psb]], t_rstd[ri])
            return ri

        def modulate(c0, n, gs_ap, sh_ap, dst3, dcol0, dst_t, psb, cap=None):
            ri = rms_rstd(X, c0, n, [xg(c0)], psb)
            for kc in range(KC):
                ti = nxt("tmp", 3)
                _stt(S, tmpf[ti][:, 0:n], X[:, kc, c0:c0 + n], gs_ap[:, kc:kc + 1], rstd[ri][:, 0:n], ALU.mult, ALU.mult,
                     [xg(c0), t_rstd[ri], tmod], [t_tmp[ti]])
                _act(S, dst3[:, kc, dcol0:dcol0 + n], tmpf[ti][:, 0:n], AF.Identity, [t_tmp[ti], tmod], [dst_t],
                     bias=sh_ap[:, kc:kc + 1])
                if cap is not None:
                    for (lc, oc) in cap:
                        _act(S, shcap[:, kc, oc:oc + 1], tmpf[ti][:, lc:lc + 1], AF.Identity, [t_tmp[ti], tmod], [t_shcap],
                             bias=sh_ap[:, kc:kc + 1])

        def postnorm_add(src3, src_t, scol0, c0, n, co_ap, psb):
            ri = rms_rstd(src3, scol0, n, [src_t], psb)
            for kc in range(KC):
                ti = nxt("tmp", 3)
                _stt(S, tmpf[ti][:, 0:n], src3[:, kc, scol0:scol0 + n], co_ap[:, kc:kc + 1], rstd[ri][:, 0:n],
                     ALU.mult, ALU.mult, [src_t, t_rstd[ri], tmod], [t_tmp[ti]])
                _tt(S, "dve", X[:, kc, c0:c0 + n], X[:, kc, c0:c0 + n], tmpf[ti][:, 0:n], ALU.add,
                    [t_tmp[ti], xg(c0)], [xg(c0)])


        def ffn(l, i, w):
            ar.push()
            NP = 1088
            hF = ar.f32(KC * NP)
            hb = hF.bitcast(BF16)[:, 0:KC * NP].rearrange("p (k n) -> p k n", k=KC)
            Fo = hF.rearrange("p (k n) -> p k n", k=KC)
            t_hF = T("hF")
            hid = ar.bf16(NHC * NP).rearrange("p (k n) -> p k n", k=NHC)
            t_hid = [T("hid0"), T("hid1"), T("hid2")]
            sg = [ar.bf16(512) for _ in range(2)]
            t_sg = [T("sg0"), T("sg1")]
            for p in range(2):
                cgs = [(p * 1024, 512, 0), (p * 1024 + 512, 512, 512), (2048 + 64 * p, 64, 1024)]
                for (c0, n, hc0) in cgs:
                    s = seq_of(c0)
                    modulate(c0, n, gsv[:, l, w, s, :], modv[:, l, 3 * w * 8:(3 * w + 1) * 8, s], hb, hc0, t_hF, 6)
                for hc in range(NHC):
                    wv = ffn_w_in[l, i].rearrange("(k p) (u n) -> p k u n", p=128, u=2)[:, :, :, hc * 128:(hc + 1) * 128]
                    si = wctr[0] % NSLOT
                    wctr[0] += 1
                    f = wst_f[si][:, 0:KC * 256].rearrange("p (k u n) -> p k u n", k=KC, u=2)
                    b = wst_b[si][:, 0:KC * 256].rearrange("p (k u n) -> p k u n", k=KC, u=2)
                    for u in range(2):
                        S.dma("sp", f[:, :, u, :], wv[:, :, u, :], writes=[t_wf[si]])
                    _cp(S, "pool" if hc % 2 == 0 else "act", b, f, [t_wf[si]], [t_wb[si]])
                    for ci, (c0, n, hc0) in enumerate(cgs):
                        pg, pu = 2 * ci, 2 * ci + 1
                        for kc in range(KC):
                            _mm(S, ps[pg][:, 0:n], b[:, kc, 0, :], hb[:, kc, hc0:hc0 + n], kc == 0, kc == KC - 1,
                                [t_wb[si], t_hF], [tps[pg]])
                        for kc in range(KC):
                            _mm(S, ps[pu][:, 0:n], b[:, kc, 1, :], hb[:, kc, hc0:hc0 + n], kc == 0, kc == KC - 1,
                                [t_wb[si], t_hF], [tps[pu]])
                        gi2 = (hc * 3 + ci) % 2
                        _act(S, sg[gi2][:, 0:n], ps[pg][:, 0:n], AF.Silu, [tps[pg]], [t_sg[gi2]])
                        _tt(S, "dve", hid[:, hc, hc0:hc0 + n], sg[gi2][:, 0:n], ps[pu][:, 0:n], ALU.mult,
                            [t_sg[gi2], tps[pu]], [t_hid[ci]])
                for dc in range(KC):
                    wv = ffn_w_out[l, i][:, dc * 128:(dc + 1) * 128].rearrange("(k p) n -> p k n", p=128)
                    b, tb = load_w(wv, NHC, 128, cast_eng="pool" if dc % 2 == 0 else "act")
                    for ci, (c0, n, hc0) in enumerate(cgs):
                        pb = ci
                        for hc in range(NHC):
                            _mm(S, ps[pb][:, 0:n], b[:, hc, :], hid[:, hc, hc0:hc0 + n], hc == 0, hc == NHC - 1,
                                [tb, t_hid[ci]], [tps[pb]])
                        _cp(S, "act" if ci % 2 == 0 else "dve", Fo[:, dc, hc0:hc0 + n], ps[pb][:, 0:n], [tps[pb]], [t_hF])
                for (c0, n, hc0) in cgs:
                    s = seq_of(c0)
                    postnorm_add(Fo, t_hF, hc0, c0, n, cov[:, l, w, s, :], 7)
            ar.pop()

        def emit_hm(gs_sel, sh_sel, dst_in, t_dst, capture):
            ar.push()
            hb = [ar.bf16(KC * 512).rearrange("p (k n) -> p k n", k=KC) for _ in range(2)]
            t_hb = [T("hmb0"), T("hmb1")]
            for gi, (c0, n) in enumerate([(0, 512), (512, 512), (1024, 512), (1536, 512), (2048, 64), (2112, 64)]):
                s = seq_of(c0)
                bi = gi % 2
                cap = None
                if capture:
                    if c0 == 1536:
                        cap = [(511, 0)]
                    elif c0 >= 2048:
                        cap = [(63, 1 + (c0 - 2048) // 64)]
                modulate(c0, n, gs_sel(s), sh_sel(s), hb[bi], 0, t_hb[bi], 6, cap=cap)
                S.dma("pool", dst_in[:, c0:c0 + n].rearrange("(k p) n -> p k n", p=128), hb[bi][:, :, 0:n],
                      reads=[t_hb[bi]], writes=[t_dst])
            ar.pop()

        def gcol_of(gi, tt):
            if gi < 16:
                return (gi // 4) * L + (gi % 4) * 512 + tt * 128
            return tt * L + 2048

        def attention():
            ar.push()
            ar.off = common_start
            KT = ar.bf16(2 * G4).rearrange("p (h n) -> p h n", h=2)
            Vt = ar.bf16(68 * 256).rearrange("p (t n) -> p t n", t=68)
            t_KT = T("KT"); t_Vt = T("Vt")
            Wk = ar.bf16(KC * 256).rearrange("p (k n) -> p k n", k=KC)
            Wv = ar.bf16(KC * 256).rearrange("p (k n) -> p k n", k=KC)
            Wq = ar.bf16(KC * 256).rearrange("p (k n) -> p k n", k=KC)
            t_W = T("attW")
            _mark = ar.off
            stg = ar.f32(2048)
            t_stg = T("stg")
            for (wd, wb) in ((kvwk, Wk), (kvwv, Wv), (wq, Wq)):
                S.dma("sp", stg.rearrange("p (k n) -> p k n", k=KC), wd.rearrange("(k p) n -> p k n", p=128), writes=[t_stg])
                _cp(S, "act", wb, stg.rearrange("p (k n) -> p k n", k=KC), [t_stg], [t_W])
            S.barrier()
            ar.off = _mark
            PT = ar.f32(128)
            S.dma("sp", PT, permT, writes=[t_W])
            am = ar.bf16(4 * 512).rearrange("p (j n) -> p j n", j=4)
            S.dma("sp", am, amask.rearrange("j p n -> p j n"), writes=[t_W])
            sub = ar.f32(1)
            S.dma("sp", sub, sublnT, writes=[t_W])
            lam_t = ar.f32(256)
            S.dma("sp", lam_t, lamb.partition_broadcast(128), writes=[t_W])
            lsc = ar.f32(8)
            _tt(S, "dve", lam_t[:, 0:64], lam_t[:, 0:64], lam_t[:, 64:128], ALU.mult, [t_W], [t_W])
            _tt(S, "dve", lam_t[:, 128:192], lam_t[:, 128:192], lam_t[:, 192:256], ALU.mult, [t_W], [t_W])
            S.op("dve", lambda e: e.reduce_sum(out=lsc[:, 0:1], in_=lam_t[:, 0:64], axis=AX.X), [t_W], [t_W])
            S.op("dve", lambda e: e.reduce_sum(out=lsc[:, 1:2], in_=lam_t[:, 128:192], axis=AX.X), [t_W], [t_W])
            _act(S, lsc[:, 2:4], lsc[:, 0:2], AF.Exp, [t_W], [t_W])
            _stt(S, lsc[:, 4:5], lsc[:, 3:4], -LAM_INIT, lsc[:, 2:3], ALU.add, ALU.subtract, [t_W], [t_W])
            _ts(S, "dve", sub, sub, 1.0 - LAM_INIT, None, ALU.mult, None, [t_W], [t_W])
            _ckpt(11)
            hkb = ar.bf16(KC * 512).rearrange("p (k n) -> p k n", k=KC)
            t_hk = T("hkb")
            rc = ar.f32(512); rs = ar.f32(512)
            t_rope = T("rope")
            kA = ar.f32(512); kr = ar.f32(512); kt2 = ar.f32(512)
            t_kA = T("kA"); t_kr = T("kr"); t_kt2 = T("kt2")
            vst = [ar.f32(256) for _ in range(2)]
            t_vst = [T("vst0"), T("vst1")]

            def load_grp(src, t_src, gi):
                for (r, c0, n, dst) in grp_pieces(gi):
                    S.dma("sp", hkb[:, :, dst:dst + n],
                          tok_view(src, r, c0, n),
                          reads=[t_src], writes=[t_hk])
                S.dma("sp", rc, ropeC[gi], writes=[t_rope])
                S.dma("sp", rs, ropeS[gi], writes=[t_rope])

            def proj_rope(W, hc, scale_out=None):
                for kc in range(KC):
                    _mm(S, ps[0], W[:, kc, hc * 128:(hc + 1) * 128], hkb[:, kc, :], kc == 0, kc == KC - 1, [t_W, t_hk], [tps[0]])
                _cp(S, "act", kA, ps[0], [tps[0]], [t_kA])
                _mm(S, ps[1], PT, kA, True, True, [t_W, t_kA], [tps[1]])
                _tt(S, "dve", kr, kA, rc, ALU.mult, [t_kA, t_rope], [t_kr])
                _tt(S, "dve", kt2, ps[1], rs, ALU.mult, [tps[1], t_rope], [t_kt2])
                _tt(S, "pool", kr, kr, kt2, ALU.add, [t_kr, t_kt2], [t_kr])

            for gi in range(NGRP):
                load_grp(C_out, tC_out, gi)
                for hc in range(2):
                    proj_rope(Wk, hc)
                    for (r, c0, n, dst) in grp_pieces(gi):
                        g0 = r * L + c0
                        _cp(S, "act", KT[:, hc, g0:g0 + n], kr[:, dst:dst + n], [t_kr], [t_KT])
                        S.dma("pool", o_k[hc * 128:(hc + 1) * 128, g0:g0 + n], kr[:, dst:dst + n], reads=[t_kr], writes=[t_out])
                for tt in range(4):
                    vi = tt % 2
                    for kc in range(KC):
                        _mm(S, ps[2 + vi][:, 0:256], hkb[:, kc, tt * 128:(tt + 1) * 128], Wv[:, kc, :], kc == 0, kc == KC - 1,
                            [t_W, t_hk], [tps[2 + vi]])
                    g0 = gcol_of(gi, tt)
                    _cp(S, "act", vst[vi], ps[2 + vi][:, 0:256], [tps[2 + vi]], [t_vst[vi]])
                    _cp(S, "dve", Vt[:, g0 // 128, :], vst[vi], [t_vst[vi]], [t_Vt])
                    S.dma("pool", o_v[g0:g0 + 128, :], vst[vi], reads=[t_vst[vi]], writes=[t_out])
                if gi == ATT_G:
                    _ckpt(12)
            S.barrier()
            _ckpt(13)
            Qp = [[ar.bf16(512) for _ in range(2)] for _ in range(2)]
            t_Qp = [[T(f"Qp{a}{b}") for b in range(2)] for a in range(2)]
            for a_ in range(2):
                for b_ in range(2):
                    _memset(S, "pool", Qp[a_][b_], 0.0, [t_Qp[a_][b_]])

            def qproj_gen(gq_):
                load_grp(D_out, tD_out, gq_)
                yield
                for hc_ in range(2):
                    for kc in range(KC):
                        _mm(S, ps[0], Wq[:, kc, hc_ * 128:(hc_ + 1) * 128], hkb[:, kc, :], kc == 0, kc == KC - 1, [t_W, t_hk], [tps[0]])
                    yield
                    _cp(S, "act", kA, ps[0], [tps[0]], [t_kA])
                    yield
                    _mm(S, ps[0], PT, kA, True, True, [t_W, t_kA], [tps[0]])
                    _tt(S, "dve", kr, kA, rc, ALU.mult, [t_kA, t_rope], [t_kr])
                    yield
                    _tt(S, "dve", kt2, ps[0], rs, ALU.mult, [tps[0], t_rope], [t_kt2])
                    yield
                    _tt(S, "pool", kr, kr, kt2, ALU.add, [t_kr, t_kt2], [t_kr])
                    yield
                    _cp(S, "act", Qp[hc_][0][0:64, :], kr[0:64, :], [t_kr], [t_Qp[hc_][0]])
                    _cp(S, "act", Qp[hc_][1][64:128, :], kr[64:128, :], [t_kr], [t_Qp[hc_][1]])
                    yield
            Eb = [[ar.bf16(512) for _ in range(2)] for _ in range(2)]
            t_Eb = [[T(f"E{a}{b}") for b in range(2)] for a in range(2)]
            o0 = ar.f32(512); o1 = ar.f32(512); rr = ar.f32(512)
            t_o0 = T("o0"); t_o1 = T("o1"); t_rr = T("rr")
            zb = [ar.bf16(512) for _ in range(2)]
            t_zb = [T("zo0"), T("zo1")]
            Eacc = [ar.f32(512) for _ in range(2)]
            t_Eacc = [T("Eacc0"), T("Eacc1")]

            def finish(hc, n, pO0, pO1, pS0, pS1, dsts):
                S.op("dve", lambda e: e.reciprocal(out=rr[:, 0:n], in_=pS0), [tps[6]], [t_rr])
                _tt(S, "dve", o0[:, 0:n], pO0, rr[:, 0:n], ALU.mult, [tps[4], t_rr], [t_o0])
                S.op("dve", lambda e: e.reciprocal(out=rr[:, 0:n], in_=pS1), [tps[7], tps[6]], [t_rr])
                _tt(S, "dve", o1[:, 0:n], pO1, rr[:, 0:n], ALU.mult, [tps[5], tps[4], t_rr], [t_o1])
                _stt(S, o0[:, 0:n], o1[:, 0:n], lsc[:, 4:5], o0[:, 0:n], ALU.mult, ALU.add, [t_o1, t_o0, t_W], [t_o0])
                _tt(S, "pool", o1[:, 0:n], o0[:, 0:n], o0[:, 0:n], ALU.mult, [t_o0], [t_o1])
                _mm(S, ps[0][:, 0:n], ones_f, o1[:, 0:n], True, True, [tconst, t_o1], [tps[0]])
                _rsqrt(S, rr[:, 0:n], ps[0][:, 0:n], EPS, [tps[0]], t_rr, scale=1.0 / 128)
                _tt(S, "dve", o0[:, 0:n], o0[:, 0:n], rr[:, 0:n], ALU.mult, [t_o0, t_rr], [t_o0])
                zi = hc
                _ts(S, "dve", zb[zi][:, 0:n], o0[:, 0:n], sub[:, 0:1], None, ALU.mult, None, [t_o0, t_W], [t_zb[zi]])
                for (g0, d0, nn) in dsts:
                    for (ch, off, m, rel) in zsplit(g0, nn):
                        S.dma("pool", E_in[ch * 256 + hc * 128:ch * 256 + (hc + 1) * 128, off:off + m],
                              zb[zi][:, d0 + rel:d0 + rel + m], reads=[t_zb[zi]], writes=[tE_in])

            for gq in range(16):
                for _ in qproj_gen(gq):
                    pass
                nkt = 4 * (gq + 1)
                EbL = [Eb[0][0], Eb[0][1], Eb[1][0], Eb[1][1]]
                t_EbL = [t_Eb[0][0], t_Eb[0][1], t_Eb[1][0], t_Eb[1][1]]
                LOOK = 2
                for hc in range(2):
                    units = [(kt_, c) for kt_ in range(nkt) for c in range(2)]

                    def emit_s(i, hc=hc, units=units):
                        kt_, c = units[i]
                        g0 = (kt_ // 16) * L + (kt_ % 16) * 128
                        sb = 1 + i % 3
                        _mm(S, ps[sb], KT[:, hc, g0:g0 + 128], Qp[hc][c], True, True, [t_KT, t_Qp[hc][c]], [tps[sb]])

                    def emit_rest(i, hc=hc, units=units, nkt=nkt, gq=gq):
                        kt_, c = units[i]
                        g0 = (kt_ // 16) * L + (kt_ % 16) * 128
                        sb = 1 + i % 3
                        E_, tE_ = EbL[i % 4], t_EbL[i % 4]
                        _act(S, E_, ps[sb], AF.Exp, [tps[sb]], [tE_], scale=ATT_SCALE)
                        if kt_ >= 4 * gq:
                            _tt(S, "pool" if c == 0 else "dve", E_, E_, am[:, kt_ - 4 * gq, :], ALU.mult, [tE_, t_W], [tE_])
                        _mm(S, ps[4 + c], Vt[:, g0 // 128, hc * 128:(hc + 1) * 128], E_, kt_ == 0, kt_ == nkt - 1,
                            [t_Vt, tE_], [tps[4 + c]])
                        aeng = "pool" if c == 0 else "dve"
                        if kt_ == 0:
                            _cp(S, aeng, Eacc[c], E_, [tE_], [t_Eacc[c]])
                        else:
                            _tt(S, aeng, Eacc[c], Eacc[c], E_, ALU.add, [tE_, t_Eacc[c]], [t_Eacc[c]])
                        if kt_ == nkt - 1:
                            _mm(S, ps[6 + c], ones_f, Eacc[c], True, True, [tconst, t_Eacc[c]], [tps[6 + c]])
                    for i in range(min(LOOK, len(units))):
                        emit_s(i)
                    for i in range(len(units)):
                        if i + LOOK < len(units):
                            emit_s(i + LOOK)
                        emit_rest(i)
                    gg0 = (gq // 4) * L + (gq % 4) * 512
                    finish(hc, 512, ps[4], ps[5], ps[6], ps[7], [(gg0, 0, 512)])
                _ckpt(14)
            _ckpt(15)
            for _ in qproj_gen(16):
                pass
            ck = [ar.f32(256) for _ in range(2)]; cv = [ar.f32(256) for _ in range(2)]
            t_ck = [T("ck0"), T("ck1")]; t_cv = [T("cv0"), T("cv1")]
            kTt = [ar.bf16(256) for _ in range(2)]; vbt = [ar.bf16(256) for _ in range(2)]
            t_kTt = [T("kTt0"), T("kTt1")]; t_vbt = [T("vbt0"), T("vbt1")]
            EbS = [Eb[0][0], Eb[0][1], Eb[1][0], Eb[1][1]]
            t_EbS = [t_Eb[0][0], t_Eb[0][1], t_Eb[1][0], t_Eb[1][1]]
            for s in range(8):
                q0 = 64 * s
                gs0 = (s // 2) * L + 2048 + 64 * (s % 2)
                R_ = slice(64 * (s % 2), 64 * (s % 2) + 64)
                for hc in range(2):
                    hcs = slice(hc * 128, (hc + 1) * 128)
                    units = [(kt_, c) for kt_ in range(17) for c in range(2)]

                    def s_emit_s(i, s=s, hc=hc, hcs=hcs, units=units, q0=q0, gs0=gs0, R_=R_):
                        kt_, c = units[i]
                        bi = kt_ % 2
                        last = kt_ == 16
                        P = slice(64 * c, 64 * c + 64)
                        sb = 1 + i % 3
                        if c == 0 and not last:
                            S.dma("sp", ck[bi][:, 0:128], cache_k[s, kt_ * 128:(kt_ + 1) * 128, hcs], writes=[t_ck[bi]])
                            S.dma("sp", cv[bi][:, 0:128], cache_v[s, kt_ * 128:(kt_ + 1) * 128, hcs], writes=[t_cv[bi]])
                            _cp(S, "pool", vbt[bi][:, 0:128], cv[bi][:, 0:128], [t_cv[bi]], [t_vbt[bi]])
                            _tr(S, ps[0][:, 0:128], ck[bi][:, 0:128], ident, [t_ck[bi], tconst], [tps[0]])
                            _cp(S, "act", kTt[bi][:, 0:128], ps[0][:, 0:128], [tps[0]], [t_kTt[bi]])
                        if not last:
                            _mm(S, ps[sb][:, 0:64], kTt[bi][:, 0:128], Qp[hc][c][:, q0:q0 + 64], True, True,
                                [t_kTt[bi], t_Qp[hc][c]], [tps[sb]])
                        else:
                            _mm(S, ps[sb][R_, 0:64], KT[:, hc, gs0:gs0 + 64], Qp[hc][c][:, q0:q0 + 64], True, True,
                                [t_KT, t_Qp[hc][c]], [tps[sb]])

                    def s_emit_rest(i, hc=hc, hcs=hcs, units=units, gs0=gs0, R_=R_):
                        kt_, c = units[i]
                        bi = kt_ % 2
                        last = kt_ == 16
                        sb = 1 + i % 3
                        E_, tE_ = EbS[i % 4], t_EbS[i % 4]
                        if not last:
                            so = ps[sb][:, 0:64]
                            eo = E_[:, 0:64]
                            vl = vbt[bi][:, 0:128]
                            vr = [t_vbt[bi]]
                        else:
                            _memset(S, "pool", E_[:, 0:64], 0.0, [tE_])
                            so = ps[sb][R_, 0:64]
                            eo = E_[R_, 0:64]
                            vl = Vt[:, gs0 // 128, hcs]
                            vr = [t_Vt]
                        _act(S, eo, so, AF.Exp, [tps[sb]], [tE_], scale=ATT_SCALE)
                        ef = E_[:, 0:64]
                        _mm(S, ps[4 + c][:, 0:64], vl, ef, kt_ == 0, last, vr + [tE_], [tps[4 + c]])
                        _mm(S, ps[6 + c][:, 0:64], ones_bf, ef, kt_ == 0, last, [tconst, tE_], [tps[6 + c]])
                    for i in range(2):
                        s_emit_s(i)
                    for i in range(len(units)):
                        if i + 2 < len(units):
                            s_emit_s(i + 2)
                        s_emit_rest(i)
                    finish(hc, 64, ps[4][:, 0:64], ps[5][:, 0:64], ps[6][:, 0:64], ps[7][:, 0:64], [(gs0, 0, 64)])
            ar.pop()


        def out_proj(Zout, t_Z, w_dram, co_sel):
            ar.push()
            zb = ar.bf16(KC * 1088).rearrange("p (k n) -> p k n", k=KC)
            t_zb = T("zb")
            zt = [ar.bf16(KC * 1088).rearrange("p (k n) -> p k n", k=KC) for _ in range(2)]
            t_zt = [T("zt0"), T("zt1")]
            Fo = ar.f32(KC * 1088).rearrange("p (k n) -> p k n", k=KC)
            t_Fo = T("Fo2")
            for p in range(2):
                cgs = [(p * 1024, 512, 0), (p * 1024 + 512, 512, 512), (2048 + 64 * p, 64, 1024)]
                for r in range(4):
                    zi = r % 2
                    for (c0, n, hc0) in cgs:
                        for (ch, off, m, rel) in zsplit(r * L + c0, n):
                            S.dma("sp", zt[zi][:, :, hc0 + rel:hc0 + rel + m],
                                  Zout[ch * D:(ch + 1) * D, off:off + m].rearrange("(k p) n -> p k n", p=128),
                                  reads=[t_Z], writes=[t_zt[zi]])
                    if r == 0:
                        _ts(S, "dve", zb, zt[zi], sel[:, 0:1], None, ALU.mult, None, [t_zt[zi], t_sel], [t_zb])
                    else:
                        for kc in range(KC):
                            _stt(S, zb[:, kc, :], zt[zi][:, kc, :], sel[:, r:r + 1], zb[:, kc, :], ALU.mult, ALU.add,
                                 [t_zt[zi], t_sel, t_zb], [t_zb])
                for dc in range(KC):
                    wv = w_dram[:, dc * 128:(dc + 1) * 128].rearrange("(k p) n -> p k n", p=128)
                    b, tb = load_w(wv, KC, 128)
                    for ci, (c0, n, hc0) in enumerate(cgs):
                        pb = ci
                        for kc in range(KC):
                            _mm(S, ps[pb][:, 0:n], b[:, kc, :], zb[:, kc, hc0:hc0 + n], kc == 0, kc == KC - 1,
                                [tb, t_zb], [tps[pb]])
                        _cp(S, "act" if ci % 2 == 0 else "dve", Fo[:, dc, hc0:hc0 + n], ps[pb][:, 0:n], [tps[pb]], [t_Fo])
                for (c0, n, hc0) in cgs:
                    postnorm_add(Fo, t_Fo, hc0, c0, n, co_sel(seq_of(c0)), 7)
            ar.pop()

        def rwkv():
            ar.push()
            ar.off = common_start
            cm = ar.f32(640)
            t_cm = T("cm")
            S.dma("sp", cm, cmask, writes=[t_cm])
            m4 = cm[:, 0:512]
            m_ij = cm[:, 512:640]
            scanm = ar.f32(512)
            _memset(S, "pool", scanm, 1.0, [t_cm])
            _memset(S, "pool", scanm.rearrange("p (c t) -> p c t", t=64)[:, :, 0:1], 0.0, [t_cm])
            rv = ar.f32(14).rearrange("p (w h) -> p w h", w=7)
            mu = ar.f32(48).rearrange("p (w k) -> p w k", w=6)
            shf = ar.f32(64).rearrange("p (k s) -> p k s", k=KC)
            S.dma("sp", rv, rvecT, writes=[t_cm])
            S.dma("sp", mu, muT, writes=[t_cm])
            S.dma("sp", shf, shiftT, writes=[t_cm])
            Wst = ar.bf16(16 * 768).rearrange("p (k w n) -> p k w n", k=16, w=3)
            Wl = ar.bf16(16 * 256).rearrange("p (k n) -> p k n", k=16)
            W2 = ar.bf16(768)
            t_W = T("rwkvW")
            NT = 9
            tfall = ar.f32(NT * 512)
            tf = [tfall[:, i * 512:(i + 1) * 512] for i in range(NT)]
            t_wf = [T("stgA"), T("stgB")]
            wst_l = [tfall[:, 0:2048], tfall[:, 2048:4096]]
            lctr = [0]
            for pi, mi in ((0, 0), (1, 2), (2, 3)):
                si = lctr[0] % 2
                lctr[0] += 1
                f = wst_l[si][:, 0:KC * 256].rearrange("p (k n) -> p k n", k=KC)
                S.dma("sp", f, w_rkv[pi].rearrange("(k p) n -> p k n", p=128), writes=[t_wf[si]])
                _cp(S, "act", Wst[:, 0:8, pi, :], f, [t_wf[si]], [t_W])
                for kc in range(KC):
                    _ts(S, "dve", Wst[:, 8 + kc, pi, :], f[:, kc, :], mu[:, mi, kc:kc + 1], None, ALU.mult, None,
                        [t_wf[si], t_cm], [t_W])
            si = lctr[0] % 2
            lctr[0] += 1
            f = wst_l[si][:, 0:KC * 256].rearrange("p (k n) -> p k n", k=KC)
            S.dma("sp", f, w_l1.rearrange("(k p) n -> p k n", p=128), writes=[t_wf[si]])
            _cp(S, "act", Wl[:, 0:8, :], f, [t_wf[si]], [t_W])
            for kc in range(KC):
                for (a0_, a1_, mi) in ((0, 64, 1), (64, 128, 4), (128, 256, 5)):
                    _ts(S, "dve", Wl[:, 8 + kc, a0_:a1_], f[:, kc, a0_:a1_], mu[:, mi, kc:kc + 1], None, ALU.mult, None,
                        [t_wf[si], t_cm], [t_W])
            si = lctr[0] % 2
            lctr[0] += 1
            f = wst_l[si][:, 0:768]
            S.dma("sp", f[0:64, 0:256], w_w2, writes=[t_wf[si]])
            S.dma("sp", f[0:64, 256:512], w_a2, writes=[t_wf[si]])
            S.dma("sp", f[:, 512:768], w_g2, writes=[t_wf[si]])
            _cp(S, "act", W2[0:64, 0:512], f[0:64, 0:512], [t_wf[si]], [t_W])
            _cp(S, "act", W2[:, 512:768], f[:, 512:768], [t_wf[si]], [t_W])

            Hb1 = ar.bf16(KC * 514).rearrange("p (k n) -> p k n", k=KC)
            Hb = [Hb1, Hb1]
            t_H1 = T("H0")
            t_H = [t_H1, t_H1]
            lastc = ar.bf16(KC * 2).rearrange("p (k n) -> p k n", k=KC)
            t_lastc = T("lastc")
            dxb = ar.bf16(KC * 512).rearrange("p (k n) -> p k n", k=KC)
            t_dx = T("dx")
            twb = ar.bf16(512); tab = ar.bf16(512); tgb = ar.bf16(512)
            t_lm = T("loramid")
            t_tf = [T(f"tf{i}") for i in range(NT)]
            S.barrier()
            _ckpt(1)
            rt = [ar.bf16(512) for _ in range(2)]; kt = [ar.bf16(512) for _ in range(2)]
            at = [ar.bf16(512) for _ in range(2)]; bt = [ar.bf16(512) for _ in range(2)]
            t_fm = [T("fm0"), T("fm1")]
            bon = [ar.f32(512) for _ in range(2)]; gg = [ar.bf16(512) for _ in range(2)]
            t_bg = [T("bg0"), T("bg1")]
            gCs = ar.f32(16).rearrange("p (h c) -> p h c", h=2)
            t_gC = T("gC")
            Vtm = ar.bf16(4 * 256).rearrange("p (t n) -> p t n", t=4)
            Khtm = ar.bf16(4 * 256).rearrange("p (t n) -> p t n", t=4)
            Bhtm = ar.bf16(4 * 256).rearrange("p (t n) -> p t n", t=4)
            t_tm = T("tokmaj")
            A_k = [[ar.bf16(128) for _ in range(2)] for _ in range(4)]
            BS = [[ar.bf16(256) for _ in range(2)] for _ in range(4)]
            Sf = [ar.f32(128) for _ in range(4)]
            t_ut = [T(f"ut{i}") for i in range(4)]
            Tt = [[ar.bf16(128) for _ in range(4)] for _ in range(2)]
            A3 = [[ar.bf16(384) for _ in range(4)] for _ in range(2)]
            t_res = [[T(f"res{a}{b}") for b in range(4)] for a in range(2)]
            Mst = ar.f32(128).rearrange("p (h v) -> p h v", h=2)
            Mbf = ar.bf16(256).rearrange("p (k v) -> p k v", k=4)
            Mbf4 = Mbf.rearrange("p (c h) v -> p c h v", h=2)
            t_M = T("M"); t_Mbf = T("Mbf")
            Xs = ar.bf16(256); Us = ar.bf16(256)
            t_Xs = T("Xs"); t_Us = T("Us")
            Yg = [ar.f32(512) for _ in range(2)]; Zb = [ar.bf16(512) for _ in range(2)]
            t_Yg = [T("Yg0"), T("Yg1")]
            t_Yf = T("Yf"); t_Zb = [T("Zb0"), T("Zb1")]
            _memset(S, "dve", Mst, 0.0, [t_M])
            _memset(S, "dve", Mbf, 0.0, [t_Mbf])
            _memset(S, "dve", Xs, 0.0, [t_Xs])
            _memset(S, "dve", Us, 0.0, [t_Us])

            def upd_mbf(eng):
                for hh_ in range(2):
                    P_ = slice(64 * hh_, 64 * hh_ + 64)
                    _cp(S, eng, Mbf4[P_, :, hh_, :], Mst[P_, :, :], [t_M], [t_Mbf])
            PYM = 0; PXU = 1; PEX = 2

            for gi in range(NGRP):
                hi = gi % 2
                H = Hb[hi]
                if gi > 0:
                    _cp(S, "dve", lastc[:, :, 0:1], H[:, :, 512:513], [t_H[hi]], [t_lastc])
                for (r, c0, n, dst) in grp_pieces(gi):
                    S.dma("sp", H[:, :, 1 + dst:1 + dst + n],
                          tok_view(A_out, r, c0, n),
                          reads=[tA_out], writes=[t_H[hi]])
                if gi == 0:
                    _memset(S, "dve", H[:, :, 0:1], 0.0, [t_H[hi]])
                else:
                    _cp(S, "dve", H[:, :, 0:1], lastc[:, :, 0:1], [t_lastc], [t_H[hi]])
                _tt(S, "dve", dxb, H[:, :, 0:512], H[:, :, 1:513], ALU.subtract, [t_H[hi]], [t_dx])
                if gi == 16:
                    _tt(S, "dve", dxb.rearrange("p k (s t) -> p k s t", t=64)[:, :, :, 0], shf,
                        H[:, :, 1:513].rearrange("p k (s t) -> p k s t", t=64)[:, :, :, 0], ALU.subtract,
                        [t_H[hi], t_cm, t_dx], [t_dx])

                def rhs(kk):
                    return H[:, kk, 1:513] if kk < 8 else dxb[:, kk - 8, :]
                for (b, c0_, c1_, dstb, fn) in ((4, 0, 64, twb, AF.Tanh), (5, 64, 128, tab, AF.Copy), (6, 128, 256, tgb, AF.Sigmoid)):
                    m = c1_ - c0_
                    for kk in range(16):
                        _mm(S, ps[b][0:m, :], Wl[:, kk, c0_:c1_], rhs(kk), kk == 0, kk == 15, [t_W, t_H[hi], t_dx], [tps[b]])
                    _act(S, dstb[0:m, :], ps[b][0:m, :], fn, [tps[b]], [t_lm])
                _ckpt(2)
                for hc in range(2):
                    R, K_, V_, LW, CUM, A_, KKN, T1, T2 = tf
                    tR, tK, tV, tLW, tCUM, tA, tKKN, tT1, tT2 = t_tf
                    for pi, (dstt, tdst) in enumerate(((R, tR), (K_, tK), (V_, tV))):
                        b = 4 + pi
                        for kk in range(16):
                            _mm(S, ps[b], Wst[:, kk, pi, hc * 128:(hc + 1) * 128], rhs(kk), kk == 0, kk == 15,
                                [t_W, t_H[hi], t_dx], [tps[b]])
                        _cp(S, "act" if pi != 1 else "dve", dstt, ps[b], [tps[b]], [tdst])
                    _mm(S, ps[7], W2[0:64, hc * 128:(hc + 1) * 128], twb[0:64, :], True, True, [t_W, t_lm], [tps[7]])
                    _act(S, LW, ps[7], AF.Sigmoid, [tps[7], t_cm], [tLW], bias=rv[:, 0, hc:hc + 1])
                    _ts(S, "dve", LW, LW, -math.exp(-0.5), None, ALU.mult, None, [tLW], [tLW])
                    _mm(S, ps[7], W2[0:64, 256 + hc * 128:256 + (hc + 1) * 128], tab[0:64, :], True, True, [t_W, t_lm], [tps[7]])
                    _act(S, A_, ps[7], AF.Sigmoid, [tps[7], t_cm], [tA], bias=rv[:, 1, hc:hc + 1])
                    _mm(S, ps[7], W2[:, 512 + hc * 128:512 + (hc + 1) * 128], tgb, True, True, [t_W, t_lm], [tps[7]])
                    _cp(S, "act", gg[hc], ps[7], [tps[7]], [t_bg[hc]])
                    _ts(S, "dve", T1, K_, rv[:, 2, hc:hc + 1], None, ALU.mult, None, [tK, t_cm], [tT1])
                    _tt(S, "pool", T2, T1, T1, ALU.mult, [tT1], [tT2])
                    _mm(S, ps[7], blk_f, T2, True, True, [tconst, tT2], [tps[7]])
                    _rsqrt(S, T2, ps[7], 1e-24, [tps[7]], tT2)
                    _tt(S, "dve", KKN, T1, T2, ALU.mult, [tT1, tT2], [tKKN])
                    _ts(S, "dve", T1, A_, -1.0, rv[:, 3, hc:hc + 1], ALU.add, ALU.mult, [tA, t_cm], [tT1])
                    _ts(S, "dve", T1, T1, 1.0, None, ALU.add, None, [tT1], [tT1])
                    _tt(S, "dve", K_, K_, T1, ALU.mult, [tK, tT1], [tK])
                    _stt(S, T1, R, rv[:, 6, hc:hc + 1], K_, ALU.mult, ALU.mult, [tR, tK, t_cm], [tT1])
                    _mm(S, ps[7], blk_f, T1, True, True, [tconst, tT1], [tps[7]])
                    _tt(S, "dve", bon[hc], ps[7], V_, ALU.mult, [tps[7], tV], [t_bg[hc]])
                    _tt(S, "pool", A_, KKN, A_, ALU.mult, [tKKN, tA], [tA])
                    S.op("dve", lambda e, o=CUM, d0=scanm, d1=LW: e.tensor_tensor_scan(out=o, data0=d0, data1=d1, initial=0.0,
                                                                                        op0=ALU.mult, op1=ALU.add),
                         [tLW, t_cm], [tCUM])
                    cum3 = CUM.rearrange("p (c t) -> p c t", t=64)
                    _ckpt(3)
                    _act(S, gCs[:, hc, :], cum3[:, :, 63], AF.Exp, [tCUM], [t_gC])
                    _act(S, T1, CUM, AF.Exp, [tCUM], [tT1])
                    _tt(S, "dve", rt[hc], R, T1, ALU.mult, [tR, tT1], [t_fm[hc]])
                    _tt(S, "pool", T2, CUM, LW, ALU.subtract, [tCUM, tLW], [tT2])
                    _act(S, T2, T2, AF.Exp, [tT2], [tT2])
                    _stt(S, at[hc], KKN, -1.0, T2, ALU.mult, ALU.mult, [tKKN, tT2], [t_fm[hc]])
                    _act(S, T1, CUM, AF.Exp, [tCUM], [tT1], scale=-1.0)
                    _tt(S, "dve", kt[hc], K_, T1, ALU.mult, [tK, tT1], [t_fm[hc]])
                    _tt(S, "pool", bt[hc], A_, T1, ALU.mult, [tA, tT1], [t_fm[hc]])
                    _tt(S, "dve", T2.rearrange("p (c t) -> p c t", t=64), cum3[:, :, 63:64].broadcast_to([128, 8, 64]), cum3,
                        ALU.subtract, [tCUM], [tT2])
                    _act(S, T2, T2, AF.Exp, [tT2], [tT2])
                    _tt(S, "dve", K_, K_, T2, ALU.mult, [tK, tT2], [tK])
                    _tt(S, "pool", A_, A_, T2, ALU.mult, [tA, tT2], [tA])
                    for (src, tsrc, dstm, b) in ((V_, tV, Vtm, 4), (K_, tK, Khtm, 5), (A_, tA, Bhtm, 6)):
                        for tp in range(4):
                            _tr(S, ps[b][:, tp * 128:(tp + 1) * 128], src[:, tp * 128:(tp + 1) * 128], ident,
                                [tsrc, tconst], [tps[b]])
                        _cp(S, "act" if b != 5 else "dve", dstm[:, :, hc * 128:(hc + 1) * 128],
                            ps[b].rearrange("p (t n) -> p t n", t=4), [tps[b]], [t_tm])

                _ckpt(4)
                def ut_init(tp, k4):
                    hc, hh = k4 // 2, k4 % 2
                    P = slice(64 * hh, 64 * hh + 64)
                    cs_ = slice(tp * 128, (tp + 1) * 128)
                    rb = tp % 2
                    pa = 4 + k4
                    fmr = [t_fm[hc]]
                    ex = ps[PEX][:, k4 * 128:(k4 + 1) * 128]
                    _mm(S, ps[pa][:, 0:128], bt[hc][P, cs_], at[hc][P, cs_], True, True, fmr, [tps[pa]])
                    _mm(S, ps[pa][:, 128:256], bt[hc][P, cs_], rt[hc][P, cs_], True, True, fmr, [tps[pa]])
                    _mm(S, ps[pa][:, 256:384], kt[hc][P, cs_], at[hc][P, cs_], True, True, fmr, [tps[pa]])
                    _mm(S, ps[pa][:, 384:512], kt[hc][P, cs_], rt[hc][P, cs_], True, True, fmr, [tps[pa]])
                    _mm(S, ex, at[hc][P, cs_], bt[hc][P, cs_], True, True, fmr, [tps[PEX]])
                    _tt(S, "dve", Sf[k4], ps[pa][:, 0:128], m4[:, 0:128], ALU.mult, [tps[pa], t_cm], [t_ut[k4]])
                    _tt(S, "dve", A3[rb][k4], ps[pa][:, 128:512], m4[:, 128:512], ALU.mult, [tps[pa], t_cm], [t_res[rb][k4]])
                    _cp(S, "act", BS[k4][0][:, 0:128], Sf[k4], [t_ut[k4]], [t_ut[k4]])
                    _tt(S, "dve", Sf[k4], Sf[k4], ident, ALU.add, [t_ut[k4], tconst], [t_ut[k4]])
                    _cp(S, "act", BS[k4][0][:, 128:256], Sf[k4], [t_ut[k4]], [t_ut[k4]])
                    _tt(S, "dve", A_k[k4][0], ex, m_ij, ALU.mult, [tps[PEX], t_cm], [t_ut[k4]])

                def ut_level(tp, k4, lvl):
                    rb = tp % 2
                    pa = 4 + k4
                    tu = t_ut[k4]
                    if lvl == 0:
                        _mm(S, ps[pa][:, 0:128], A_k[k4][0], BS[k4][0][:, 0:128], True, True, [tu], [tps[pa]])
                        _mm(S, ps[pa][:, 256:384], BS[k4][0][:, 0:128], A_k[k4][0], True, True, [tu], [tps[pa]])
                        _cp(S, "act", BS[k4][1][:, 0:128], ps[pa][:, 0:128], [tps[pa]], [tu])
                        _cp(S, "act", BS[k4][1][:, 128:256], BS[k4][0][:, 128:256], [tu], [tu])
                        _cp(S, "dve", A_k[k4][1], ps[pa][:, 256:384], [tps[pa]], [tu])
                    elif lvl < 5:
                        c_, n_ = (lvl % 2), 1 - (lvl % 2)
                        _mm(S, ps[pa][:, 0:256], A_k[k4][c_], BS[k4][c_], True, True, [tu], [tps[pa]])
                        _mm(S, ps[pa][:, 256:384], BS[k4][c_][:, 0:128], A_k[k4][c_], True, True, [tu], [tps[pa]])
                        _cp(S, "act", BS[k4][n_][:, 0:128], ps[pa][:, 0:128], [tps[pa]], [tu])
                        _tt(S, "dve", Sf[k4], Sf[k4], ps[pa][:, 128:256], ALU.add, [tps[pa], tu], [tu])
                        _cp(S, "act", BS[k4][n_][:, 128:256], Sf[k4], [tu], [tu])
                        _cp(S, "dve", A_k[k4][n_], ps[pa][:, 256:384], [tps[pa]], [tu])
                    else:
                        c_ = lvl % 2
                        _mm(S, ps[pa][:, 0:128], A_k[k4][c_], BS[k4][c_][:, 128:256], True, True, [tu], [tps[pa]])
                        _tt(S, "dve", Tt[rb][k4], Sf[k4], ps[pa][:, 0:128], ALU.add, [tps[pa], tu], [t_res[rb][k4]])

                def chain_stage(tp, cc, st):
                    rb = tp % 2
                    c = 2 * tp + cc
                    Q = slice(64 * cc, 64 * cc + 64)
                    ccol = slice(c * 64, c * 64 + 64)
                    if st == 0:
                        if gi == 16:
                            S.dma("sp", Mst, wkv0[:, :, c, :], writes=[t_M])
                            upd_mbf("dve")
                        for k4 in range(4):
                            hc, hh = k4 // 2, k4 % 2
                            vcol = slice(hc * 128 + hh * 64, hc * 128 + hh * 64 + 64)
                            _mm(S, ps[PXU][Q, k4 * 64:(k4 + 1) * 64], at[hc][:, ccol], Mbf[:, k4, :], True, False,
                                [t_fm[hc], t_Mbf], [tps[PXU]])
                            _mm(S, ps[PXU][Q, k4 * 64:(k4 + 1) * 64], A3[rb][k4][:, 128 + 64 * cc:128 + 64 * cc + 64],
                                Vtm[:, tp, vcol], False, True, [t_res[rb][k4], t_tm], [tps[PXU]])
                        _cp(S, "act", Xs[Q, :], ps[PXU][Q, 0:256], [tps[PXU]], [t_Xs])
                    elif st == 1:
                        for k4 in range(4):
                            _mm(S, ps[PXU][Q, 256 + k4 * 64:256 + (k4 + 1) * 64], Tt[rb][k4][:, 64 * cc:64 * cc + 64],
                                Xs[:, k4 * 64:(k4 + 1) * 64], True, True, [t_res[rb][k4], t_Xs], [tps[PXU]])
                        _cp(S, "dve", Us[Q, :], ps[PXU][Q, 256:512], [tps[PXU]], [t_Us])
                    else:
                        for k4 in range(4):
                            hc, hh = k4 // 2, k4 % 2
                            P = slice(64 * hh, 64 * hh + 64)
                            vcol = slice(hc * 128 + hh * 64, hc * 128 + hh * 64 + 64)
                            yo = ps[PYM][P, hc * 128 + cc * 64:hc * 128 + cc * 64 + 64]
                            _mm(S, yo, Mbf[:, k4, :], rt[hc][:, ccol], True, False, [t_Mbf, t_fm[hc]], [tps[PYM]])
                            _mm(S, yo, Us[:, k4 * 64:(k4 + 1) * 64], A3[rb][k4][:, 64 * cc:64 * cc + 64], False, False,
                                [t_Us, t_res[rb][k4]], [tps[PYM]])
                            _mm(S, yo, Vtm[:, tp, vcol], A3[rb][k4][:, 256 + 64 * cc:256 + 64 * cc + 64], False, True,
                                [t_tm, t_res[rb][k4]], [tps[PYM]])
                            mo = ps[PYM][P, 256 + hc * 64:256 + (hc + 1) * 64]
                            _mm(S, mo, Bhtm[Q, tp, vcol], Us[Q, k4 * 64:(k4 + 1) * 64], True, False, [t_tm, t_Us], [tps[PYM]])
                            _mm(S, mo, Khtm[Q, tp, vcol], Vtm[Q, tp, vcol], False, True, [t_tm], [tps[PYM]])
                        for hc in range(2):
                            _stt(S, Mst[:, hc, :], Mst[:, hc, :], gCs[:, hc, c:c + 1], ps[PYM][:, 256 + hc * 64:256 + (hc + 1) * 64],
                                 ALU.mult, ALU.add, [t_M, t_gC, tps[PYM]], [t_M])
                        upd_mbf("act")
                        if gi == 16:
                            S.dma("pool", o_wkv[:, 1 + c, :, :], Mst, reads=[t_M], writes=[t_out])
                        elif gi == 15 and c == 7:
                            S.dma("pool", o_wkv[:, 0, :, :], Mst, reads=[t_M], writes=[t_out])
                        if cc == 1:
                            for hc in range(2):
                                _cp(S, "dve", Yg[hc][:, tp * 128:(tp + 1) * 128], ps[PYM][:, hc * 128:(hc + 1) * 128],
                                    [tps[PYM]], [t_Yg[hc]])

                for tp in range(5):
                    for step in range(7):
                        if tp < 4:
                            for k4 in range(4):
                                if step == 0:
                                    ut_init(tp, k4)
                                else:
                                    ut_level(tp, k4, step - 1)
                        if tp >= 1 and step >= 1:
                            cc_, st_ = divmod(step - 1, 3)
                            chain_stage(tp - 1, cc_, st_)
                _ckpt(6)
                for hc in range(2):
                    T1, T2, T3 = tf[0], tf[1], tf[2]
                    tT1, tT2, tT3 = t_tf[0], t_tf[1], t_tf[2]
                    Yf = Yg[hc]
                    t_Yf = t_Yg[hc]
                    _mm(S, ps[7], blk_f, Yf, True, True, [tconst, t_Yf], [tps[7]])
                    _ts(S, "dve", T1, ps[7], 1.0 / 64, None, ALU.mult, None, [tps[7]], [tT1])
                    _tt(S, "pool", T2, Yf, Yf, ALU.mult, [t_Yf], [tT2])
                    _mm(S, ps[7], blk_f, T2, True, True, [tconst, tT2], [tps[7]])
                    _tt(S, "pool", T3, T1, T1, ALU.mult, [tT1], [tT3])
                    _stt(S, T2, ps[7], 1.0 / 64, T3, ALU.mult, ALU.subtract, [tps[7], tT3], [tT2])
                    _rsqrt(S, T2, T2, GN_EPS, [tT2], tT2)
                    _tt(S, "dve", T1, Yf, T1, ALU.subtract, [t_Yf, tT1], [tT1])
                    _tt(S, "dve", T1, T1, T2, ALU.mult, [tT1, tT2], [tT1])
                    _ts(S, "dve", T1, T1, rv[:, 4, hc:hc + 1], rv[:, 5, hc:hc + 1], ALU.mult, ALU.add, [tT1, t_cm], [tT1])
                    _tt(S, "dve", T1, T1, bon[hc], ALU.add, [tT1, t_bg[hc]], [tT1])
                    zi = hc
                    _tt(S, "dve", Zb[zi], T1, gg[hc], ALU.mult, [tT1, t_bg[hc]], [t_Zb[zi]])
                    for (r, c0, n, dst) in grp_pieces(gi):
                        for (ch, off, m, rel) in zsplit(r * L + c0, n):
                            S.dma("pool", B_in[ch * 256 + hc * 128:ch * 256 + (hc + 1) * 128, off:off + m],
                                  Zb[zi][:, dst + rel:dst + rel + m], reads=[t_Zb[zi]], writes=[tB_in])
            ar.pop()

        S.barrier()

        def PH(name):
            return PHASES is None or name in PHASES

        def NOCC(name):
            return SKIP_CC is not None and name in SKIP_CC
        if PH("ffn0a"):
            ffn(0, 0, 0)
            S.barrier()
        if PH("hm1"):
            emit_hm(lambda s: gsv[:, 0, 1, s, :], lambda s: modv[:, 0, 24:32, s], A_in, tA_in, True)
            S.dma("pool", o_shift, shcap, reads=[t_shcap], writes=[t_out])
            if not NOCC("A"):
                gath_tok(A_in, tA_in, A_out, tA_out)
            S.barrier()
        if PH("rwkv"):
            try:
                rwkv()
            except _Stop:
                ar.pop()
            S.barrier()
            if not NOCC("B"):
                gath_z(B_in, tB_in, B_out, tB_out)
            S.barrier()
        if PH("oproj1"):
            out_proj(B_out, tB_out, w_o1, lambda s: cov[:, 0, 1, s, :])
            S.barrier()
        if PH("ffn0b"):
            ffn(0, 1, 2)
            S.barrier()
        if PH("hk"):
            emit_hm(lambda s: kgs[:, s, :], lambda s: kvmod[:, 0:8, s], C_in, tC_in, False)
            if not NOCC("C"):
                gath_tok(C_in, tC_in, C_out, tC_out)
            S.barrier()
        if PH("ffn1a"):
            ffn(1, 0, 0)
            S.barrier()
        if PH("hm2"):
            emit_hm(lambda s: gsv[:, 1, 1, s, :], lambda s: modv[:, 1, 24:32, s], D_in, tD_in, False)
            if not NOCC("D"):
                gath_tok(D_in, tD_in, D_out, tD_out)
            S.barrier()
        if PH("attn"):
            try:
                attention()
            except _Stop:
                ar.pop()
            S.barrier()
            if not NOCC("E"):
                gath_z(E_in, tE_in, E_out, tE_out)
            S.barrier()
        if PH("oproj2"):
            out_proj(E_out, tE_out, w_o2, lambda s: cov[:, 1, 1, s, :])
            S.barrier()
        if PH("ffn1b"):
            ffn(1, 1, 2)
            S.barrier()
        for gi, (c0, n) in enumerate([(0, 512), (512, 512), (1024, 512), (1536, 512), (2048, 128)]):
            S.dma("pool", yT[:, c0:c0 + n].rearrange("(k p) n -> p k n", p=128), X[:, :, c0:c0 + n],
                  reads=[tX[gi]], writes=[t_out])
        S.emit(st)
        print("op stats", S.stats, "arena peak", ar.peak, "of", ARENA_WORDS, flush=True)
    return nc


_NC_CACHE = {}


def _consts():
    half = 8
    inv = (500000.0 ** (-np.arange(half, dtype=np.float32) * 2.0 / 16)).astype(np.float32)
    ropeC = np.ones((NGRP, 128, 512), np.float32)
    ropeS = np.zeros((NGRP, 128, 512), np.float32)
    for gi in range(NGRP):
        pos = np.zeros(512, np.float32)
        for (r, c0, n, dst) in grp_pieces(gi):
            if gi < 16:
                pos[dst:dst + n] = 2048 * r + c0 + np.arange(n)
            else:
                pos[dst:dst + n] = 2048 + np.arange(n)
        ang = pos[None, :].astype(np.float32) * inv[:, None]
        cs, sn = np.cos(ang).astype(np.float32), np.sin(ang).astype(np.float32)
        for blk in range(2):
            b0 = blk * 64
            ropeC[gi, b0:b0 + 8] = cs
            ropeC[gi, b0 + 8:b0 + 16] = cs
            ropeS[gi, b0:b0 + 8] = -sn
            ropeS[gi, b0 + 8:b0 + 16] = sn
    permT = np.zeros((128, 128), np.float32)
    for p in range(128):
        d = p % 64
        if d < 8:
            permT[p + 8, p] = 1.0
        elif d < 16:
            permT[p - 8, p] = 1.0
    k = np.arange(128)[:, None]
    q = np.arange(512)[None, :]
    amask = np.stack([((128 * j + k) // 64 <= q // 64) for j in range(4)]).astype(np.float32).astype(ml_dtypes.bfloat16)
    jj = np.arange(128)[:, None]
    ii = np.arange(128)[None, :]
    same = (jj // 64) == (ii // 64)
    strict = ((ii > jj) & same).astype(np.float32)
    incl = ((ii >= jj) & same).astype(np.float32)
    strict_ij = ((jj > ii) & same).astype(np.float32)
    cmask = np.concatenate([strict, incl, strict, incl, strict_ij], axis=1).astype(np.float32)
    return ropeC, ropeS, permT, amask, cmask


def _fm(v):
    v = np.asarray(v, np.float32)
    lead = v.shape[:-1]
    x = v.reshape(lead + (8, 128))
    x = np.moveaxis(x, -1, 0)
    return np.ascontiguousarray(x)


def kernel(**inp):
    f32 = np.float32
    I = {k: np.asarray(v) for k, v in inp.items()}
    if "nc" not in _NC_CACHE:
        _NC_CACHE["nc"] = build_program()
    nc = _NC_CACHE["nc"]
    ropeC, ropeS, permT, amask, cmask = _consts()
    in_maps = []
    for c in range(8):
        g, j = c // 4, c % 4
        s0 = 8 * g + 2 * j
        m = {}
        m["xT"] = np.ascontiguousarray(np.concatenate(
            [I["x_prompt"][g, 2048 * j:2048 * (j + 1)].T, I["x_sample"][s0].T, I["x_sample"][s0 + 1].T], axis=1))
        cv = np.stack([I["c_prompt"][g], I["c_sample"][s0], I["c_sample"][s0 + 1]], 0)
        m["cT"] = np.ascontiguousarray(cv.reshape(3, 8, 128).transpose(2, 1, 0))
        m["ada_w"] = I["ada_w"]
        m["ada_bT"] = np.ascontiguousarray(I["ada_b"].reshape(2, 72, 128).transpose(2, 0, 1))
        m["normgT"] = np.ascontiguousarray(I["norm_g"].reshape(2, 6, 8, 128).transpose(3, 0, 1, 2))
        m["ffn_w_in"] = I["ffn_w_in"]
        m["ffn_w_out"] = I["ffn_w_out"]
        m["kv_ada_w"] = I["kv_ada_w"]
        m["kv_ada_bT"] = np.ascontiguousarray(I["kv_ada_b"].reshape(16, 128).T)
        m["kv_normgT"] = np.ascontiguousarray(I["kv_norm_g"].reshape(8, 128).T)
        m["muT"] = np.ascontiguousarray(I["rwkv_mu"][0].reshape(6, 8, 128).transpose(2, 0, 1))
        cols = slice(256 * j, 256 * j + 256)
        m["w_rkv"] = np.ascontiguousarray(I["rwkv_w_rkv"][0][:, :, cols])
        m["w_l1"] = np.ascontiguousarray(np.concatenate([I["rwkv_w1"][0], I["rwkv_a1"][0], I["rwkv_g1"][0]], axis=1))
        m["w_w2"] = np.ascontiguousarray(I["rwkv_w2"][0][:, cols])
        m["w_a2"] = np.ascontiguousarray(I["rwkv_a2"][0][:, cols])
        m["w_g2"] = np.ascontiguousarray(I["rwkv_g2"][0][:, cols])
        vecs = [I["rwkv_w0"][0][cols], I["rwkv_a0"][0][cols], I["rwkv_k_k"][0][cols], I["rwkv_k_a"][0][cols],
                I["rwkv_ln_w"][0][cols], I["rwkv_ln_b"][0][cols], I["rwkv_r_k"][0].reshape(-1)[cols]]
        m["rvecT"] = np.ascontiguousarray(np.stack(vecs, 0).reshape(7, 2, 128).transpose(2, 0, 1))
        m["shiftT"] = np.ascontiguousarray(I["state_shift"][0, 8 * g:8 * g + 8, 0, :].reshape(8, 8, 128).transpose(2, 1, 0))
        sw = I["state_wkv"][0, 8 * g:8 * g + 8, 4 * j:4 * j + 4]
        sw = sw.reshape(8, 2, 2, 64, 64)
        m["wkv0"] = np.ascontiguousarray(sw.transpose(2, 4, 1, 0, 3).reshape(128, 2, 8, 64))
        m["w_o1"] = I["rwkv_w_o"][0]
        m["kvwk"] = np.ascontiguousarray(I["kv_w"][:, cols])
        m["kvwv"] = np.ascontiguousarray(I["kv_w"][:, 1024 + 256 * j:1024 + 256 * j + 256])
        m["wq"] = np.ascontiguousarray(I["diff_w_q"][0][:, cols])
        m["w_o2"] = I["diff_w_o"][0]
        m["cache_k"] = np.ascontiguousarray(I["cache_k"][8 * g:8 * g + 8, :, 2 * j:2 * j + 2].reshape(8, 2048, 256))
        m["cache_v"] = np.ascontiguousarray(I["cache_v"][8 * g:8 * g + 8, :, 2 * j:2 * j + 2].reshape(8, 2048, 256))
        m["lamb"] = np.ascontiguousarray(I["diff_lambda"][0].reshape(1, 256))
        m["sublnT"] = np.ascontiguousarray(I["diff_subln_g"][0].reshape(128, 1))
        m["ropeC"] = ropeC; m["ropeS"] = ropeS; m["permT"] = permT; m["amask"] = amask; m["cmask"] = cmask
        sel = np.zeros((128, 4), f32); sel[:, j] = 1.0
        m["selT"] = sel
        in_maps.append({k: (v if v.dtype != np.float64 else v.astype(f32)) for k, v in m.items()})
    res = run_bass_kernel_spmd(nc, in_maps, core_ids=list(range(8)))
    R = res.results
    y_prompt = np.zeros((2, 8192, D), f32); y_sample = np.zeros((16, 64, D), f32)
    wkv_prompt = np.zeros((1, 2, 16, 64, 64), f32); wkv_sample = np.zeros((1, 16, 16, 64, 64), f32)
    shift_prompt = np.zeros((1, 2, 1, D), f32); shift_sample = np.zeros((1, 16, 1, D), f32)
    k_prompt = np.zeros((2, 8192, 8, 2, 64), f32); v_prompt = np.zeros((2, 8192, 8, 128), f32)
    k_sample = np.zeros((16, 64, 8, 2, 64), f32); v_sample = np.zeros((16, 64, 8, 128), f32)
    for c in range(8):
        g, j = c // 4, c % 4
        s0 = 8 * g + 2 * j
        r = R[c]
        yT = np.asarray(r["yT"])
        y_prompt[g, 2048 * j:2048 * (j + 1)] = yT[:, :2048].T
        y_sample[s0] = yT[:, 2048:2112].T
        y_sample[s0 + 1] = yT[:, 2112:2176].T
        ow = np.asarray(r["o_wkv"]).reshape(2, 64, 9, 2, 64)
        st = ow.transpose(2, 3, 0, 4, 1)
        wkv_prompt[0, g, 4 * j:4 * j + 4] = st[0].reshape(4, 64, 64)
        for s in range(8):
            wkv_sample[0, 8 * g + s, 4 * j:4 * j + 4] = st[1 + s].reshape(4, 64, 64)
        osf = np.asarray(r["o_shift"])
        sh = osf.transpose(2, 1, 0).reshape(3, D)
        if j == 3:
            shift_prompt[0, g, 0] = sh[0]
        shift_sample[0, s0, 0] = sh[1]
        shift_sample[0, s0 + 1, 0] = sh[2]
        ok = np.asarray(r["o_k"]).reshape(2, 2, 64, 4, L)
        ov = np.asarray(r["o_v"]).reshape(4, L, 2, 128)
        for rk in range(4):
            k_prompt[g, 2048 * rk:2048 * (rk + 1), 2 * j:2 * j + 2] = ok[:, :, :, rk, :2048].transpose(3, 0, 1, 2)
            v_prompt[g, 2048 * rk:2048 * (rk + 1), 2 * j:2 * j + 2] = ov[rk, :2048]
            for p in range(2):
                sidx = 8 * g + 2 * rk + p
                k_sample[sidx, :, 2 * j:2 * j + 2] = ok[:, :, :, rk, 2048 + 64 * p:2048 + 64 * p + 64].transpose(3, 0, 1, 2)
                v_sample[sidx, :, 2 * j:2 * j + 2] = ov[rk, 2048 + 64 * p:2048 + 64 * p + 64]
    return (y_prompt, y_sample, wkv_prompt, shift_prompt, k_prompt, v_prompt,
            wkv_sample, shift_sample, k_sample, v_sample)
```

```python
import numpy as np
import concourse.bass as bass
import concourse.mybir as mybir

F32 = mybir.dt.float32
BF16 = mybir.dt.bfloat16
ALU = mybir.AluOpType
AF = mybir.ActivationFunctionType
AX = mybir.AxisListType

SEM_EPOCH = 30000


class T:
    __slots__ = ("name", "lw", "rd", "excl")

    def __init__(self, name="", excl=False):
        self.name = name
        self.excl = excl
        self.lw = None
        self.rd = []


class Sched:
    CE = ("pe", "act", "dve", "pool", "sp")

    def __init__(self, nc, nsp=8, npool=4):
        self.nc = nc
        self.ops = {e: [] for e in self.CE}
        self.known = {e: {f: -1 for f in self.CE} for e in self.CE}
        self.clock = {e: [] for e in self.CE}
        self.dwaited = {e: set() for e in self.CE}
        self.nq = {"sp": nsp, "pool": npool}
        self.dq = {"sp": [], "pool": []}
        self.ndma = 0
        self.dma_info = {}
        self.ncc = 0
        self.pending_barrier = {e: None for e in self.CE}

    def _deps(self, reads, writes, eng=None):
        deps = []
        for t in reads:
            if t.lw is not None:
                deps.append((t.lw, "raw"))
            if t.excl:
                for r in t.rd:
                    if r[0] == "c" and r[1] != eng:
                        deps.append((r, "raw"))
        for t in writes:
            if t.lw is not None:
                deps.append((t.lw, "waw"))
            for r in t.rd:
                deps.append((r, "war"))
        return deps

    def _add_wait(self, eng, waits, ev, kind):
        if ev[0] == "d":
            if ev[1] in self.dwaited[eng]:
                return
            self.dwaited[eng].add(ev[1])
            waits.append(ev)
            return
        if ev[0] == "cc":
            if ev in self.dwaited[eng]:
                return
            self.dwaited[eng].add(ev)
            waits.append(ev)
            return
        _, f, j = ev
        if f == eng:
            if eng in ("pe", "sp"):
                return
            if self.known[eng][f] >= j:
                return
        elif self.known[eng][f] >= j:
            return
        waits.append(ev)
        snap = self.clock[f][j]
        k = self.known[eng]
        for g, v in snap.items():
            if v > k[g]:
                k[g] = v
        if j > k[f]:
            k[f] = j

    def _commit(self, eng, ev, reads, writes):
        for t in reads:
            t.rd.append(ev)
        for t in writes:
            t.lw = ev
            t.rd = []

    def _barrier_waits(self, eng, waits):
        b = self.pending_barrier[eng]
        if b is None:
            return
        self.pending_barrier[eng] = None
        for ev in b:
            self._add_wait(eng, waits, ev, "raw")

    def barrier(self):
        evs = []
        for e in self.CE:
            for j in range(len(self.ops[e]) - 1, -1, -1):
                if self.ops[e][j]["kind"] == "c":
                    evs.append(("c", e, j))
                    break
        for c in range(self.ncc):
            evs.append(("cc", c))
        for q in self.dq:
            for d in self.dq[q][-self.nq[q]:]:
                evs.append(("d", d))
        for e in self.CE:
            self.pending_barrier[e] = list(evs)

    def op(self, eng, fn, reads=(), writes=()):
        waits = []
        self._barrier_waits(eng, waits)
        for ev, kind in self._deps(reads, writes, eng):
            self._add_wait(eng, waits, ev, kind)
        j = len(self.ops[eng])
        self.ops[eng].append(dict(fn=fn, waits=waits, kind="c"))
        self.clock[eng].append(dict(self.known[eng]))
        ev = ("c", eng, j)
        self._commit(eng, ev, reads, writes)
        return ev

    def dma(self, q, out_ap, in_ap, reads=(), writes=(), **kw):
        waits = []
        self._barrier_waits(q, waits)
        for ev, kind in self._deps(reads, writes):
            self._add_wait(q, waits, ev, "raw")
        k = len(self.dq[q])
        n = self.nq[q]
        if k >= n:
            self._add_wait(q, waits, ("d", self.dq[q][k - n]), "raw")
        did = self.ndma
        self.ndma += 1
        self.dq[q].append(did)
        self.dma_info[did] = (q, k)
        j = len(self.ops[q])
        self.ops[q].append(dict(fn=None, waits=waits, kind="d", out=out_ap, in_=in_ap, did=did, kw=kw))
        self.clock[q].append(dict(self.known[q]))
        ev = ("d", did)
        self._commit(q, ev, reads, writes)
        return ev

    def collective(self, kind, groups, in_ap, out_ap, reads=(), writes=()):
        q = "pool"
        waits = []
        self._barrier_waits(q, waits)
        for ev, kd in self._deps(reads, writes):
            self._add_wait(q, waits, ev, "raw")
        cid = self.ncc
        self.ncc += 1
        self.ops[q].append(dict(fn=None, waits=waits, kind="cc", cckind=kind, groups=groups,
                                in_=in_ap, out=out_ap, cid=cid))
        self.clock[q].append(dict(self.known[q]))
        ev = ("cc", cid)
        self._commit(q, ev, reads, writes)
        return ev

    def emit(self, stack):
        nc = self.nc
        marked = {e: set() for e in self.CE}
        for e in self.CE:
            for o in self.ops[e]:
                for w in o["waits"]:
                    if w[0] == "c":
                        marked[w[1]].add(w[2])
        rank = {}
        csem = {}
        for e in self.CE:
            ms = sorted(marked[e])
            rank[e] = {j: i for i, j in enumerate(ms)}
            nep = (len(ms) + SEM_EPOCH - 1) // SEM_EPOCH
            csem[e] = [stack.enter_context(nc.semaphore(f"c_{e}_{i}")) for i in range(max(nep, 1))]
        dsem = {q: [stack.enter_context(nc.semaphore(f"d_{q}_{i}")) for i in range(self.nq[q])]
                for q in self.dq}
        ccsem = [stack.enter_context(nc.semaphore(f"cc_{i}")) for i in range(self.ncc)]
        self.stats = {e: (len(self.ops[e]), len(marked[e])) for e in self.CE}

        def waitspec(w):
            if w[0] == "c":
                r = rank[w[1]][w[2]]
                return csem[w[1]][r // SEM_EPOCH], (r % SEM_EPOCH) + 1
            if w[0] == "d":
                q, k = self.dma_info[w[1]]
                n = self.nq[q]
                return dsem[q][k % n], 16 * (k // n + 1)
            if w[0] == "cc":
                return ccsem[w[1]], 1
            raise ValueError(w)

        def run(engname, e):
            for j, o in enumerate(self.ops[engname]):
                for w in o["waits"]:
                    s, v = waitspec(w)
                    e.wait_ge(s, v)
                if o["kind"] == "c":
                    ins = o["fn"](e)
                    if j in rank[engname]:
                        r = rank[engname][j]
                        ins.then_inc(csem[engname][r // SEM_EPOCH], 1)
                elif o["kind"] == "d":
                    q, k = self.dma_info[o["did"]]
                    n = self.nq[q]
                    e.dma_start(out=o["out"], in_=o["in_"], **o["kw"]).then_inc(dsem[q][k % n], 16)
                elif o["kind"] == "cc":
                    e.collective_compute(o["cckind"], ALU.bypass, replica_groups=o["groups"],
                                         ins=[o["in_"]], outs=[o["out"]]).then_inc(ccsem[o["cid"]], 1)
            if engname in self.dq:
                q = engname
                for d in self.dq[q][-self.nq[q]:]:
                    s, v = waitspec(("d", d))
                    e.wait_ge(s, v)
            if engname == "pool":
                for c in range(self.ncc):
                    e.wait_ge(ccsem[c], 1)

        with nc.Block() as block:
            @block.tensor
            def _(e):
                run("pe", e)

            @block.scalar
            def _(e):
                run("act", e)

            @block.vector
            def _(e):
                run("dve", e)

            @block.gpsimd
            def _(e):
                run("pool", e)

            @block.sync
            def _(e):
                run("sp", e)


class Arena:
    def __init__(self, ap_f32, n):
        self.ap = ap_f32
        self.n = n
        self.off = 0
        self.marks = []

    def push(self):
        self.marks.append(self.off)

    def pop(self):
        self.off = self.marks.pop()

    def f32(self, n):
        a = self.ap[:, self.off:self.off + n]
        self.off += n
        self.peak = max(getattr(self, "peak", 0), self.off)
        assert self.off <= self.n, f"arena overflow {self.off} > {self.n}"
        return a

    def bf16(self, n):
        m = (n + 1) // 2
        a = self.ap[:, self.off:self.off + m].bitcast(BF16)
        self.off += m
        self.peak = max(getattr(self, "peak", 0), self.off)
        assert self.off <= self.n, f"arena overflow {self.off} > {self.n}"
        return a[:, 0:n]

import math
from contextlib import ExitStack
import ml_dtypes
from concourse.bass_utils import run_bass_kernel_spmd

D = 1024
KC = 8
L = 2176
G4 = 8704
DFF = 2816
NHC = 22
NGRP = 17
EPS = 1e-6
GN_EPS = 64e-5
GROUPS = [[0, 1, 2, 3], [4, 5, 6, 7]]
LAM_INIT = 0.8 - 0.6 * math.exp(-0.3 * 1)
ATT_SCALE = 0.125
ARENA_WORDS = 53184
STOP_AFTER = 99
PHASES = None
RWKV_STOP = 0
ATT_G = 0


class _Stop(Exception):
    pass


def _ckpt(level):
    if RWKV_STOP == level:
        raise _Stop()
SKIP_CC = None
DEBUG = False


def _mm(S, out, lhsT, rhs, start, stop, reads, writes):
    S.op("pe", lambda e: e.matmul(out, lhsT=lhsT, rhs=rhs, start=start, stop=stop), reads, writes)


def _tr(S, out, in_, ident, reads, writes):
    S.op("pe", lambda e: e.transpose(out, in_, ident), reads, writes)


def _act(S, out, in_, func, reads, writes, bias=None, scale=None):
    kw = {}
    if bias is not None:
        kw["bias"] = bias
    if scale is not None:
        kw["scale"] = scale
    S.op("act", lambda e: e.activation(out=out, in_=in_, func=func, **kw), reads, writes)


def _tt(S, eng, out, in0, in1, op, reads, writes):
    S.op(eng, lambda e: e.tensor_tensor(out=out, in0=in0, in1=in1, op=op), reads, writes)


def _ts(S, eng, out, in0, s1, s2, op0, op1, reads, writes):
    if s2 is None:
        S.op(eng, lambda e: e.tensor_scalar(out=out, in0=in0, scalar1=s1, scalar2=None, op0=op0), reads, writes)
    else:
        S.op(eng, lambda e: e.tensor_scalar(out=out, in0=in0, scalar1=s1, scalar2=s2, op0=op0, op1=op1), reads, writes)


def _stt(S, out, in0, scalar, in1, op0, op1, reads, writes):
    S.op("dve", lambda e: e.scalar_tensor_tensor(out=out, in0=in0, scalar=scalar, in1=in1, op0=op0, op1=op1),
         reads, writes)


def _cp(S, eng, out, in_, reads, writes):
    if eng == "act":
        S.op("act", lambda e: e.activation(out=out, in_=in_, func=AF.Copy), reads, writes)
    else:
        S.op(eng, lambda e: e.tensor_copy(out=out, in_=in_), reads, writes)


def _rsqrt(S, out, in_, eps, reads, t_out, scale=1.0):
    S.op("act", lambda e: e.activation(out=out, in_=in_, func=AF.Sqrt, bias=eps, scale=scale), reads, [t_out])
    S.op("dve", lambda e: e.reciprocal(out=out, in_=out), [t_out], [t_out])


def _memset(S, eng, ap, val, writes):
    S.op(eng, lambda e: e.memset(ap, val), (), writes)


def grp_pieces(gi):
    if gi < 16:
        return [(gi // 4, (gi % 4) * 512, 512, 0)]
    return [(s // 2, 2048 + 64 * (s % 2), 64, 64 * s) for s in range(8)]


def build_program():
    nc = bass.Bass("TRN2", target_bir_lowering=False)

    def din(name, shape, dt=F32):
        return nc.dram_tensor(name, list(shape), dt, kind="ExternalInput").ap()

    def dout(name, shape, dt=F32):
        return nc.dram_tensor(name, list(shape), dt, kind="ExternalOutput").ap()

    def dscr(name, shape, dt=BF16):
        return nc.dram_tensor(name, list(shape), dt).ap()

    xT = din("xT", [D, L])
    cT = din("cT", [128, KC, 3])
    ada_w = din("ada_w", [2, D, 9 * D])
    ada_bT = din("ada_bT", [128, 2, 72])
    normgT = din("normgT", [128, 2, 6, KC])
    ffn_w_in = din("ffn_w_in", [2, 2, D, 2 * DFF])
    ffn_w_out = din("ffn_w_out", [2, 2, DFF, D])
    kv_ada_w = din("kv_ada_w", [D, 2 * D])
    kv_ada_bT = din("kv_ada_bT", [128, 16])
    kv_normgT = din("kv_normgT", [128, KC])
    muT = din("muT", [128, 6, KC])
    w_rkv = din("w_rkv", [3, D, 256])
    w_l1 = din("w_l1", [D, 256])
    w_w2 = din("w_w2", [64, 256])
    w_a2 = din("w_a2", [64, 256])
    w_g2 = din("w_g2", [128, 256])
    rvecT = din("rvecT", [128, 7, 2])
    shiftT = din("shiftT", [128, KC, 8])
    wkv0 = din("wkv0", [128, 2, 8, 64])
    w_o1 = din("w_o1", [D, D])
    kvwk = din("kvwk", [D, 256])
    kvwv = din("kvwv", [D, 256])
    wq = din("wq", [D, 256])
    w_o2 = din("w_o2", [D, D])
    cache_k = din("cache_k", [8, 2048, 256])
    cache_v = din("cache_v", [8, 2048, 256])
    lamb = din("lamb", [1, 256])
    sublnT = din("sublnT", [128, 1])
    ropeC = din("ropeC", [NGRP, 128, 512])
    ropeS = din("ropeS", [NGRP, 128, 512])
    permT = din("permT", [128, 128])
    amask = din("amask", [4, 128, 512], BF16)
    selT = din("selT", [128, 4])
    cmask = din("cmask", [128, 640])
    yT = dout("yT", [D, L])
    o_wkv = dout("o_wkv", [128, 9, 2, 64])
    o_shift = dout("o_shift", [128, KC, 3])
    o_k = dout("o_k", [256, G4])
    o_v = dout("o_v", [G4, 256])
    A_in = dscr("A_in", [D, L]); A_out = dscr("A_out", [4 * D, L])
    B_in = dscr("B_in", [8 * 256, 1088]); B_out = dscr("B_out", [8 * D, 1088])
    C_in = dscr("C_in", [D, L]); C_out = dscr("C_out", [4 * D, L])
    D_in = dscr("D_in", [D, L]); D_out = dscr("D_out", [4 * D, L])
    E_in = dscr("E_in", [8 * 256, 1088]); E_out = dscr("E_out", [8 * D, 1088])
    tA_in, tA_out, tB_in, tB_out, tC_in, tC_out, tD_in, tD_out, tE_in, tE_out = [T(f"scr{i}") for i in range(10)]
    t_out = T("outs")

    with ExitStack() as st:
        arena_t = st.enter_context(nc.sbuf_tensor("arena", [128, ARENA_WORDS], F32))
        ps = [st.enter_context(nc.psum_tensor(f"ps{i}", [128, 512], F32))[:] for i in range(8)]
        tps = [T(f"ps{i}", excl=True) for i in range(8)]
        ar = Arena(arena_t[:], ARENA_WORDS)
        S = Sched(nc, nsp=8, npool=6)

        def gath_tok(buf_in, t_in, buf_out, t_out_):
            for kc in range(KC):
                S.collective("AllGather", GROUPS, buf_in[kc * 128:(kc + 1) * 128, :],
                             buf_out[kc * 512:(kc + 1) * 512, :], reads=[t_in], writes=[t_out_])

        def tok_view(buf_out, r, c0, n):
            return buf_out.rearrange("(k r p) n -> r p k n", k=KC, r=4)[r][:, :, c0:c0 + n]

        def gath_z(buf_in, t_in, buf_out, t_out_):
            for ch in range(8):
                S.collective("AllGather", GROUPS, buf_in[ch * 256:(ch + 1) * 256, :],
                             buf_out[ch * D:(ch + 1) * D, :], reads=[t_in], writes=[t_out_])

        def zsplit(g0, n):
            out = []
            rel = 0
            while n > 0:
                ch, off = g0 // 1088, g0 % 1088
                m = min(n, 1088 - off)
                out.append((ch, off, m, rel))
                g0 += m; n -= m; rel += m
            return out

        X = ar.f32(KC * L).rearrange("p (k n) -> p k n", k=KC)
        tX = [T(f"X{i}") for i in range(5)]

        def xg(col0):
            return tX[min(col0 // 512, 4)]
        ones_bf = ar.bf16(128); ones_f = ar.f32(128); ident = ar.f32(128); blk_f = ar.f32(128)
        tconst = T("const")
        modv = ar.f32(2 * 72 * 3).rearrange("p (l f s) -> p l f s", l=2, f=72)
        kvmod = ar.f32(16 * 3).rearrange("p (f s) -> p f s", f=16)
        normg = ar.f32(2 * 6 * KC).rearrange("p (l i k) -> p l i k", l=2, i=6)
        kvng = ar.f32(KC)
        gsv = ar.f32(2 * 3 * 3 * KC).rearrange("p (l w s k) -> p l w s k", l=2, w=3, s=3)
        cov = ar.f32(2 * 3 * 3 * KC).rearrange("p (l w s k) -> p l w s k", l=2, w=3, s=3)
        kgs = ar.f32(3 * KC).rearrange("p (s k) -> p s k", s=3)
        tmod = T("mod")
        _memset(S, "dve", ones_bf, 1.0, [tconst])
        _memset(S, "dve", ones_f, 1.0, [tconst])
        _memset(S, "pool", ident, 0.0, [tconst])
        S.op("pool", lambda e: e.affine_select(out=ident, in_=ident, pattern=[[-1, 128]], compare_op=ALU.not_equal,
                                               fill=1.0, base=0, channel_multiplier=1), [tconst], [tconst])
        _memset(S, "dve", blk_f, 0.0, [tconst])
        _memset(S, "dve", blk_f[0:64, 0:64], 1.0, [tconst])
        _memset(S, "dve", blk_f[64:128, 64:128], 1.0, [tconst])

        shcap = ar.f32(KC * 3).rearrange("p (k s) -> p k s", k=KC)
        t_shcap = T("shcap")
        sel = ar.f32(4)
        t_sel = T("sel")
        S.dma("sp", sel, selT, writes=[t_sel])
        cs = ar.f32(KC * 3).rearrange("p (k s) -> p k s", k=KC)
        common_start = ar.off
        NSLOT = 2
        WS = 2816
        wst_f = [ar.f32(WS) for _ in range(NSLOT)]
        wst_b = [ar.bf16(WS) for _ in range(NSLOT)]
        t_wf = [T(f"wf{i}") for i in range(NSLOT)]
        t_wb = [T(f"wb{i}") for i in range(NSLOT)]
        wctr = [0]

        def load_w(dram_view, kdim, n, cast_eng=None):
            i = wctr[0] % NSLOT
            wctr[0] += 1
            f = wst_f[i][:, 0:kdim * n].rearrange("p (k n) -> p k n", k=kdim)
            b = wst_b[i][:, 0:kdim * n].rearrange("p (k n) -> p k n", k=kdim)
            S.dma("sp", f, dram_view, writes=[t_wf[i]])
            eng = cast_eng or ("pool" if (wctr[0] % 2 == 0) else "act")
            _cp(S, eng, b, f, [t_wf[i]], [t_wb[i]])
            return b, t_wb[i]

        rstd = [ar.f32(512) for _ in range(2)]
        t_rstd = [T("rstd0"), T("rstd1")]
        sqb = [ar.bf16(512) for _ in range(3)]
        t_sq = [T(f"sq{i}") for i in range(3)]
        tmpf = [ar.f32(512) for _ in range(3)]
        t_tmp = [T(f"tmp{i}") for i in range(3)]
        ctr = {"sq": 0, "tmp": 0, "rstd": 0}

        def nxt(kind, n):
            i = ctr[kind] % n
            ctr[kind] += 1
            return i

        for gi, (c0, n) in enumerate([(0, 512), (512, 512), (1024, 512), (1536, 512), (2048, 128)]):
            S.dma("sp", X[:, :, c0:c0 + n], xT[:, c0:c0 + n].rearrange("(k p) n -> p k n", p=128), writes=[tX[gi]])
        t_cs = T("cs")
        S.dma("sp", cs, cT, writes=[t_cs])
        S.dma("sp", normg, normgT, writes=[tmod])
        S.dma("sp", kvng, kv_normgT, writes=[tmod])
        _act(S, cs, cs, AF.Silu, [t_cs], [t_cs])
        ar.push()
        NAST = 4
        ast = [ar.f32(KC * 256).rearrange("p (k n) -> p k n", k=KC) for _ in range(NAST)]
        t_ast = [T(f"ast{i}") for i in range(NAST)]
        bias_tmp = ar.f32(2 * 72 + 16)
        S.dma("sp", bias_tmp[:, 0:144].rearrange("p (l f) -> p l f", l=2), ada_bT, writes=[tmod])
        S.dma("sp", bias_tmp[:, 144:160], kv_ada_bT, writes=[tmod])
        ai = 0
        jobs = [(ada_w[0], 36, lambda f: (modv[:, 0, f, :], bias_tmp[:, f:f + 1])),
                (ada_w[1], 36, lambda f: (modv[:, 1, f, :], bias_tmp[:, 72 + f:72 + f + 1])),
                (kv_ada_w, 8, lambda f: (kvmod[:, f, :], bias_tmp[:, 144 + f:144 + f + 1]))]
        for wsrc, ntile, dst in jobs:
            for ti in range(ntile):
                sl = ai % NAST
                ai += 1
                S.dma("sp", ast[sl], wsrc[:, ti * 256:(ti + 1) * 256].rearrange("(k p) n -> p k n", p=128),
                      writes=[t_ast[sl]])
                pb = ai % 2
                for fi in range(2):
                    for kc in range(KC):
                        _mm(S, ps[pb][:, fi * 4:fi * 4 + 3], ast[sl][:, kc, fi * 128:(fi + 1) * 128], cs[:, kc, :],
                            kc == 0, kc == KC - 1, [t_ast[sl], t_cs], [tps[pb]])
                for fi in range(2):
                    o, b = dst(ti * 2 + fi)
                    _ts(S, "dve", o, ps[pb][:, fi * 4:fi * 4 + 3], b, None, ALU.add, None, [tps[pb], tmod], [tmod])
        ar.pop()
        SQD = 32.0
        for l in range(2):
            for w in range(3):
                for s in range(3):
                    sc_ap = modv[:, l, (3 * w + 1) * 8:(3 * w + 2) * 8, s]
                    g_ap = modv[:, l, (3 * w + 2) * 8:(3 * w + 3) * 8, s]
                    _stt(S, gsv[:, l, w, s, :], sc_ap, 1.0, normg[:, l, 2 * w, :], ALU.add, ALU.mult, [tmod], [tmod])
                    _ts(S, "dve", gsv[:, l, w, s, :], gsv[:, l, w, s, :], SQD, None, ALU.mult, None, [tmod], [tmod])
                    fct = (1.0 if w == 1 else 0.5) * SQD
                    _stt(S, cov[:, l, w, s, :], g_ap, fct, normg[:, l, 2 * w + 1, :], ALU.mult, ALU.mult, [tmod], [tmod])
        for s in range(3):
            _stt(S, kgs[:, s, :], kvmod[:, 8:16, s], 1.0, kvng, ALU.add, ALU.mult, [tmod], [tmod])
            _ts(S, "dve", kgs[:, s, :], kgs[:, s, :], SQD, None, ALU.mult, None, [tmod], [tmod])

        def seq_of(c0):
            return 0 if c0 < 2048 else 1 + (c0 - 2048) // 64

        def rms_rstd(src3, c0, n, src_reads, psb):
            ri = nxt("rstd", 2)
            for kc in range(KC):
                qi = nxt("sq", 3)
                _act(S, sqb[qi][:, 0:n], src3[:, kc, c0:c0 + n], AF.Square, src_reads, [t_sq[qi]])
                _mm(S, ps[psb][:, 0:n], ones_bf, sqb[qi][:, 0:n], kc == 0, kc == KC - 1, [t_sq[qi], tconst], [tps[psb]])
            _rsqrt(S, rstd[ri][:, 0:n], ps[psb][:, 0:n], EPS * D, [tps[psb]], t_rstd[ri])
            return ri

        def modulate(c0, n, gs_ap, sh_ap, dst3, dcol0, dst_t, psb, cap=None):
            ri = rms_rstd(X, c0, n, [xg(c0)], psb)
            for kc in range(KC):
                ti = nxt("tmp", 3)
                _stt(S, tmpf[ti][:, 0:n], X[:, kc, c0:c0 + n], gs_ap[:, kc:kc + 1], rstd[ri][:, 0:n], ALU.mult, ALU.mult,
                     [xg(c0), t_rstd[ri], tmod], [t_tmp[ti]])
                _act(S, dst3[:, kc, dcol0:dcol0 + n], tmpf[ti][:, 0:n], AF.Identity, [t_tmp[ti], tmod], [dst_t],
                     bias=sh_ap[:, kc:kc + 1])
                if cap is not None:
                    for (lc, oc) in cap:
                        _act(S, shcap[:, kc, oc:oc + 1], tmpf[ti][:, lc:lc + 1], AF.Identity, [t_tmp[ti], tmod], [t_shcap],
                             bias=sh_ap[:, kc:kc + 1])

        def postnorm_add(src3, src_t, scol0, c0, n, co_ap, psb):
            ri = rms_rstd(src3, scol0, n, [src_t], psb)
            for kc in range(KC):
                ti = nxt("tmp", 3)
                _stt(S, tmpf[ti][:, 0:n], src3[:, kc, scol0:scol0 + n], co_ap[:, kc:kc + 1], rstd[ri][:, 0:n],
                     ALU.mult, ALU.mult, [src_t, t_rstd[ri], tmod], [t_tmp[ti]])
                _tt(S, "pool", X[:, kc, c0:c0 + n], X[:, kc, c0:c0 + n], tmpf[ti][:, 0:n], ALU.add,
                    [t_tmp[ti], xg(c0)], [xg(c0)])


        def ffn(l, i, w):
            ar.push()
            NP = 1088
            hF = ar.f32(KC * NP)
            hb = hF.bitcast(BF16)[:, 0:KC * NP].rearrange("p (k n) -> p k n", k=KC)
            Fo = hF.rearrange("p (k n) -> p k n", k=KC)
            t_hF = T("hF")
            hid = ar.bf16(NHC * NP).rearrange("p (k n) -> p k n", k=NHC)
            t_hid = [T("hid0"), T("hid1"), T("hid2")]
            sg = [ar.bf16(512) for _ in range(2)]
            t_sg = [T("sg0"), T("sg1")]
            for p in range(2):
                cgs = [(p * 1024, 512, 0), (p * 1024 + 512, 512, 512), (2048 + 64 * p, 64, 1024)]
                for (c0, n, hc0) in cgs:
                    s = seq_of(c0)
                    modulate(c0, n, gsv[:, l, w, s, :], modv[:, l, 3 * w * 8:(3 * w + 1) * 8, s], hb, hc0, t_hF, 6)
                for hc in range(NHC):
                    wv = ffn_w_in[l, i].rearrange("(k p) (u n) -> p k u n", p=128, u=2)[:, :, :, hc * 128:(hc + 1) * 128]
                    si = wctr[0] % NSLOT
                    wctr[0] += 1
                    f = wst_f[si][:, 0:KC * 256].rearrange("p (k u n) -> p k u n", k=KC, u=2)
                    b = wst_b[si][:, 0:KC * 256].rearrange("p (k u n) -> p k u n", k=KC, u=2)
                    for u in range(2):
                        S.dma("sp", f[:, :, u, :], wv[:, :, u, :], writes=[t_wf[si]])
                    _cp(S, "pool" if hc % 2 == 0 else "act", b, f, [t_wf[si]], [t_wb[si]])
                    for ci, (c0, n, hc0) in enumerate(cgs):
                        pg, pu = 2 * ci, 2 * ci + 1
                        for kc in range(KC):
                            _mm(S, ps[pg][:, 0:n], b[:, kc, 0, :], hb[:, kc, hc0:hc0 + n], kc == 0, kc == KC - 1,
                                [t_wb[si], t_hF], [tps[pg]])
                        for kc in range(KC):
                            _mm(S, ps[pu][:, 0:n], b[:, kc, 1, :], hb[:, kc, hc0:hc0 + n], kc == 0, kc == KC - 1,
                                [t_wb[si], t_hF], [tps[pu]])
                        gi2 = (hc * 3 + ci) % 2
                        _act(S, sg[gi2][:, 0:n], ps[pg][:, 0:n], AF.Silu, [tps[pg]], [t_sg[gi2]])
                        _tt(S, "dve", hid[:, hc, hc0:hc0 + n], sg[gi2][:, 0:n], ps[pu][:, 0:n], ALU.mult,
                            [t_sg[gi2], tps[pu]], [t_hid[ci]])
                for dc in range(KC):
                    wv = ffn_w_out[l, i][:, dc * 128:(dc + 1) * 128].rearrange("(k p) n -> p k n", p=128)
                    b, tb = load_w(wv, NHC, 128, cast_eng="pool" if dc % 2 == 0 else "act")
                    for ci, (c0, n, hc0) in enumerate(cgs):
                        pb = ci
                        for hc in range(NHC):
                            _mm(S, ps[pb][:, 0:n], b[:, hc, :], hid[:, hc, hc0:hc0 + n], hc == 0, hc == NHC - 1,
                                [tb, t_hid[ci]], [tps[pb]])
                        _cp(S, "act" if ci % 2 == 0 else "dve", Fo[:, dc, hc0:hc0 + n], ps[pb][:, 0:n], [tps[pb]], [t_hF])
                for (c0, n, hc0) in cgs:
                    s = seq_of(c0)
                    postnorm_add(Fo, t_hF, hc0, c0, n, cov[:, l, w, s, :], 7)
            ar.pop()

        def emit_hm(gs_sel, sh_sel, dst_in, t_dst, capture):
            ar.push()
            hb = [ar.bf16(KC * 512).rearrange("p (k n) -> p k n", k=KC) for _ in range(2)]
            t_hb = [T("hmb0"), T("hmb1")]
            for gi, (c0, n) in enumerate([(0, 512), (512, 512), (1024, 512), (1536, 512), (2048, 64), (2112, 64)]):
                s = seq_of(c0)
                bi = gi % 2
                cap = None
                if capture:
                    if c0 == 1536:
                        cap = [(511, 0)]
                    elif c0 >= 2048:
                        cap = [(63, 1 + (c0 - 2048) // 64)]
                modulate(c0, n, gs_sel(s), sh_sel(s), hb[bi], 0, t_hb[bi], 6, cap=cap)
                S.dma("pool", dst_in[:, c0:c0 + n].rearrange("(k p) n -> p k n", p=128), hb[bi][:, :, 0:n],
                      reads=[t_hb[bi]], writes=[t_dst])
            ar.pop()

        def gcol_of(gi, tt):
            if gi < 16:
                return (gi // 4) * L + (gi % 4) * 512 + tt * 128
            return tt * L + 2048

        def attention():
            ar.push()
            ar.off = common_start
            KT = ar.bf16(2 * G4).rearrange("p (h n) -> p h n", h=2)
            Vt = ar.bf16(68 * 256).rearrange("p (t n) -> p t n", t=68)
            t_KT = T("KT"); t_Vt = T("Vt")
            Wk = ar.bf16(KC * 256).rearrange("p (k n) -> p k n", k=KC)
            Wv = ar.bf16(KC * 256).rearrange("p (k n) -> p k n", k=KC)
            Wq = ar.bf16(KC * 256).rearrange("p (k n) -> p k n", k=KC)
            t_W = T("attW")
            _mark = ar.off
            stg = ar.f32(2048)
            t_stg = T("stg")
            for (wd, wb) in ((kvwk, Wk), (kvwv, Wv), (wq, Wq)):
                S.dma("sp", stg.rearrange("p (k n) -> p k n", k=KC), wd.rearrange("(k p) n -> p k n", p=128), writes=[t_stg])
                _cp(S, "act", wb, stg.rearrange("p (k n) -> p k n", k=KC), [t_stg], [t_W])
            S.barrier()
            ar.off = _mark
            PT = ar.f32(128)
            S.dma("sp", PT, permT, writes=[t_W])
            am = ar.bf16(4 * 512).rearrange("p (j n) -> p j n", j=4)
            S.dma("sp", am, amask.rearrange("j p n -> p j n"), writes=[t_W])
            sub = ar.f32(1)
            S.dma("sp", sub, sublnT, writes=[t_W])
            lam_t = ar.f32(256)
            S.dma("sp", lam_t, lamb.partition_broadcast(128), writes=[t_W])
            lsc = ar.f32(8)
            _tt(S, "dve", lam_t[:, 0:64], lam_t[:, 0:64], lam_t[:, 64:128], ALU.mult, [t_W], [t_W])
            _tt(S, "dve", lam_t[:, 128:192], lam_t[:, 128:192], lam_t[:, 192:256], ALU.mult, [t_W], [t_W])
            S.op("dve", lambda e: e.reduce_sum(out=lsc[:, 0:1], in_=lam_t[:, 0:64], axis=AX.X), [t_W], [t_W])
            S.op("dve", lambda e: e.reduce_sum(out=lsc[:, 1:2], in_=lam_t[:, 128:192], axis=AX.X), [t_W], [t_W])
            _act(S, lsc[:, 2:4], lsc[:, 0:2], AF.Exp, [t_W], [t_W])
            _stt(S, lsc[:, 4:5], lsc[:, 3:4], -LAM_INIT, lsc[:, 2:3], ALU.add, ALU.subtract, [t_W], [t_W])
            _ts(S, "dve", sub, sub, 1.0 - LAM_INIT, None, ALU.mult, None, [t_W], [t_W])
            _ckpt(11)
            hkb = ar.bf16(KC * 512).rearrange("p (k n) -> p k n", k=KC)
            t_hk = T("hkb")
            rc = ar.f32(512); rs = ar.f32(512)
            t_rope = T("rope")
            kA = ar.f32(512); kr = ar.f32(512); kt2 = ar.f32(512)
            t_kA = T("kA"); t_kr = T("kr"); t_kt2 = T("kt2")
            vst = [ar.f32(256) for _ in range(2)]
            t_vst = [T("vst0"), T("vst1")]

            def load_grp(src, t_src, gi):
                for (r, c0, n, dst) in grp_pieces(gi):
                    S.dma("sp", hkb[:, :, dst:dst + n],
                          tok_view(src, r, c0, n),
                          reads=[t_src], writes=[t_hk])
                S.dma("sp", rc, ropeC[gi], writes=[t_rope])
                S.dma("sp", rs, ropeS[gi], writes=[t_rope])

            def proj_rope(W, hc, scale_out=None):
                for kc in range(KC):
                    _mm(S, ps[0], W[:, kc, hc * 128:(hc + 1) * 128], hkb[:, kc, :], kc == 0, kc == KC - 1, [t_W, t_hk], [tps[0]])
                _cp(S, "act", kA, ps[0], [tps[0]], [t_kA])
                _mm(S, ps[1], PT, kA, True, True, [t_W, t_kA], [tps[1]])
                _tt(S, "dve", kr, kA, rc, ALU.mult, [t_kA, t_rope], [t_kr])
                _tt(S, "dve", kt2, ps[1], rs, ALU.mult, [tps[1], t_rope], [t_kt2])
                _tt(S, "pool", kr, kr, kt2, ALU.add, [t_kr, t_kt2], [t_kr])

            for gi in range(NGRP):
                load_grp(C_out, tC_out, gi)
                for hc in range(2):
                    proj_rope(Wk, hc)
                    for (r, c0, n, dst) in grp_pieces(gi):
                        g0 = r * L + c0
                        _cp(S, "act", KT[:, hc, g0:g0 + n], kr[:, dst:dst + n], [t_kr], [t_KT])
                        S.dma("pool", o_k[hc * 128:(hc + 1) * 128, g0:g0 + n], kr[:, dst:dst + n], reads=[t_kr], writes=[t_out])
                for tt in range(4):
                    vi = tt % 2
                    for kc in range(KC):
                        _mm(S, ps[2 + vi][:, 0:256], hkb[:, kc, tt * 128:(tt + 1) * 128], Wv[:, kc, :], kc == 0, kc == KC - 1,
                            [t_W, t_hk], [tps[2 + vi]])
                    g0 = gcol_of(gi, tt)
                    _cp(S, "act", vst[vi], ps[2 + vi][:, 0:256], [tps[2 + vi]], [t_vst[vi]])
                    _cp(S, "dve", Vt[:, g0 // 128, :], vst[vi], [t_vst[vi]], [t_Vt])
                    S.dma("pool", o_v[g0:g0 + 128, :], vst[vi], reads=[t_vst[vi]], writes=[t_out])
                if gi == ATT_G:
                    _ckpt(12)
            S.barrier()
            _ckpt(13)
            Qp = [[ar.bf16(512) for _ in range(2)] for _ in range(2)]
            t_Qp = [[T(f"Qp{a}{b}") for b in range(2)] for a in range(2)]
            for a_ in range(2):
                for b_ in range(2):
                    _memset(S, "pool", Qp[a_][b_], 0.0, [t_Qp[a_][b_]])

            def qproj_gen(gq_):
                load_grp(D_out, tD_out, gq_)
                yield
                for hc_ in range(2):
                    for kc in range(KC):
                        _mm(S, ps[0], Wq[:, kc, hc_ * 128:(hc_ + 1) * 128], hkb[:, kc, :], kc == 0, kc == KC - 1, [t_W, t_hk], [tps[0]])
                    yield
                    _cp(S, "act", kA, ps[0], [tps[0]], [t_kA])
                    yield
                    _mm(S, ps[0], PT, kA, True, True, [t_W, t_kA], [tps[0]])
                    _tt(S, "dve", kr, kA, rc, ALU.mult, [t_kA, t_rope], [t_kr])
                    yield
                    _tt(S, "dve", kt2, ps[0], rs, ALU.mult, [tps[0], t_rope], [t_kt2])
                    yield
                    _tt(S, "pool", kr, kr, kt2, ALU.add, [t_kr, t_kt2], [t_kr])
                    yield
                    _cp(S, "act", Qp[hc_][0][0:64, :], kr[0:64, :], [t_kr], [t_Qp[hc_][0]])
                    _cp(S, "act", Qp[hc_][1][64:128, :], kr[64:128, :], [t_kr], [t_Qp[hc_][1]])
                    yield
            Eb = [[ar.bf16(512) for _ in range(2)] for _ in range(2)]
            t_Eb = [[T(f"E{a}{b}") for b in range(2)] for a in range(2)]
            o0 = ar.f32(512); o1 = ar.f32(512); rr = ar.f32(512)
            t_o0 = T("o0"); t_o1 = T("o1"); t_rr = T("rr")
            zb = [ar.bf16(512) for _ in range(2)]
            t_zb = [T("zo0"), T("zo1")]
            Eacc = [ar.f32(512) for _ in range(2)]
            t_Eacc = [T("Eacc0"), T("Eacc1")]

            def finish(hc, n, pO0, pO1, pS0, pS1, dsts):
                S.op("dve", lambda e: e.reciprocal(out=rr[:, 0:n], in_=pS0), [tps[6]], [t_rr])
                _tt(S, "dve", o0[:, 0:n], pO0, rr[:, 0:n], ALU.mult, [tps[4], t_rr], [t_o0])
                S.op("dve", lambda e: e.reciprocal(out=rr[:, 0:n], in_=pS1), [tps[7], tps[6]], [t_rr])
                _tt(S, "dve", o1[:, 0:n], pO1, rr[:, 0:n], ALU.mult, [tps[5], tps[4], t_rr], [t_o1])
                _stt(S, o0[:, 0:n], o1[:, 0:n], lsc[:, 4:5], o0[:, 0:n], ALU.mult, ALU.add, [t_o1, t_o0, t_W], [t_o0])
                _tt(S, "pool", o1[:, 0:n], o0[:, 0:n], o0[:, 0:n], ALU.mult, [t_o0], [t_o1])
                _mm(S, ps[0][:, 0:n], ones_f, o1[:, 0:n], True, True, [tconst, t_o1], [tps[0]])
                _rsqrt(S, rr[:, 0:n], ps[0][:, 0:n], EPS, [tps[0]], t_rr, scale=1.0 / 128)
                _tt(S, "dve", o0[:, 0:n], o0[:, 0:n], rr[:, 0:n], ALU.mult, [t_o0, t_rr], [t_o0])
                zi = hc
                _ts(S, "dve", zb[zi][:, 0:n], o0[:, 0:n], sub[:, 0:1], None, ALU.mult, None, [t_o0, t_W], [t_zb[zi]])
                for (g0, d0, nn) in dsts:
                    for (ch, off, m, rel) in zsplit(g0, nn):
                        S.dma("pool", E_in[ch * 256 + hc * 128:ch * 256 + (hc + 1) * 128, off:off + m],
                              zb[zi][:, d0 + rel:d0 + rel + m], reads=[t_zb[zi]], writes=[tE_in])

            for gq in range(16):
                for _ in qproj_gen(gq):
                    pass
                nkt = 4 * (gq + 1)
                EbL = [Eb[0][0], Eb[0][1], Eb[1][0], Eb[1][1]]
                t_EbL = [t_Eb[0][0], t_Eb[0][1], t_Eb[1][0], t_Eb[1][1]]
                LOOK = 2
                for hc in range(2):
                    units = [(kt_, c) for kt_ in range(nkt) for c in range(2)]

                    def emit_s(i, hc=hc, units=units):
                        kt_, c = units[i]
                        g0 = (kt_ // 16) * L + (kt_ % 16) * 128
                        sb = 1 + i % 3
                        _mm(S, ps[sb], KT[:, hc, g0:g0 + 128], Qp[hc][c], True, True, [t_KT, t_Qp[hc][c]], [tps[sb]])

                    def emit_rest(i, hc=hc, units=units, nkt=nkt, gq=gq):
                        kt_, c = units[i]
                        g0 = (kt_ // 16) * L + (kt_ % 16) * 128
                        sb = 1 + i % 3
                        E_, tE_ = EbL[i % 4], t_EbL[i % 4]
                        _act(S, E_, ps[sb], AF.Exp, [tps[sb]], [tE_], scale=ATT_SCALE)
                        if kt_ >= 4 * gq:
                            _tt(S, "pool" if c == 0 else "dve", E_, E_, am[:, kt_ - 4 * gq, :], ALU.mult, [tE_, t_W], [tE_])
                        _mm(S, ps[4 + c], Vt[:, g0 // 128, hc * 128:(hc + 1) * 128], E_, kt_ == 0, kt_ == nkt - 1,
                            [t_Vt, tE_], [tps[4 + c]])
                        aeng = "pool" if c == 0 else "dve"
                        if kt_ == 0:
                            _cp(S, aeng, Eacc[c], E_, [tE_], [t_Eacc[c]])
                        else:
                            _tt(S, aeng, Eacc[c], Eacc[c], E_, ALU.add, [tE_, t_Eacc[c]], [t_Eacc[c]])
                        if kt_ == nkt - 1:
                            _mm(S, ps[6 + c], ones_f, Eacc[c], True, True, [tconst, t_Eacc[c]], [tps[6 + c]])
                    for i in range(min(LOOK, len(units))):
                        emit_s(i)
                    for i in range(len(units)):
                        if i + LOOK < len(units):
                            emit_s(i + LOOK)
                        emit_rest(i)
                    gg0 = (gq // 4) * L + (gq % 4) * 512
                    finish(hc, 512, ps[4], ps[5], ps[6], ps[7], [(gg0, 0, 512)])
                _ckpt(14)
            _ckpt(15)
            for _ in qproj_gen(16):
                pass
            ck = [ar.f32(256) for _ in range(2)]; cv = [ar.f32(256) for _ in range(2)]
            t_ck = [T("ck0"), T("ck1")]; t_cv = [T("cv0"), T("cv1")]
            kTt = [ar.bf16(256) for _ in range(2)]; vbt = [ar.bf16(256) for _ in range(2)]
            t_kTt = [T("kTt0"), T("kTt1")]; t_vbt = [T("vbt0"), T("vbt1")]
            EbS = [Eb[0][0], Eb[0][1], Eb[1][0], Eb[1][1]]
            t_EbS = [t_Eb[0][0], t_Eb[0][1], t_Eb[1][0], t_Eb[1][1]]
            for s in range(8):
                q0 = 64 * s
                gs0 = (s // 2) * L + 2048 + 64 * (s % 2)
                R_ = slice(64 * (s % 2), 64 * (s % 2) + 64)
                for hc in range(2):
                    hcs = slice(hc * 128, (hc + 1) * 128)
                    units = [(kt_, c) for kt_ in range(17) for c in range(2)]

                    def s_emit_s(i, s=s, hc=hc, hcs=hcs, units=units, q0=q0, gs0=gs0, R_=R_):
                        kt_, c = units[i]
                        bi = kt_ % 2
                        last = kt_ == 16
                        P = slice(64 * c, 64 * c + 64)
                        sb = 1 + i % 3
                        if c == 0 and not last:
                            S.dma("sp", ck[bi][:, 0:128], cache_k[s, kt_ * 128:(kt_ + 1) * 128, hcs], writes=[t_ck[bi]])
                            S.dma("sp", cv[bi][:, 0:128], cache_v[s, kt_ * 128:(kt_ + 1) * 128, hcs], writes=[t_cv[bi]])
                            _cp(S, "pool", vbt[bi][:, 0:128], cv[bi][:, 0:128], [t_cv[bi]], [t_vbt[bi]])
                            _tr(S, ps[0][:, 0:128], ck[bi][:, 0:128], ident, [t_ck[bi], tconst], [tps[0]])
                            _cp(S, "act", kTt[bi][:, 0:128], ps[0][:, 0:128], [tps[0]], [t_kTt[bi]])
                        if not last:
                            _mm(S, ps[sb][:, 0:64], kTt[bi][:, 0:128], Qp[hc][c][:, q0:q0 + 64], True, True,
                                [t_kTt[bi], t_Qp[hc][c]], [tps[sb]])
                        else:
                            _mm(S, ps[sb][R_, 0:64], KT[:, hc, gs0:gs0 + 64], Qp[hc][c][:, q0:q0 + 64], True, True,
                                [t_KT, t_Qp[hc][c]], [tps[sb]])

                    def s_emit_rest(i, hc=hc, hcs=hcs, units=units, gs0=gs0, R_=R_):
                        kt_, c = units[i]
                        bi = kt_ % 2
                        last = kt_ == 16
                        sb = 1 + i % 3
                        E_, tE_ = EbS[i % 4], t_EbS[i % 4]
                        if not last:
                            so = ps[sb][:, 0:64]
                            eo = E_[:, 0:64]
                            vl = vbt[bi][:, 0:128]
                            vr = [t_vbt[bi]]
                        else:
                            _memset(S, "pool", E_[:, 0:64], 0.0, [tE_])
                            so = ps[sb][R_, 0:64]
                            eo = E_[R_, 0:64]
                            vl = Vt[:, gs0 // 128, hcs]
                            vr = [t_Vt]
                        _act(S, eo, so, AF.Exp, [tps[sb]], [tE_], scale=ATT_SCALE)
                        ef = E_[:, 0:64]
                        _mm(S, ps[4 + c][:, 0:64], vl, ef, kt_ == 0, last, vr + [tE_], [tps[4 + c]])
                        _mm(S, ps[6 + c][:, 0:64], ones_bf, ef, kt_ == 0, last, [tconst, tE_], [tps[6 + c]])
                    for i in range(2):
                        s_emit_s(i)
                    for i in range(len(units)):
                        if i + 2 < len(units):
                            s_emit_s(i + 2)
                        s_emit_rest(i)
                    finish(hc, 64, ps[4][:, 0:64], ps[5][:, 0:64], ps[6][:, 0:64], ps[7][:, 0:64], [(gs0, 0, 64)])
            ar.pop()


        def out_proj(Zout, t_Z, w_dram, co_sel):
            ar.push()
            zb = ar.bf16(KC * 1088).rearrange("p (k n) -> p k n", k=KC)
            t_zb = T("zb")
            zt = [ar.bf16(KC * 1088).rearrange("p (k n) -> p k n", k=KC) for _ in range(2)]
            t_zt = [T("zt0"), T("zt1")]
            Fo = ar.f32(KC * 1088).rearrange("p (k n) -> p k n", k=KC)
            t_Fo = T("Fo2")
            for p in range(2):
                cgs = [(p * 1024, 512, 0), (p * 1024 + 512, 512, 512), (2048 + 64 * p, 64, 1024)]
                for r in range(4):
                    zi = r % 2
                    for (c0, n, hc0) in cgs:
                        for (ch, off, m, rel) in zsplit(r * L + c0, n):
                            S.dma("sp", zt[zi][:, :, hc0 + rel:hc0 + rel + m],
                                  Zout[ch * D:(ch + 1) * D, off:off + m].rearrange("(k p) n -> p k n", p=128),
                                  reads=[t_Z], writes=[t_zt[zi]])
                    if r == 0:
                        _ts(S, "dve", zb, zt[zi], sel[:, 0:1], None, ALU.mult, None, [t_zt[zi], t_sel], [t_zb])
                    else:
                        for kc in range(KC):
                            _stt(S, zb[:, kc, :], zt[zi][:, kc, :], sel[:, r:r + 1], zb[:, kc, :], ALU.mult, ALU.add,
                                 [t_zt[zi], t_sel, t_zb], [t_zb])
                for dc in range(KC):
                    wv = w_dram[:, dc * 128:(dc + 1) * 128].rearrange("(k p) n -> p k n", p=128)
                    b, tb = load_w(wv, KC, 128)
                    for ci, (c0, n, hc0) in enumerate(cgs):
                        pb = ci
                        for kc in range(KC):
                            _mm(S, ps[pb][:, 0:n], b[:, kc, :], zb[:, kc, hc0:hc0 + n], kc == 0, kc == KC - 1,
                                [tb, t_zb], [tps[pb]])
                        _cp(S, "act" if ci % 2 == 0 else "dve", Fo[:, dc, hc0:hc0 + n], ps[pb][:, 0:n], [tps[pb]], [t_Fo])
                for (c0, n, hc0) in cgs:
                    postnorm_add(Fo, t_Fo, hc0, c0, n, co_sel(seq_of(c0)), 7)
            ar.pop()

        def rwkv():
            ar.push()
            ar.off = common_start
            cm = ar.f32(640)
            t_cm = T("cm")
            S.dma("sp", cm, cmask, writes=[t_cm])
            m4 = cm[:, 0:512]
            m_ij = cm[:, 512:640]
            scanm = ar.f32(512)
            _memset(S, "pool", scanm, 1.0, [t_cm])
            _memset(S, "pool", scanm.rearrange("p (c t) -> p c t", t=64)[:, :, 0:1], 0.0, [t_cm])
            rv = ar.f32(14).rearrange("p (w h) -> p w h", w=7)
            mu = ar.f32(48).rearrange("p (w k) -> p w k", w=6)
            shf = ar.f32(64).rearrange("p (k s) -> p k s", k=KC)
            S.dma("sp", rv, rvecT, writes=[t_cm])
            S.dma("sp", mu, muT, writes=[t_cm])
            S.dma("sp", shf, shiftT, writes=[t_cm])
            Wst = ar.bf16(16 * 768).rearrange("p (k w n) -> p k w n", k=16, w=3)
            Wl = ar.bf16(16 * 256).rearrange("p (k n) -> p k n", k=16)
            W2 = ar.bf16(768)
            t_W = T("rwkvW")
            NT = 9
            tfall = ar.f32(NT * 512)
            tf = [tfall[:, i * 512:(i + 1) * 512] for i in range(NT)]
            t_wf = [T("stgA"), T("stgB")]
            wst_l = [tfall[:, 0:2048], tfall[:, 2048:4096]]
            lctr = [0]
            for pi, mi in ((0, 0), (1, 2), (2, 3)):
                si = lctr[0] % 2
                lctr[0] += 1
                f = wst_l[si][:, 0:KC * 256].rearrange("p (k n) -> p k n", k=KC)
                S.dma("sp", f, w_rkv[pi].rearrange("(k p) n -> p k n", p=128), writes=[t_wf[si]])
                _cp(S, "act", Wst[:, 0:8, pi, :], f, [t_wf[si]], [t_W])
                for kc in range(KC):
                    _ts(S, "dve", Wst[:, 8 + kc, pi, :], f[:, kc, :], mu[:, mi, kc:kc + 1], None, ALU.mult, None,
                        [t_wf[si], t_cm], [t_W])
            si = lctr[0] % 2
            lctr[0] += 1
            f = wst_l[si][:, 0:KC * 256].rearrange("p (k n) -> p k n", k=KC)
            S.dma("sp", f, w_l1.rearrange("(k p) n -> p k n", p=128), writes=[t_wf[si]])
            _cp(S, "act", Wl[:, 0:8, :], f, [t_wf[si]], [t_W])
            for kc in range(KC):
                for (a0_, a1_, mi) in ((0, 64, 1), (64, 128, 4), (128, 256, 5)):
                    _ts(S, "dve", Wl[:, 8 + kc, a0_:a1_], f[:, kc, a0_:a1_], mu[:, mi, kc:kc + 1], None, ALU.mult, None,
                        [t_wf[si], t_cm], [t_W])
            si = lctr[0] % 2
            lctr[0] += 1
            f = wst_l[si][:, 0:768]
            S.dma("sp", f[0:64, 0:256], w_w2, writes=[t_wf[si]])
            S.dma("sp", f[0:64, 256:512], w_a2, writes=[t_wf[si]])
            S.dma("sp", f[:, 512:768], w_g2, writes=[t_wf[si]])
            _cp(S, "act", W2[0:64, 0:512], f[0:64, 0:512], [t_wf[si]], [t_W])
            _cp(S, "act", W2[:, 512:768], f[:, 512:768], [t_wf[si]], [t_W])

            Hb1 = ar.bf16(KC * 514).rearrange("p (k n) -> p k n", k=KC)
            Hb = [Hb1, Hb1]
            t_H1 = T("H0")
            t_H = [t_H1, t_H1]
            lastc = ar.bf16(KC * 2).rearrange("p (k n) -> p k n", k=KC)
            t_lastc = T("lastc")
            dxb = ar.bf16(KC * 512).rearrange("p (k n) -> p k n", k=KC)
            t_dx = T("dx")
            twb = ar.bf16(512); tab = ar.bf16(512); tgb = ar.bf16(512)
            t_lm = T("loramid")
            t_tf = [T(f"tf{i}") for i in range(NT)]
            S.barrier()
            _ckpt(1)
            rt = [ar.bf16(512) for _ in range(2)]; kt = [ar.bf16(512) for _ in range(2)]
            at = [ar.bf16(512) for _ in range(2)]; bt = [ar.bf16(512) for _ in range(2)]
            t_fm = [T("fm0"), T("fm1")]
            bon = [ar.f32(512) for _ in range(2)]; gg = [ar.bf16(512) for _ in range(2)]
            t_bg = [T("bg0"), T("bg1")]
            gCs = ar.f32(16).rearrange("p (h c) -> p h c", h=2)
            t_gC = T("gC")
            Vtm = ar.bf16(4 * 256).rearrange("p (t n) -> p t n", t=4)
            Khtm = ar.bf16(4 * 256).rearrange("p (t n) -> p t n", t=4)
            Bhtm = ar.bf16(4 * 256).rearrange("p (t n) -> p t n", t=4)
            t_tm = T("tokmaj")
            A_k = [[ar.bf16(128) for _ in range(2)] for _ in range(4)]
            BS = [[ar.bf16(256) for _ in range(2)] for _ in range(4)]
            Sf = [ar.f32(128) for _ in range(4)]
            t_ut = [T(f"ut{i}") for i in range(4)]
            Tt = [[ar.bf16(128) for _ in range(4)] for _ in range(2)]
            A3 = [[ar.bf16(384) for _ in range(4)] for _ in range(2)]
            t_res = [[T(f"res{a}{b}") for b in range(4)] for a in range(2)]
            Mst = ar.f32(128).rearrange("p (h v) -> p h v", h=2)
            Mbf = ar.bf16(256).rearrange("p (k v) -> p k v", k=4)
            Mbf4 = Mbf.rearrange("p (c h) v -> p c h v", h=2)
            t_M = T("M"); t_Mbf = T("Mbf")
            Xs = ar.bf16(256); Us = ar.bf16(256)
            t_Xs = T("Xs"); t_Us = T("Us")
            Yg = [ar.f32(512) for _ in range(2)]; Zb = [ar.bf16(512) for _ in range(2)]
            t_Yg = [T("Yg0"), T("Yg1")]
            t_Yf = T("Yf"); t_Zb = [T("Zb0"), T("Zb1")]
            _memset(S, "dve", Mst, 0.0, [t_M])
            _memset(S, "dve", Mbf, 0.0, [t_Mbf])
            _memset(S, "dve", Xs, 0.0, [t_Xs])
            _memset(S, "dve", Us, 0.0, [t_Us])

            def upd_mbf(eng):
                for hh_ in range(2):
                    P_ = slice(64 * hh_, 64 * hh_ + 64)
                    _cp(S, eng, Mbf4[P_, :, hh_, :], Mst[P_, :, :], [t_M], [t_Mbf])
            PYM = 0; PXU = 1; PEX = 2

            for gi in range(NGRP):
                hi = gi % 2
                H = Hb[hi]
                if gi > 0:
                    _cp(S, "dve", lastc[:, :, 0:1], H[:, :, 512:513], [t_H[hi]], [t_lastc])
                for (r, c0, n, dst) in grp_pieces(gi):
                    S.dma("sp", H[:, :, 1 + dst:1 + dst + n],
                          tok_view(A_out, r, c0, n),
                          reads=[tA_out], writes=[t_H[hi]])
                if gi == 0:
                    _memset(S, "dve", H[:, :, 0:1], 0.0, [t_H[hi]])
                else:
                    _cp(S, "dve", H[:, :, 0:1], lastc[:, :, 0:1], [t_lastc], [t_H[hi]])
                _tt(S, "dve", dxb, H[:, :, 0:512], H[:, :, 1:513], ALU.subtract, [t_H[hi]], [t_dx])
                if gi == 16:
                    _tt(S, "dve", dxb.rearrange("p k (s t) -> p k s t", t=64)[:, :, :, 0], shf,
                        H[:, :, 1:513].rearrange("p k (s t) -> p k s t", t=64)[:, :, :, 0], ALU.subtract,
                        [t_H[hi], t_cm, t_dx], [t_dx])

                def rhs(kk):
                    return H[:, kk, 1:513] if kk < 8 else dxb[:, kk - 8, :]
                for (b, c0_, c1_, dstb, fn) in ((4, 0, 64, twb, AF.Tanh), (5, 64, 128, tab, AF.Copy), (6, 128, 256, tgb, AF.Sigmoid)):
                    m = c1_ - c0_
                    for kk in range(16):
                        _mm(S, ps[b][0:m, :], Wl[:, kk, c0_:c1_], rhs(kk), kk == 0, kk == 15, [t_W, t_H[hi], t_dx], [tps[b]])
                    _act(S, dstb[0:m, :], ps[b][0:m, :], fn, [tps[b]], [t_lm])
                _ckpt(2)
                for hc in range(2):
                    R, K_, V_, LW, CUM, A_, KKN, T1, T2 = tf
                    tR, tK, tV, tLW, tCUM, tA, tKKN, tT1, tT2 = t_tf
                    for pi, (dstt, tdst) in enumerate(((R, tR), (K_, tK), (V_, tV))):
                        b = 4 + pi
                        for kk in range(16):
                            _mm(S, ps[b], Wst[:, kk, pi, hc * 128:(hc + 1) * 128], rhs(kk), kk == 0, kk == 15,
                                [t_W, t_H[hi], t_dx], [tps[b]])
                        _cp(S, "act" if pi != 1 else "dve", dstt, ps[b], [tps[b]], [tdst])
                    _mm(S, ps[7], W2[0:64, hc * 128:(hc + 1) * 128], twb[0:64, :], True, True, [t_W, t_lm], [tps[7]])
                    _act(S, LW, ps[7], AF.Sigmoid, [tps[7], t_cm], [tLW], bias=rv[:, 0, hc:hc + 1])
                    _ts(S, "dve", LW, LW, -math.exp(-0.5), None, ALU.mult, None, [tLW], [tLW])
                    _mm(S, ps[7], W2[0:64, 256 + hc * 128:256 + (hc + 1) * 128], tab[0:64, :], True, True, [t_W, t_lm], [tps[7]])
                    _act(S, A_, ps[7], AF.Sigmoid, [tps[7], t_cm], [tA], bias=rv[:, 1, hc:hc + 1])
                    _mm(S, ps[7], W2[:, 512 + hc * 128:512 + (hc + 1) * 128], tgb, True, True, [t_W, t_lm], [tps[7]])
                    _cp(S, "act", gg[hc], ps[7], [tps[7]], [t_bg[hc]])
                    _ts(S, "dve", T1, K_, rv[:, 2, hc:hc + 1], None, ALU.mult, None, [tK, t_cm], [tT1])
                    _tt(S, "pool", T2, T1, T1, ALU.mult, [tT1], [tT2])
                    _mm(S, ps[7], blk_f, T2, True, True, [tconst, tT2], [tps[7]])
                    _rsqrt(S, T2, ps[7], 1e-24, [tps[7]], tT2)
                    _tt(S, "dve", KKN, T1, T2, ALU.mult, [tT1, tT2], [tKKN])
                    _ts(S, "dve", T1, A_, -1.0, rv[:, 3, hc:hc + 1], ALU.add, ALU.mult, [tA, t_cm], [tT1])
                    _ts(S, "dve", T1, T1, 1.0, None, ALU.add, None, [tT1], [tT1])
                    _tt(S, "dve", K_, K_, T1, ALU.mult, [tK, tT1], [tK])
                    _stt(S, T1, R, rv[:, 6, hc:hc + 1], K_, ALU.mult, ALU.mult, [tR, tK, t_cm], [tT1])
                    _mm(S, ps[7], blk_f, T1, True, True, [tconst, tT1], [tps[7]])
                    _tt(S, "dve", bon[hc], ps[7], V_, ALU.mult, [tps[7], tV], [t_bg[hc]])
                    _tt(S, "pool", A_, KKN, A_, ALU.mult, [tKKN, tA], [tA])
                    S.op("dve", lambda e, o=CUM, d0=scanm, d1=LW: e.tensor_tensor_scan(out=o, data0=d0, data1=d1, initial=0.0,
                                                                                        op0=ALU.mult, op1=ALU.add),
                         [tLW, t_cm], [tCUM])
                    cum3 = CUM.rearrange("p (c t) -> p c t", t=64)
                    _ckpt(3)
                    _act(S, gCs[:, hc, :], cum3[:, :, 63], AF.Exp, [tCUM], [t_gC])
                    _act(S, T1, CUM, AF.Exp, [tCUM], [tT1])
                    _tt(S, "dve", rt[hc], R, T1, ALU.mult, [tR, tT1], [t_fm[hc]])
                    _tt(S, "pool", T2, CUM, LW, ALU.subtract, [tCUM, tLW], [tT2])
                    _act(S, T2, T2, AF.Exp, [tT2], [tT2])
                    _stt(S, at[hc], KKN, -1.0, T2, ALU.mult, ALU.mult, [tKKN, tT2], [t_fm[hc]])
                    _act(S, T1, CUM, AF.Exp, [tCUM], [tT1], scale=-1.0)
                    _tt(S, "dve", kt[hc], K_, T1, ALU.mult, [tK, tT1], [t_fm[hc]])
                    _tt(S, "pool", bt[hc], A_, T1, ALU.mult, [tA, tT1], [t_fm[hc]])
                    _tt(S, "dve", T2.rearrange("p (c t) -> p c t", t=64), cum3[:, :, 63:64].broadcast_to([128, 8, 64]), cum3,
                        ALU.subtract, [tCUM], [tT2])
                    _act(S, T2, T2, AF.Exp, [tT2], [tT2])
                    _tt(S, "dve", K_, K_, T2, ALU.mult, [tK, tT2], [tK])
                    _tt(S, "pool", A_, A_, T2, ALU.mult, [tA, tT2], [tA])
                    for (src, tsrc, dstm, b) in ((V_, tV, Vtm, 4), (K_, tK, Khtm, 5), (A_, tA, Bhtm, 6)):
                        for tp in range(4):
                            _tr(S, ps[b][:, tp * 128:(tp + 1) * 128], src[:, tp * 128:(tp + 1) * 128], ident,
                                [tsrc, tconst], [tps[b]])
                        _cp(S, "act" if b != 5 else "dve", dstm[:, :, hc * 128:(hc + 1) * 128],
                            ps[b].rearrange("p (t n) -> p t n", t=4), [tps[b]], [t_tm])

                _ckpt(4)
                def ut_init(tp, k4):
                    hc, hh = k4 // 2, k4 % 2
                    P = slice(64 * hh, 64 * hh + 64)
                    cs_ = slice(tp * 128, (tp + 1) * 128)
                    rb = tp % 2
                    pa = 4 + k4
                    fmr = [t_fm[hc]]
                    ex = ps[PEX][:, k4 * 128:(k4 + 1) * 128]
                    _mm(S, ps[pa][:, 0:128], bt[hc][P, cs_], at[hc][P, cs_], True, True, fmr, [tps[pa]])
                    _mm(S, ps[pa][:, 128:256], bt[hc][P, cs_], rt[hc][P, cs_], True, True, fmr, [tps[pa]])
                    _mm(S, ps[pa][:, 256:384], kt[hc][P, cs_], at[hc][P, cs_], True, True, fmr, [tps[pa]])
                    _mm(S, ps[pa][:, 384:512], kt[hc][P, cs_], rt[hc][P, cs_], True, True, fmr, [tps[pa]])
                    _mm(S, ex, at[hc][P, cs_], bt[hc][P, cs_], True, True, fmr, [tps[PEX]])
                    _tt(S, "dve", Sf[k4], ps[pa][:, 0:128], m4[:, 0:128], ALU.mult, [tps[pa], t_cm], [t_ut[k4]])
                    _tt(S, "dve", A3[rb][k4], ps[pa][:, 128:512], m4[:, 128:512], ALU.mult, [tps[pa], t_cm], [t_res[rb][k4]])
                    _cp(S, "act", BS[k4][0][:, 0:128], Sf[k4], [t_ut[k4]], [t_ut[k4]])
                    _tt(S, "dve", Sf[k4], Sf[k4], ident, ALU.add, [t_ut[k4], tconst], [t_ut[k4]])
                    _cp(S, "act", BS[k4][0][:, 128:256], Sf[k4], [t_ut[k4]], [t_ut[k4]])
                    _tt(S, "dve", A_k[k4][0], ex, m_ij, ALU.mult, [tps[PEX], t_cm], [t_ut[k4]])

                def ut_level(tp, k4, lvl):
                    rb = tp % 2
                    pa = 4 + k4
                    tu = t_ut[k4]
                    if lvl == 0:
                        _mm(S, ps[pa][:, 0:128], A_k[k4][0], BS[k4][0][:, 0:128], True, True, [tu], [tps[pa]])
                        _mm(S, ps[pa][:, 256:384], BS[k4][0][:, 0:128], A_k[k4][0], True, True, [tu], [tps[pa]])
                        _cp(S, "act", BS[k4][1][:, 0:128], ps[pa][:, 0:128], [tps[pa]], [tu])
                        _cp(S, "act", BS[k4][1][:, 128:256], BS[k4][0][:, 128:256], [tu], [tu])
                        _cp(S, "dve", A_k[k4][1], ps[pa][:, 256:384], [tps[pa]], [tu])
                    elif lvl < 5:
                        c_, n_ = (lvl % 2), 1 - (lvl % 2)
                        _mm(S, ps[pa][:, 0:256], A_k[k4][c_], BS[k4][c_], True, True, [tu], [tps[pa]])
                        _mm(S, ps[pa][:, 256:384], BS[k4][c_][:, 0:128], A_k[k4][c_], True, True, [tu], [tps[pa]])
                        _cp(S, "act", BS[k4][n_][:, 0:128], ps[pa][:, 0:128], [tps[pa]], [tu])
                        _tt(S, "dve", Sf[k4], Sf[k4], ps[pa][:, 128:256], ALU.add, [tps[pa], tu], [tu])
                        _cp(S, "act", BS[k4][n_][:, 128:256], Sf[k4], [tu], [tu])
                        _cp(S, "dve", A_k[k4][n_], ps[pa][:, 256:384], [tps[pa]], [tu])
                    else:
                        c_ = lvl % 2
                        _mm(S, ps[pa][:, 0:128], A_k[k4][c_], BS[k4][c_][:, 128:256], True, True, [tu], [tps[pa]])
                        _tt(S, "dve", Tt[rb][k4], Sf[k4], ps[pa][:, 0:128], ALU.add, [tps[pa], tu], [t_res[rb][k4]])

                def chain_stage(tp, cc, st):
                    rb = tp % 2
                    c = 2 * tp + cc
                    Q = slice(64 * cc, 64 * cc + 64)
                    ccol = slice(c * 64, c * 64 + 64)
                    if st == 0:
                        if gi == 16:
                            S.dma("sp", Mst, wkv0[:, :, c, :], writes=[t_M])
                            upd_mbf("dve")
                        for k4 in range(4):
                            hc, hh = k4 // 2, k4 % 2
                            vcol = slice(hc * 128 + hh * 64, hc * 128 + hh * 64 + 64)
                            _mm(S, ps[PXU][Q, k4 * 64:(k4 + 1) * 64], at[hc][:, ccol], Mbf[:, k4, :], True, False,
                                [t_fm[hc], t_Mbf], [tps[PXU]])
                            _mm(S, ps[PXU][Q, k4 * 64:(k4 + 1) * 64], A3[rb][k4][:, 128 + 64 * cc:128 + 64 * cc + 64],
                                Vtm[:, tp, vcol], False, True, [t_res[rb][k4], t_tm], [tps[PXU]])
                        _cp(S, "act", Xs[Q, :], ps[PXU][Q, 0:256], [tps[PXU]], [t_Xs])
                    elif st == 1:
                        for k4 in range(4):
                            _mm(S, ps[PXU][Q, 256 + k4 * 64:256 + (k4 + 1) * 64], Tt[rb][k4][:, 64 * cc:64 * cc + 64],
                                Xs[:, k4 * 64:(k4 + 1) * 64], True, True, [t_res[rb][k4], t_Xs], [tps[PXU]])
                        _cp(S, "dve", Us[Q, :], ps[PXU][Q, 256:512], [tps[PXU]], [t_Us])
                    else:
                        for k4 in range(4):
                            hc, hh = k4 // 2, k4 % 2
                            P = slice(64 * hh, 64 * hh + 64)
                            vcol = slice(hc * 128 + hh * 64, hc * 128 + hh * 64 + 64)
                            yo = ps[PYM][P, hc * 128 + cc * 64:hc * 128 + cc * 64 + 64]
                            _mm(S, yo, Mbf[:, k4, :], rt[hc][:, ccol], True, False, [t_Mbf, t_fm[hc]], [tps[PYM]])
                            _mm(S, yo, Us[:, k4 * 64:(k4 + 1) * 64], A3[rb][k4][:, 64 * cc:64 * cc + 64], False, False,
                                [t_Us, t_res[rb][k4]], [tps[PYM]])
                            _mm(S, yo, Vtm[:, tp, vcol], A3[rb][k4][:, 256 + 64 * cc:256 + 64 * cc + 64], False, True,
                                [t_tm, t_res[rb][k4]], [tps[PYM]])
                            mo = ps[PYM][P, 256 + hc * 64:256 + (hc + 1) * 64]
                            _mm(S, mo, Bhtm[Q, tp, vcol], Us[Q, k4 * 64:(k4 + 1) * 64], True, False, [t_tm, t_Us], [tps[PYM]])
                            _mm(S, mo, Khtm[Q, tp, vcol], Vtm[Q, tp, vcol], False, True, [t_tm], [tps[PYM]])
                        for hc in range(2):
                            _stt(S, Mst[:, hc, :], Mst[:, hc, :], gCs[:, hc, c:c + 1], ps[PYM][:, 256 + hc * 64:256 + (hc + 1) * 64],
                                 ALU.mult, ALU.add, [t_M, t_gC, tps[PYM]], [t_M])
                        upd_mbf("act")
                        if gi == 16:
                            S.dma("pool", o_wkv[:, 1 + c, :, :], Mst, reads=[t_M], writes=[t_out])
                        elif gi == 15 and c == 7:
                            S.dma("pool", o_wkv[:, 0, :, :], Mst, reads=[t_M], writes=[t_out])
                        if cc == 1:
                            for hc in range(2):
                                _cp(S, "dve", Yg[hc][:, tp * 128:(tp + 1) * 128], ps[PYM][:, hc * 128:(hc + 1) * 128],
                                    [tps[PYM]], [t_Yg[hc]])

                for tp in range(5):
                    for step in range(7):
                        if tp < 4:
                            for k4 in range(4):
                                if step == 0:
                                    ut_init(tp, k4)
                                else:
                                    ut_level(tp, k4, step - 1)
                        if tp >= 1 and step >= 1:
                            cc_, st_ = divmod(step - 1, 3)
                            chain_stage(tp - 1, cc_, st_)
                _ckpt(6)
                for hc in range(2):
                    T1, T2, T3 = tf[0], tf[1], tf[2]
                    tT1, tT2, tT3 = t_tf[0], t_tf[1], t_tf[2]
                    Yf = Yg[hc]
                    t_Yf = t_Yg[hc]
                    _mm(S, ps[7], blk_f, Yf, True, True, [tconst, t_Yf], [tps[7]])
                    _ts(S, "dve", T1, ps[7], 1.0 / 64, None, ALU.mult, None, [tps[7]], [tT1])
                    _tt(S, "pool", T2, Yf, Yf, ALU.mult, [t_Yf], [tT2])
                    _mm(S, ps[7], blk_f, T2, True, True, [tconst, tT2], [tps[7]])
                    _tt(S, "pool", T3, T1, T1, ALU.mult, [tT1], [tT3])
                    _stt(S, T2, ps[7], 1.0 / 64, T3, ALU.mult, ALU.subtract, [tps[7], tT3], [tT2])
                    _rsqrt(S, T2, T2, GN_EPS, [tT2], tT2)
                    _tt(S, "dve", T1, Yf, T1, ALU.subtract, [t_Yf, tT1], [tT1])
                    _tt(S, "dve", T1, T1, T2, ALU.mult, [tT1, tT2], [tT1])
                    _ts(S, "dve", T1, T1, rv[:, 4, hc:hc + 1], rv[:, 5, hc:hc + 1], ALU.mult, ALU.add, [tT1, t_cm], [tT1])
                    _tt(S, "dve", T1, T1, bon[hc], ALU.add, [tT1, t_bg[hc]], [tT1])
                    zi = hc
                    _tt(S, "dve", Zb[zi], T1, gg[hc], ALU.mult, [tT1, t_bg[hc]], [t_Zb[zi]])
                    for (r, c0, n, dst) in grp_pieces(gi):
                        for (ch, off, m, rel) in zsplit(r * L + c0, n):
                            S.dma("pool", B_in[ch * 256 + hc * 128:ch * 256 + (hc + 1) * 128, off:off + m],
                                  Zb[zi][:, dst + rel:dst + rel + m], reads=[t_Zb[zi]], writes=[tB_in])
            ar.pop()

        S.barrier()

        def PH(name):
            return PHASES is None or name in PHASES

        def NOCC(name):
            return SKIP_CC is not None and name in SKIP_CC
        if PH("ffn0a"):
            ffn(0, 0, 0)
            S.barrier()
        if PH("hm1"):
            emit_hm(lambda s: gsv[:, 0, 1, s, :], lambda s: modv[:, 0, 24:32, s], A_in, tA_in, True)
            S.dma("pool", o_shift, shcap, reads=[t_shcap], writes=[t_out])
            if not NOCC("A"):
                gath_tok(A_in, tA_in, A_out, tA_out)
            S.barrier()
        if PH("rwkv"):
            try:
                rwkv()
            except _Stop:
                ar.pop()
            S.barrier()
            if not NOCC("B"):
                gath_z(B_in, tB_in, B_out, tB_out)
            S.barrier()
        if PH("oproj1"):
            out_proj(B_out, tB_out, w_o1, lambda s: cov[:, 0, 1, s, :])
            S.barrier()
        if PH("ffn0b"):
            ffn(0, 1, 2)
            S.barrier()
        if PH("hk"):
            emit_hm(lambda s: kgs[:, s, :], lambda s: kvmod[:, 0:8, s], C_in, tC_in, False)
            if not NOCC("C"):
                gath_tok(C_in, tC_in, C_out, tC_out)
            S.barrier()
        if PH("ffn1a"):
            ffn(1, 0, 0)
            S.barrier()
        if PH("hm2"):
            emit_hm(lambda s: gsv[:, 1, 1, s, :], lambda s: modv[:, 1, 24:32, s], D_in, tD_in, False)
            if not NOCC("D"):
                gath_tok(D_in, tD_in, D_out, tD_out)
            S.barrier()
        if PH("attn"):
            try:
                attention()
            except _Stop:
                ar.pop()
            S.barrier()
            if not NOCC("E"):
                gath_z(E_in, tE_in, E_out, tE_out)
            S.barrier()
        if PH("oproj2"):
            out_proj(E_out, tE_out, w_o2, lambda s: cov[:, 1, 1, s, :])
            S.barrier()
        if PH("ffn1b"):
            ffn(1, 1, 2)
            S.barrier()
        for gi, (c0, n) in enumerate([(0, 512), (512, 512), (1024, 512), (1536, 512), (2048, 128)]):
            S.dma("pool", yT[:, c0:c0 + n].rearrange("(k p) n -> p k n", p=128), X[:, :, c0:c0 + n],
                  reads=[tX[gi]], writes=[t_out])
        S.emit(st)
        print("op stats", S.stats, "arena peak", ar.peak, "of", ARENA_WORDS, flush=True)
    return nc


_NC_CACHE = {}


def _consts():
    half = 8
    inv = (500000.0 ** (-np.arange(half, dtype=np.float32) * 2.0 / 16)).astype(np.float32)
    ropeC = np.ones((NGRP, 128, 512), np.float32)
    ropeS = np.zeros((NGRP, 128, 512), np.float32)
    for gi in range(NGRP):
        pos = np.zeros(512, np.float32)
        for (r, c0, n, dst) in grp_pieces(gi):
            if gi < 16:
                pos[dst:dst + n] = 2048 * r + c0 + np.arange(n)
            else:
                pos[dst:dst + n] = 2048 + np.arange(n)
        ang = pos[None, :].astype(np.float32) * inv[:, None]
        cs, sn = np.cos(ang).astype(np.float32), np.sin(ang).astype(np.float32)
        for blk in range(2):
            b0 = blk * 64
            ropeC[gi, b0:b0 + 8] = cs
            ropeC[gi, b0 + 8:b0 + 16] = cs
            ropeS[gi, b0:b0 + 8] = -sn
            ropeS[gi, b0 + 8:b0 + 16] = sn
    permT = np.zeros((128, 128), np.float32)
    for p in range(128):
        d = p % 64
        if d < 8:
            permT[p + 8, p] = 1.0
        elif d < 16:
            permT[p - 8, p] = 1.0
    k = np.arange(128)[:, None]
    q = np.arange(512)[None, :]
    amask = np.stack([((128 * j + k) // 64 <= q // 64) for j in range(4)]).astype(np.float32).astype(ml_dtypes.bfloat16)
    jj = np.arange(128)[:, None]
    ii = np.arange(128)[None, :]
    same = (jj // 64) == (ii // 64)
    strict = ((ii > jj) & same).astype(np.float32)
    incl = ((ii >= jj) & same).astype(np.float32)
    strict_ij = ((jj > ii) & same).astype(np.float32)
    cmask = np.concatenate([strict, incl, strict, incl, strict_ij], axis=1).astype(np.float32)
    return ropeC, ropeS, permT, amask, cmask


def _fm(v):
    v = np.asarray(v, np.float32)
    lead = v.shape[:-1]
    x = v.reshape(lead + (8, 128))
    x = np.moveaxis(x, -1, 0)
    return np.ascontiguousarray(x)


def kernel(**inp):
    f32 = np.float32
    I = {k: np.asarray(v) for k, v in inp.items()}
    if "nc" not in _NC_CACHE:
        _NC_CACHE["nc"] = build_program()
    nc = _NC_CACHE["nc"]
    ropeC, ropeS, permT, amask, cmask = _consts()
    in_maps = []
    for c in range(8):
        g, j = c // 4, c % 4
        s0 = 8 * g + 2 * j
        m = {}
        m["xT"] = np.ascontiguousarray(np.concatenate(
            [I["x_prompt"][g, 2048 * j:2048 * (j + 1)].T, I["x_sample"][s0].T, I["x_sample"][s0 + 1].T], axis=1))
        cv = np.stack([I["c_prompt"][g], I["c_sample"][s0], I["c_sample"][s0 + 1]], 0)
        m["cT"] = np.ascontiguousarray(cv.reshape(3, 8, 128).transpose(2, 1, 0))
        m["ada_w"] = I["ada_w"]
        m["ada_bT"] = np.ascontiguousarray(I["ada_b"].reshape(2, 72, 128).transpose(2, 0, 1))
        m["normgT"] = np.ascontiguousarray(I["norm_g"].reshape(2, 6, 8, 128).transpose(3, 0, 1, 2))
        m["ffn_w_in"] = I["ffn_w_in"]
        m["ffn_w_out"] = I["ffn_w_out"]
        m["kv_ada_w"] = I["kv_ada_w"]
        m["kv_ada_bT"] = np.ascontiguousarray(I["kv_ada_b"].reshape(16, 128).T)
        m["kv_normgT"] = np.ascontiguousarray(I["kv_norm_g"].reshape(8, 128).T)
        m["muT"] = np.ascontiguousarray(I["rwkv_mu"][0].reshape(6, 8, 128).transpose(2, 0, 1))
        cols = slice(256 * j, 256 * j + 256)
        m["w_rkv"] = np.ascontiguousarray(I["rwkv_w_rkv"][0][:, :, cols])
        m["w_l1"] = np.ascontiguousarray(np.concatenate([I["rwkv_w1"][0], I["rwkv_a1"][0], I["rwkv_g1"][0]], axis=1))
        m["w_w2"] = np.ascontiguousarray(I["rwkv_w2"][0][:, cols])
        m["w_a2"] = np.ascontiguousarray(I["rwkv_a2"][0][:, cols])
        m["w_g2"] = np.ascontiguousarray(I["rwkv_g2"][0][:, cols])
        vecs = [I["rwkv_w0"][0][cols], I["rwkv_a0"][0][cols], I["rwkv_k_k"][0][cols], I["rwkv_k_a"][0][cols],
                I["rwkv_ln_w"][0][cols], I["rwkv_ln_b"][0][cols], I["rwkv_r_k"][0].reshape(-1)[cols]]
        m["rvecT"] = np.ascontiguousarray(np.stack(vecs, 0).reshape(7, 2, 128).transpose(2, 0, 1))
        m["shiftT"] = np.ascontiguousarray(I["state_shift"][0, 8 * g:8 * g + 8, 0, :].reshape(8, 8, 128).transpose(2, 1, 0))
        sw = I["state_wkv"][0, 8 * g:8 * g + 8, 4 * j:4 * j + 4]
        sw = sw.reshape(8, 2, 2, 64, 64)
        m["wkv0"] = np.ascontiguousarray(sw.transpose(2, 4, 1, 0, 3).reshape(128, 2, 8, 64))
        m["w_o1"] = I["rwkv_w_o"][0]
        m["kvwk"] = np.ascontiguousarray(I["kv_w"][:, cols])
        m["kvwv"] = np.ascontiguousarray(I["kv_w"][:, 1024 + 256 * j:1024 + 256 * j + 256])
        m["wq"] = np.ascontiguousarray(I["diff_w_q"][0][:, cols])
        m["w_o2"] = I["diff_w_o"][0]
        m["cache_k"] = np.ascontiguousarray(I["cache_k"][8 * g:8 * g + 8, :, 2 * j:2 * j + 2].reshape(8, 2048, 256))
        m["cache_v"] = np.ascontiguousarray(I["cache_v"][8 * g:8 * g + 8, :, 2 * j:2 * j + 2].reshape(8, 2048, 256))
        m["lamb"] = np.ascontiguousarray(I["diff_lambda"][0].reshape(1, 256))
        m["sublnT"] = np.ascontiguousarray(I["diff_subln_g"][0].reshape(128, 1))
        m["ropeC"] = ropeC; m["ropeS"] = ropeS; m["permT"] = permT; m["amask"] = amask; m["cmask"] = cmask
        sel = np.zeros((128, 4), f32); sel[:, j] = 1.0
        m["selT"] = sel
        in_maps.append({k: (v if v.dtype != np.float64 else v.astype(f32)) for k, v in m.items()})
    res = run_bass_kernel_spmd(nc, in_maps, core_ids=list(range(8)))
    R = res.results
    y_prompt = np.zeros((2, 8192, D), f32); y_sample = np.zeros((16, 64, D), f32)
    wkv_prompt = np.zeros((1, 2, 16, 64, 64), f32); wkv_sample = np.zeros((1, 16, 16, 64, 64), f32)
    shift_prompt = np.zeros((1, 2, 1, D), f32); shift_sample = np.zeros((1, 16, 1, D), f32)
    k_prompt = np.zeros((2, 8192, 8, 2, 64), f32); v_prompt = np.zeros((2, 8192, 8, 128), f32)
    k_sample = np.zeros((16, 64, 8, 2, 64), f32); v_sample = np.zeros((16, 64, 8, 128), f32)
    for c in range(8):
        g, j = c // 4, c % 4
        s0 = 8 * g + 2 * j
        r = R[c]
        yT = np.asarray(r["yT"])
        y_prompt[g, 2048 * j:2048 * (j + 1)] = yT[:, :2048].T
        y_sample[s0] = yT[:, 2048:2112].T
        y_sample[s0 + 1] = yT[:, 2112:2176].T
        ow = np.asarray(r["o_wkv"]).reshape(2, 64, 9, 2, 64)
        st = ow.transpose(2, 3, 0, 4, 1)
        wkv_prompt[0, g, 4 * j:4 * j + 4] = st[0].reshape(4, 64, 64)
        for s in range(8):
            wkv_sample[0, 8 * g + s, 4 * j:4 * j + 4] = st[1 + s].reshape(4, 64, 64)
        osf = np.asarray(r["o_shift"])
        sh = osf.transpose(2, 1, 0).reshape(3, D)
        if j == 3:
            shift_prompt[0, g, 0] = sh[0]
        shift_sample[0, s0, 0] = sh[1]
        shift_sample[0, s0 + 1, 0] = sh[2]
        ok = np.asarray(r["o_k"]).reshape(2, 2, 64, 4, L)
        ov = np.asarray(r["o_v"]).reshape(4, L, 2, 128)
        for rk in range(4):
            k_prompt[g, 2048 * rk:2048 * (rk + 1), 2 * j:2 * j + 2] = ok[:, :, :, rk, :2048].transpose(3, 0, 1, 2)
            v_prompt[g, 2048 * rk:2048 * (rk + 1), 2 * j:2 * j + 2] = ov[rk, :2048]
            for p in range(2):
                sidx = 8 * g + 2 * rk + p
                k_sample[sidx, :, 2 * j:2 * j + 2] = ok[:, :, :, rk, 2048 + 64 * p:2048 + 64 * p + 64].transpose(3, 0, 1, 2)
                v_sample[sidx, :, 2 * j:2 * j + 2] = ov[rk, 2048 + 64 * p:2048 + 64 * p + 64]
    return (y_prompt, y_sample, wkv_prompt, shift_prompt, k_prompt, v_prompt,
            wkv_sample, shift_sample, k_sample, v_sample)
```
